# Optimizing a Trainium2 kernel written in Bass

```python
import jax, jax.numpy as jnp
from jax import lax
import numpy as np

D_MODEL = 2048
BATCH = 2
SEQ = 8192
DEPTH = 4
DEC_BATCH = 1
DEC_SEQ = 16384
PAST_LEN = 128

N_MIXERS = 2
HEAD_DIM = 128
HEADS_PER_GROUP = 16
DILATED_GROUPS = ((128, 1), (512, 4), (2048, 16))
N_GROUPS = len(DILATED_GROUPS)
ATTN_HEADS = N_GROUPS * HEADS_PER_GROUP
QKV_WIDTH = ATTN_HEADS * HEAD_DIM
ATTN_WIDTH = HEADS_PER_GROUP * HEAD_DIM
A_IN_WIDTH = 3 * QKV_WIDTH + ATTN_WIDTH
FNET_WIDTH = 2 * D_MODEL
FNET_GROUPS = 16
FNET_GROUP_DIM = FNET_WIDTH // FNET_GROUPS
PLE_DIM = 256
N_ATTN_LAYERS = (DEPTH + 1) // 2
N_FNET_LAYERS = DEPTH // 2
ALIBI_MAX_EXP = 8.0
EPS = 1e-6

kernel_name = "hybrid_dilated_attn_fnet_encoder"


def rms_norm(x, gain):
    xf = x.astype(jnp.float32)
    y = xf * lax.rsqrt(jnp.mean(xf * xf, axis=-1, keepdims=True) + EPS) * gain
    return y.astype(x.dtype)


def alibi_slopes(n_heads):
    return jnp.exp2(-ALIBI_MAX_EXP * jnp.arange(1, n_heads + 1, dtype=jnp.float32) / n_heads)


def dilated_window_group(q, k, v, dilation, radius, slopes):
    b, s, h, dh = q.shape
    L = s // dilation
    blk = radius
    nb = -(-L // blk)
    Lp = nb * blk
    n = b * dilation

    def to_sub(t):
        return t.reshape(b, L, dilation, h, dh).transpose(0, 2, 1, 3, 4).reshape(n, L, h, dh)

    qs = jnp.pad(to_sub(q), ((0, 0), (0, Lp - L), (0, 0), (0, 0))).reshape(n, nb, blk, h, dh)
    pad_kv = ((0, 0), (blk, Lp - L + blk), (0, 0), (0, 0))
    ks = jnp.pad(to_sub(k), pad_kv).reshape(n, nb + 2, blk, h, dh)
    vs = jnp.pad(to_sub(v), pad_kv).reshape(n, nb + 2, blk, h, dh)
    kw = jnp.concatenate([ks[:, :-2], ks[:, 1:-1], ks[:, 2:]], axis=2)
    vw = jnp.concatenate([vs[:, :-2], vs[:, 1:-1], vs[:, 2:]], axis=2)

    scores = jnp.einsum('nbqhd,nbkhd->nbhqk', qs, kw,
                        preferred_element_type=jnp.float32) * (dh ** -0.5)
    qi = jnp.arange(blk)[:, None]
    ki = jnp.arange(3 * blk)[None, :]
    rel = ki - blk - qi
    key_pos = jnp.arange(nb)[:, None] * blk + ki - blk
    valid = (jnp.abs(rel) <= radius)[None] & ((key_pos >= 0) & (key_pos < L))[:, None, :]
    dist = (dilation * jnp.abs(rel)).astype(jnp.float32)
    bias = -slopes[:, None, None] * dist[None]
    scores = jnp.where(valid[None, :, None], scores + bias[None, None], -jnp.inf)
    m = jnp.max(scores, axis=-1, keepdims=True)
    p = jnp.exp(scores - m)
    den = jnp.sum(p, axis=-1)
    o = jnp.einsum('nbhqk,nbkhd->nbqhd', p, vw.astype(jnp.float32))
    o = o / den.transpose(0, 1, 3, 2)[..., None]
    lse = (m[..., 0] + jnp.log(den)).transpose(0, 1, 3, 2)

    o = o.reshape(n, Lp, h, dh)[:, :L].reshape(b, dilation, L, h, dh)
    o = o.transpose(0, 2, 1, 3, 4).reshape(b, s, h, dh)
    lse = lse.reshape(n, Lp, h)[:, :L].reshape(b, dilation, L, h).transpose(0, 2, 1, 3).reshape(b, s, h)
    return o, lse


def dilated_attention_mixer(h, w_in, q_gain, k_gain, w_out):
    b, s, _ = h.shape
    proj = h @ w_in
    q, k, v, gate = jnp.split(proj, [QKV_WIDTH, 2 * QKV_WIDTH, 3 * QKV_WIDTH], axis=-1)
    q = rms_norm(q.reshape(b, s, ATTN_HEADS, HEAD_DIM), q_gain)
    k = rms_norm(k.reshape(b, s, ATTN_HEADS, HEAD_DIM), k_gain)
    v = v.reshape(b, s, ATTN_HEADS, HEAD_DIM)
    slopes = alibi_slopes(ATTN_HEADS)
    outs, lses = [], []
    for g, (window, dilation) in enumerate(DILATED_GROUPS):
        sl = slice(g * HEADS_PER_GROUP, (g + 1) * HEADS_PER_GROUP)
        o, lse = dilated_window_group(q[:, :, sl], k[:, :, sl], v[:, :, sl],
                                      dilation, window // (2 * dilation), slopes[sl])
        outs.append(o)
        lses.append(lse)
    wts = jax.nn.softmax(jnp.stack(lses, axis=0), axis=0)
    y = jnp.sum(wts[..., None] * jnp.stack(outs, axis=0), axis=0)
    y = y.reshape(b, s, ATTN_WIDTH).astype(h.dtype) * jax.nn.silu(gate)
    return y @ w_out


def fourier_mixer(h, w_in, w_out):
    b, s, _ = h.shape
    u, gate = jnp.split(h @ w_in, 2, axis=-1)
    u = u.reshape(b, s, FNET_GROUPS, FNET_GROUP_DIM).astype(jnp.float32)
    mixed = jnp.fft.fft2(u, axes=(1, 3), norm="ortho").real
    y = mixed.reshape(b, s, FNET_WIDTH).astype(h.dtype) * jax.nn.silu(gate)
    return y @ w_out


def trunk(x, p, norm_in, attn_w_in, attn_q_norm, attn_k_norm, attn_w_out,
          fnet_w_in, fnet_w_out, ple_proj, ple_gate, ple_norm):
    for i in range(DEPTH):
        h = rms_norm(x, norm_in[i])
        j = i // N_MIXERS
        if i % N_MIXERS == 0:
            mix = dilated_attention_mixer(h, attn_w_in[j], attn_q_norm[j], attn_k_norm[j], attn_w_out[j])
        else:
            mix = fourier_mixer(h, fnet_w_in[j], fnet_w_out[j])
        x = x + mix
        gate = jax.nn.sigmoid(rms_norm(x, ple_norm[i]) @ ple_gate[i])
        x = x + (p[i] @ ple_proj[i]) * gate
    return x


def setup_inputs(seed: int = 0) -> dict:
    key = jax.random.key(seed)
    ks = jax.random.split(key, 16)
    f32 = jnp.float32
    nrm = lambda k, shape, scale: jax.random.normal(k, shape, f32) * scale
    return {
        "x_prompt": nrm(ks[0], (BATCH, SEQ, D_MODEL), 1.0),
        "x_sample": nrm(ks[1], (DEC_BATCH, DEC_SEQ, D_MODEL), 1.0),
        "p_prompt": nrm(ks[2], (DEPTH, BATCH, SEQ, PLE_DIM), 1.0),
        "p_sample": nrm(ks[3], (DEPTH, DEC_BATCH, DEC_SEQ, PLE_DIM), 1.0),
        "norm_in": 1.0 + nrm(ks[4], (DEPTH, D_MODEL), 0.02),
        "attn_w_in": nrm(ks[5], (N_ATTN_LAYERS, D_MODEL, A_IN_WIDTH), D_MODEL ** -0.5),
        "attn_q_norm": 1.0 + nrm(ks[6], (N_ATTN_LAYERS, HEAD_DIM), 0.02),
        "attn_k_norm": 1.0 + nrm(ks[7], (N_ATTN_LAYERS, HEAD_DIM), 0.02),
        "attn_w_out": nrm(ks[8], (N_ATTN_LAYERS, ATTN_WIDTH, D_MODEL), ATTN_WIDTH ** -0.5),
        "fnet_w_in": nrm(ks[9], (N_FNET_LAYERS, D_MODEL, 2 * FNET_WIDTH), D_MODEL ** -0.5),
        "fnet_w_out": nrm(ks[10], (N_FNET_LAYERS, FNET_WIDTH, D_MODEL), FNET_WIDTH ** -0.5),
        "ple_proj": nrm(ks[11], (DEPTH, PLE_DIM, D_MODEL), PLE_DIM ** -0.5),
        "ple_gate": nrm(ks[12], (DEPTH, D_MODEL, D_MODEL), D_MODEL ** -0.5),
        "ple_norm": 1.0 + nrm(ks[13], (DEPTH, D_MODEL), 0.02),
    }


def reference(x_prompt, x_sample, p_prompt, p_sample, norm_in, attn_w_in, attn_q_norm,
              attn_k_norm, attn_w_out, fnet_w_in, fnet_w_out, ple_proj, ple_gate, ple_norm):
    y_prompt = trunk(x_prompt, p_prompt, norm_in, attn_w_in, attn_q_norm, attn_k_norm, attn_w_out,
                     fnet_w_in, fnet_w_out, ple_proj, ple_gate, ple_norm)
    y_sample = trunk(x_sample, p_sample, norm_in, attn_w_in, attn_q_norm, attn_k_norm, attn_w_out,
                     fnet_w_in, fnet_w_out, ple_proj, ple_gate, ple_norm)
    return (y_prompt, y_sample)
```

```python
from contextlib import ExitStack
import numpy as np
import ml_dtypes
import concourse.bass as bass
import concourse.mybir as mybir
from concourse.bass_utils import run_bass_kernel_spmd

F32 = mybir.dt.float32
BF16 = mybir.dt.bfloat16
I32 = mybir.dt.int32
AF = mybir.ActivationFunctionType
ALU = mybir.AluOpType
EPS = 1e-6
DIL = (1, 4, 16)
ST = 2048
TT = 512


class Buf:
    __slots__ = ("name", "w", "r", "sem", "cnt", "multi")

    def __init__(self, name, multi=False):
        self.name = name
        self.w = [] if multi else None
        self.r = []
        self.sem = None
        self.cnt = 0
        self.multi = multi


def _prune(deps):
    best = {}
    for d in deps:
        k = id(d[0])
        if k not in best or best[k][1] < d[1]:
            best[k] = d
    return list(best.values())


class Sched:
    def __init__(self, nc):
        self.nc = nc
        self.E = {}
        for name, e in [("pe", nc.tensor), ("act", nc.scalar), ("dve", nc.vector),
                        ("pool", nc.gpsimd), ("sp", nc.sync)]:
            self.E[name] = dict(eng=e, sem=nc.alloc_semaphore("e_" + name), cnt=0, waited={})
        self.ninst = 0
        self.pending = []
        self.free_sems = []
        self.all_dma = {}

    def acquire(self, b):
        if self.free_sems:
            b.sem, b.cnt = self.free_sems.pop()
        else:
            b.sem, b.cnt = self.nc.alloc_semaphore("d_" + b.name), 0
        self.all_dma[id(b.sem)] = [b.sem, b.cnt]

    def release(self, bufs):
        for b in bufs:
            if b.sem is not None:
                self.free_sems.append((b.sem, b.cnt))
                b.sem = None

    def barrier(self):
        self.flush_stores()
        for en, E in self.E.items():
            deps = [(F["sem"], F["cnt"], fn) for fn, F in self.E.items() if fn != en and F["cnt"] > 0]
            deps += [(sem, cnt, "dma") for sem, cnt in self.all_dma.values() if cnt > 0]
            self._wait(en, deps)

    def _wait(self, en, deps):
        E = self.E[en]
        need = {}
        for d in deps:
            if d is None:
                continue
            sem, val, owner = d
            if owner == en:
                continue
            k = id(sem)
            if k not in need or need[k][1] < val:
                need[k] = (sem, val)
        for k, (sem, val) in need.items():
            if E["waited"].get(k, 0) >= val:
                continue
            E["eng"].wait_ge(sem, val)
            E["waited"][k] = val
            self.ninst += 1

    @staticmethod
    def _deps(reads, writes):
        deps = []
        for b in reads:
            if b.multi:
                deps.extend(b.w)
            else:
                deps.append(b.w)
        for b in writes:
            if b.multi:
                deps.extend(b.w)
            else:
                deps.append(b.w)
            deps.extend(b.r)
        return deps

    @staticmethod
    def _update(dep, reads, writes):
        for b in writes:
            if b.multi:
                b.w.append(dep)
                if len(b.w) > 12:
                    b.w = _prune(b.w)
            else:
                b.w = dep
            b.r = []
        for b in reads:
            b.r.append(dep)
            if len(b.r) > 12:
                b.r = _prune(b.r)

    def op(self, en, fn, reads=(), writes=(), signal=True):
        E = self.E[en]
        self._wait(en, self._deps(reads, writes))
        ins = fn(E["eng"])
        self.ninst += 1
        dep = (E["sem"], E["cnt"] + 1, en)
        if signal:
            ins.then_inc(E["sem"], 1)
            E["cnt"] += 1
        self._update(dep, reads, writes)
        return ins

    def dma(self, qn, out_ap, in_ap, reads=(), writes=(), track=None, strict=False):
        E = self.E[qn]
        self._wait(qn, self._deps(reads, writes))
        tb = track
        if tb.sem is None:
            self.acquire(tb)
        ins = E["eng"].dma_start(out=out_ap, in_=in_ap)
        ins.then_inc(tb.sem, 16)
        tb.cnt += 16
        self.all_dma[id(tb.sem)][1] = tb.cnt
        self.ninst += 1
        dep = (tb.sem, tb.cnt, "dma")
        self._update(dep, reads, writes)
        return ins

    def collective(self, in_ap, out_ap, groups, reads=(), writes=()):
        E = self.E["pool"]
        self._wait("pool", self._deps(reads, writes))
        if getattr(self, "cc_sem", None) is None:
            self.cc_sem = self.nc.alloc_semaphore("cc_sem")
            self.cc_cnt = 0
        ins = E["eng"].collective_compute("AllGather", ALU.bypass, replica_groups=groups, ins=[in_ap], outs=[out_ap])
        ins.then_inc(self.cc_sem, 1)
        self.cc_cnt += 1
        self.all_dma[id(self.cc_sem)] = [self.cc_sem, self.cc_cnt]
        self.ninst += 1
        self._update((self.cc_sem, self.cc_cnt, "cc"), reads, writes)

    def idma(self, out_ap, in_ap, idx_ap, reads=(), writes=(), track=None):
        E = self.E["pool"]
        self._wait("pool", self._deps(reads, writes))
        tb = track
        if tb.sem is None:
            self.acquire(tb)
        ins = E["eng"].indirect_dma_start(out=out_ap, out_offset=None, in_=in_ap,
                                          in_offset=bass.IndirectOffsetOnAxis(ap=idx_ap, axis=0))
        ins.then_inc(tb.sem, 16)
        tb.cnt += 16
        self.all_dma[id(tb.sem)][1] = tb.cnt
        self.ninst += 1
        self._update((tb.sem, tb.cnt, "dma"), reads, writes)

    def store(self, out_ap, in_ap, reads=(), writes=(), track=None, lag=1):
        self.pending.append((out_ap, in_ap, reads, writes, track))
        while len(self.pending) > lag:
            self._emit_store()

    def _emit_store(self):
        out_ap, in_ap, reads, writes, track = self.pending.pop(0)
        self.dma("sp", out_ap, in_ap, reads=reads, writes=writes, track=track, strict=True)

    def flush_stores(self):
        while self.pending:
            self._emit_store()

    def wait_all(self, en, bufs):
        deps = []
        for b in bufs:
            deps.extend(b.w if b.multi else [b.w])
            deps.extend(b.r)
        self._wait(en, deps)


class _StopBuild(Exception):
    pass


class Ring:
    def __init__(self, items):
        self.items = items
        self.i = 0

    def next(self):
        it = self.items[self.i % len(self.items)]
        self.i += 1
        return it


class Cfg:
    def __init__(self, T=16384, D=2048, HPG=16, FG=16, PLE=256, DEPTH=4):
        self.T, self.D, self.HPG, self.FG, self.PLE, self.DEPTH = T, D, HPG, FG, PLE, DEPTH
        self.KC = D // 128
        self.NH = 3 * HPG
        self.QKV = self.NH * 128
        self.AW = HPG * 128
        self.AIN = 3 * self.QKV + self.AW
        self.FW = 2 * D
        self.FGD = self.FW // FG
        assert self.FGD == 256 and T % ST == 0
        self.NATT = (DEPTH + 1) // 2
        self.NFN = max(DEPTH // 2, 1)
        self.R = 4
        self.TS = 4 * T
        self.PL = self.FW // 64
        self.NCBO = self.PL // 4
        self.CHL = self.FW // 4
        assert self.CHL % 256 == 0
        self.NKB = [T // d // 128 + 1 for d in DIL]
        self.NKB_OFF = [0, self.NKB[0], self.NKB[0] + self.NKB[1]]
        self.NKBT = sum(self.NKB)


CH_EL = 512 * 1024


def _rpc(nr, rl):
    return max(1, min(nr, CH_EL // rl))


def _grow(r_loc, rank, nr, rl):
    rpc = _rpc(nr, rl)
    return ((r_loc // rpc) * 4 + rank) * rpc + (r_loc % rpc)


def host_tables(cfg, r, nseq):
    T, TS = cfg.T, cfg.TS
    L = TS // nseq
    bf = ml_dtypes.bfloat16
    tabs = {}
    seq_of_core = (r * T) // L
    vtn = np.zeros((128, cfg.NKBT), np.float32)
    kk = np.arange(128)[:, None]
    for g, d in enumerate(DIL):
        Lq = T // d
        j = np.arange(cfg.NKB[g])[None, :]
        X = r * Lq + 128 * j - 64 + kk
        vtn[:, cfg.NKB_OFF[g]:cfg.NKB_OFF[g] + cfg.NKB[g]] = ((X >= seq_of_core * (L // d)) & (X < (seq_of_core + 1) * (L // d))).astype(np.float32)
    tabs["vtn"] = vtn
    tabs["vts"] = np.ascontiguousarray(np.roll(vtn, 64, axis=0))
    slopes = np.exp2(-8.0 * np.arange(1, cfg.NH + 1, dtype=np.float64) / cfg.NH)
    m = np.zeros((128, cfg.NH, 3, 128), np.float64)
    qq = np.arange(128)[None, :]
    for g, d in enumerate(DIL):
        for h in range(cfg.HPG):
            for e in range(2):
                rel = 128 * e - 64 + kk - qq
                m[:, g * cfg.HPG + h, e, :] = np.where(np.abs(rel) <= 64,
                                                       np.exp(-slopes[g * cfg.HPG + h] * d * np.abs(rel)), 0.0)
            m[:, g * cfg.HPG + h, 2, :] = np.roll(m[:, g * cfg.HPG + h, 1, :], 64, axis=0)
    tabs["mtab"] = m.reshape(128, cfg.NH, 384).astype(np.float32)
    H = cfg.HPG
    p = np.arange(128)
    idxk = np.zeros((128, 3, 2 * H), np.int32)
    idxv = np.zeros((128, 3, 2 * H), np.int32)
    for g, d in enumerate(DIL):
        for side in range(2):
            nb = min(max(r - 1 if side == 0 else r + 1, 0), 3)
            src = 1 - side
            for h in range(H):
                idxk[:, g, side * H + h] = _grow((h * 2 + src) * 128 + p, nb, H * 2 * 128, d * 64)
                idxv[:, g, side * H + h] = _grow((h * 2 + src) * 64 + (p % 64), nb, H * 2 * 64, d * 128)
    tabs["idxk"] = idxk
    tabs["idxv"] = idxv
    AR = T // 128
    a = np.arange(128)
    rk = np.minimum(a // AR, 3)
    idxu = np.zeros((128, cfg.NCBO), np.int32)
    for i in range(cfg.NCBO):
        cb = r * cfg.NCBO + i
        idxu[:, i] = np.where(a < 4 * AR, _grow(cb * AR + (a % AR), rk, cfg.PL * AR, 128 * 64), 0)
    tabs["idxu"] = idxu
    ngb = cfg.FW // 128
    idxm = np.zeros((128, ngb), np.int32)
    per = cfg.CHL // 128
    for gb in range(ngb):
        idxm[:, gb] = _grow(r * cfg.CHL + (gb % per) * 128 + p, gb // per, 4 * cfg.CHL, T)
    tabs["idxm"] = idxm
    N = L
    N1 = N // 128
    aa = np.arange(128)[:, None]
    jj = np.arange(128)[None, :]
    c = jj % N1
    e = jj // N1
    dA = np.zeros((128, 2, 2, 128), np.float64)
    dB = np.zeros((128, 2, 2, 256), np.float64)
    b = np.arange(128)[:, None]
    dd = np.arange(128)[None, :]
    for sq in range(nseq):
        a0 = sq * N1
        ang = 2 * np.pi * ((aa - a0) * c) / N1
        ina = (aa >= a0) & (aa < a0 + N1)
        dA[:, sq, 0, :] = np.where(ina, np.cos(ang), 0.0)
        dA[:, sq, 1, :] = np.where(ina, -np.sin(ang), 0.0)
        angb = 2 * np.pi * (b * (dd - a0)) / N1
        ind = (dd >= a0) & (dd < a0 + N1)
        C2 = np.where(ind, np.cos(angb), 0.0)
        S2 = np.where(ind, np.sin(angb), 0.0)
        dB[:, sq, 0, :] = np.concatenate([C2, -S2], axis=1)
        dB[:, sq, 1, :] = np.concatenate([S2, C2], axis=1)
    tabs["dftA"] = dA.reshape(128, 512).astype(bf)
    tabs["dftB"] = dB.astype(bf)
    angt = 2 * np.pi * (b * c / N + b * e / 128.0)
    ct = np.cos(angt)
    st = np.sin(angt)
    tabs["dftct"] = np.stack([ct, ct], axis=1).astype(np.float32)
    tabs["dftst"] = np.stack([st, st], axis=1).astype(np.float32)
    ci = np.arange(256)[:, None]
    co = np.arange(256)[None, :]
    ang3 = 2 * np.pi * (ci * co) / 256.0
    sc = 1.0 / np.sqrt(float(N) * 256.0)
    C3 = (np.cos(ang3) * sc).reshape(2, 128, 256).transpose(1, 0, 2)
    S3 = (np.sin(ang3) * sc).reshape(2, 128, 256).transpose(1, 0, 2)
    tabs["dftC3"] = np.ascontiguousarray(np.stack([C3, S3], axis=1)).astype(bf)
    tabs["ident"] = np.eye(128, dtype=np.float32).astype(bf)
    return tabs


def build(cfg):
    T, D, KC, HPG, NH = cfg.T, cfg.D, cfg.KC, cfg.HPG, cfg.NH
    nc = bass.Bass("TRN2", target_bir_lowering=False, num_devices=8)
    S = Sched(nc)
    GROUPS = [[0, 1, 2, 3], [4, 5, 6, 7]]
    TS, PL, NCBO, CHL = cfg.TS, cfg.PL, cfg.NCBO, cfg.CHL

    def din(name, shape, dt=F32):
        return nc.dram_tensor(name, list(shape), dt, kind="ExternalInput").ap()

    x_in = din("x", [T, D])
    p_in = din("p", [cfg.DEPTH, T, cfg.PLE])
    norm_in = din("norm_in", [cfg.DEPTH, D])
    attn_w_in = din("attn_w_in", [cfg.NATT, D, cfg.AIN])
    qn_in = din("attn_q_norm", [cfg.NATT, 128])
    kn_in = din("attn_k_norm", [cfg.NATT, 128])
    attn_w_out = din("attn_w_out", [cfg.NATT, cfg.AW, D])
    fnet_w_in = din("fnet_w_in", [cfg.NFN, D, 2 * cfg.FW])
    fnet_w_out = din("fnet_w_out", [cfg.NFN, cfg.FW, D])
    ple_proj = din("ple_proj", [cfg.DEPTH, cfg.PLE, D])
    ple_gate = din("ple_gate", [cfg.DEPTH, D, D])
    ple_norm = din("ple_norm", [cfg.DEPTH, D])
    vtn_in = din("vtn", [128, cfg.NKBT])
    vts_in = din("vts", [128, cfg.NKBT])
    mtab_in = din("mtab", [128, NH, 384])
    idxk_in = din("idxk", [128, 3, 2 * HPG], I32)
    idxv_in = din("idxv", [128, 3, 2 * HPG], I32)
    idxu_in = din("idxu", [128, NCBO], I32)
    idxm_in = din("idxm", [128, cfg.FW // 128], I32)
    dftA_in = din("dftA", [128, 512], BF16)
    dftct_in = din("dftct", [128, 2, 128])
    dftst_in = din("dftst", [128, 2, 128])
    dftB_in = din("dftB", [128, 2, 2, 256], BF16)
    dftC3_in = din("dftC3", [128, 2, 2, 256], BF16)
    ident_in = din("ident", [128, 128], BF16)
    y_out = nc.dram_tensor("y", [T, D], F32, kind="ExternalOutput").ap()

    all_inputs = [x_in, p_in, norm_in, attn_w_in, qn_in, kn_in, attn_w_out, fnet_w_in, fnet_w_out, ple_proj, ple_gate,
                  ple_norm, vtn_in, vts_in, mtab_in, idxk_in, idxv_in, idxu_in, idxm_in, dftA_in, dftct_in, dftst_in,
                  dftB_in, dftC3_in, ident_in]

    def touch_inputs():
        tiles = {}
        for ap in all_inputs:
            a = ap
            while len(a.shape) > 2:
                a = a[0]
            n = min(16, a.shape[-1])
            key = str(ap.dtype)
            if key not in tiles:
                tiles[key] = (nc.alloc_sbuf_tensor("touch_" + str(len(tiles)), [1, 16], ap.dtype), Buf("touch" + str(len(tiles))))
            t, b = tiles[key]
            S.dma("sp", t[0:1, 0:n], a[0:1, 0:n], writes=[b], track=b)

    def dscr(name, shape, dt):
        return nc.dram_tensor(name, list(shape), dt, kind="Internal").ap()

    XA = dscr("XA", [T, D], F32)
    XB = dscr("XB", [T, D], F32)
    bXA, bXB, bY = Buf("XA", True), Buf("XB", True), Buf("Y", True)
    QT, KTs, VS = [], [], []
    HKin, HKout, HVin, HVout = [], [], [], []
    for g, d in enumerate(DIL):
        LP = T // d
        QT.append(dscr(f"QT{g}", [HPG, 128, d, LP], BF16))
        KTs.append(dscr(f"KT{g}", [HPG, 128, d, LP], BF16))
        VS.append(dscr(f"VS{g}", [d, LP, HPG * 128], BF16))
        HKin.append(dscr(f"HKin{g}", [HPG * 2 * 128, d * 64], BF16))
        HVin.append(dscr(f"HVin{g}", [HPG * 2 * 64, d * 128], BF16))
        HKout.append([dscr(f"HKout{g}_{a}", [4 * HPG * 2 * 128, d * 64], BF16) for a in range(cfg.NATT)])
        HVout.append([dscr(f"HVout{g}_{a}", [4 * HPG * 2 * 64, d * 128], BF16) for a in range(cfg.NATT)])
    bQKV = Buf("QKV", True)
    bHin, bHout = [Buf(f"Hin{g}", True) for g in range(3)], Buf("Hout", True)
    SGT = dscr("SGT", [cfg.FW, T], BF16)
    bSGT = Buf("SGT", True)
    YT = dscr("YT", [cfg.FW, T], BF16)
    bYT = Buf("YT", True)
    UUloc = dscr("UUloc", [PL * T, 64], BF16)
    UUall = [dscr(f"UUall{a}", [4 * PL * T, 64], BF16) for a in range(max(cfg.NFN, 1))]
    bUU, bUUall = Buf("UU", True), Buf("UUall", True)
    ZT = dscr("ZT", [CHL, 2, TS], BF16)
    bZT = Buf("ZT", True)
    MXloc = dscr("MXloc", [4 * CHL, T], BF16)
    MXall = [dscr(f"MXall{a}", [16 * CHL, T], BF16) for a in range(max(cfg.NFN, 1))]
    bMX, bMXall = Buf("MX", True), Buf("MXall", True)

    sb_i = [0]
    nph = [0]
    phase = [None]

    def sbuf(shape, dt, name=None):
        sb_i[0] += 1
        nm = f"{name or 't'}_{sb_i[0]}"
        b = Buf(nm)
        if phase[0] is None:
            return nc.alloc_sbuf_tensor(nm, list(shape), dt), b
        t = phase[0][0].enter_context(nc.sbuf_tensor(nm, list(shape), dt))
        phase[0][1].append(b)
        return t, b

    class Phase:
        def __enter__(self):
            self.es = ExitStack()
            phase[0] = (self.es, [])
            return self

        def __exit__(self, *a):
            S.barrier()
            S.release(phase[0][1])
            self.es.close()
            phase[0] = None
            if a[0] is None:
                nph[0] += 1
                if nph[0] == getattr(cfg, "stop_after", -1):
                    raise _StopBuild()
            return False

    PS = Ring([(nc.alloc_psum_tensor(f"ps{i}", [128, 512], F32), Buf(f"ps{i}")) for i in range(6)])
    PTR = Ring([(nc.alloc_psum_tensor(f"pt{i}", [128, 1024], BF16), Buf(f"pt{i}")) for i in range(2)])

    touch_inputs()
    ident, b_ident = sbuf([128, 128], BF16, "ident")
    S.dma("sp", ident[:], ident_in[:, :], writes=[b_ident], track=b_ident)
    ones_bf, b_ones = sbuf([128, 128], BF16, "ones")
    S.op("dve", lambda e: e.memset(ones_bf[:], 1.0), writes=[b_ones])
    vtn, b_vt = sbuf([128, cfg.NKBT], F32, "vtn")
    S.dma("sp", vtn[:], vtn_in[:, :], writes=[b_vt], track=b_vt)
    vts, b_vts = sbuf([128, cfg.NKBT], F32, "vts")
    S.dma("sp", vts[:], vts_in[:, :], writes=[b_vts], track=b_vts)
    idxk, b_idxk = sbuf([128, 3, 2 * HPG], I32, "idxk")
    S.dma("sp", idxk[:], idxk_in[:, :, :], writes=[b_idxk], track=b_idxk)
    idxv, b_idxv = sbuf([128, 3, 2 * HPG], I32, "idxv")
    S.dma("sp", idxv[:], idxv_in[:, :, :], writes=[b_idxv], track=b_idxv)

    def ag_chunks(in_t, out_t, nr, rl, reads, writes, ks=None):
        rpc = _rpc(nr, rl)
        for k in (range(nr // rpc) if ks is None else ks):
            ic = in_t[k * rpc:(k + 1) * rpc, :]
            oc = out_t[k * 4 * rpc:(k + 1) * 4 * rpc, :]
            if rl < 512 and (rpc * rl) % 512 == 0 and rpc % (512 // rl) == 0:
                b = 512 // rl
                ic = ic.rearrange("(a b) e -> a (b e)", b=b)
                oc = oc.rearrange("(a b) e -> a (b e)", b=b)
            elif rl > 512 and rl % 512 == 0:
                ic = ic.rearrange("a (b e) -> (a b) e", e=512)
                oc = oc.rearrange("a (b e) -> (a b) e", e=512)
            S.collective(ic, oc, GROUPS, reads=reads, writes=writes)
    idxu, b_idxu = sbuf([128, NCBO], I32, "idxu")
    S.dma("sp", idxu[:], idxu_in[:, :], writes=[b_idxu], track=b_idxu)
    idxm, b_idxm = sbuf([128, cfg.FW // 128], I32, "idxm")
    S.dma("sp", idxm[:], idxm_in[:, :], writes=[b_idxm], track=b_idxm)
    NV = 2 * cfg.DEPTH
    gvec, b_gvec = sbuf([128, NV, KC], F32, "gvec")
    with nc.allow_non_contiguous_dma(reason="tiny gain vectors"):
        for i in range(cfg.DEPTH):
            S.dma("sp", gvec[:, i, :], norm_in[i].rearrange("(kc p) -> p kc", p=128), writes=[b_gvec], track=b_gvec)
            S.dma("sp", gvec[:, cfg.DEPTH + i, :], ple_norm[i].rearrange("(kc p) -> p kc", p=128),
                  writes=[b_gvec], track=b_gvec)
        qkg, b_qkg = sbuf([128, 2 * cfg.NATT], F32, "qkg")
        for j in range(cfg.NATT):
            S.dma("sp", qkg[:, 2 * j:2 * j + 1], qn_in[j].rearrange("(p o) -> p o", o=1), writes=[b_qkg], track=b_qkg)
            S.dma("sp", qkg[:, 2 * j + 1:2 * j + 2], kn_in[j].rearrange("(p o) -> p o", o=1), writes=[b_qkg], track=b_qkg)
    eps128, b_eps = sbuf([128, 1], F32, "eps128")
    S.op("dve", lambda e: e.memset(eps128[:], EPS), writes=[b_eps])
    gfull, b_gfull = sbuf([128, KC, 128], F32, "gfull")
    onesf, b_onesf = sbuf([128, 128], F32, "onesf")
    S.op("dve", lambda e: e.memset(onesf[:], 1.0), writes=[b_onesf])

    def set_gain(vi):
        for kc in range(KC):
            S.op("dve", lambda e: e.tensor_scalar(out=gfull[:, kc, :], in0=onesf[:], scalar1=gvec[:, vi, kc:kc + 1],
                                                  scalar2=float(D) ** 0.5, op0=ALU.mult, op1=ALU.mult),
                 reads=[b_onesf, b_gvec], writes=[b_gfull])

    stat_ring = Ring([sbuf([128, 4], F32, "stat") for _ in range(4)])
    W = {}

    def alloc_prep():
        W["xt_ring"] = Ring([sbuf([128, D], F32, "xt") for _ in range(2)])
        W["xs_ring"] = Ring([sbuf([128, D], BF16, "xs") for _ in range(2)])
        W["junk"] = sbuf([128, D], BF16, "junk")
        W["wst_ring"] = Ring([sbuf([128, max(KC // 2, 1), TT], F32, "wst") for _ in range(2)])

    def alloc_proj_ev():
        W["ev_ring"] = Ring([sbuf([128, TT], BF16, "ev") for _ in range(3)])

    def alloc_proj():
        alloc_prep()
        W["hT"] = sbuf([128, KC, ST], BF16, "hT")
        W["wb_ring"] = Ring([sbuf([128, KC, TT], BF16, "wb") for _ in range(2)])
        W["ev_ring"] = Ring([sbuf([128, TT], BF16, "ev") for _ in range(3)])
        W["evf_ring"] = Ring([sbuf([128, TT], F32, "evf") for _ in range(3)])
        W["o_ring"] = Ring([sbuf([128, TT], BF16, "o") for _ in range(3)])
        W["qr_ring"] = Ring([sbuf([128, TT], F32, "qr") for _ in range(3)])

    def rstd_from_ssq(ssq_ap, b_ssq, n, out_ap, b_out):
        S.op("dve", lambda e: e.tensor_scalar(out=out_ap, in0=ssq_ap, scalar1=float(n) * EPS, scalar2=None,
                                              op0=ALU.add), reads=[b_ssq], writes=[b_out])
        S.op("act", lambda e: e.activation(out=out_ap, in_=out_ap, func=AF.Sqrt), reads=[b_out], writes=[b_out])
        S.op("dve", lambda e: e.reciprocal(out=out_ap, in_=out_ap), reads=[b_out], writes=[b_out])

    def prep_block(src_rows_ap, b_src, dst, b_dst, col0, kcn=KC):
        xt, b_xt = W["xt_ring"].next()
        S.dma("sp", xt[:], src_rows_ap, reads=[b_src], writes=[b_xt], track=b_xt)
        st, b_st = stat_ring.next()
        junk, b_junk = W["junk"]
        S.op("act", lambda e: e.activation(out=junk[:], in_=xt[:], func=AF.Square, accum_out=st[:, 0:1]),
             reads=[b_xt], writes=[b_junk, b_st])
        rstd_from_ssq(st[:, 0:1], b_st, D, st[:, 1:2], b_st)
        xs, b_xs = W["xs_ring"].next()
        S.op("act", lambda e: e.activation(out=xs[:], in_=xt[:], func=AF.Copy, scale=st[:, 1:2]),
             reads=[b_xt, b_st], writes=[b_xs])
        for half in range(KC // 8 if KC >= 8 else 1):
            n = min(8, KC)
            pt, b_pt = PTR.next()
            for k in range(n):
                kc = half * 8 + k
                S.op("pe", lambda e: e.transpose(out=pt[:, k * 128:(k + 1) * 128], in_=xs[:, kc * 128:(kc + 1) * 128],
                                                 identity=ident[:]),
                     reads=[b_xs, b_ident], writes=[b_pt], signal=(k == n - 1))
            S.op("dve", lambda e: e.tensor_tensor(out=dst[:, half * 8:half * 8 + n, col0:col0 + 128],
                                                  in0=pt[:, 0:n * 128].rearrange("p (k c) -> p k c", k=n),
                                                  in1=gfull[:, half * 8:half * 8 + n, :], op=ALU.mult),
                 reads=[b_pt, b_gfull], writes=[b_dst])

    def load_w(w_ap_cols, b_w=None):
        wb, b_wb = W["wb_ring"].next()
        nh = 2 if KC >= 2 else 1
        kh = KC // nh
        for hh in range(nh):
            wst, b_wst = W["wst_ring"].next()
            S.dma("sp", wst[:, 0:kh, :], w_ap_cols[hh * kh * 128:(hh + 1) * kh * 128, :].rearrange("(kc p) f -> p kc f", p=128),
                  writes=[b_wst], track=b_wst)
            step = 2 if kh >= 2 else 1
            for q0 in range(0, kh, step):
                if (q0 // step) % 2 == 0:
                    S.op("act", lambda e: e.activation(out=wb[:, hh * kh + q0:hh * kh + q0 + step, :], in_=wst[:, q0:q0 + step, :], func=AF.Copy),
                         reads=[b_wst], writes=[b_wb])
                else:
                    S.op("dve", lambda e: e.tensor_copy(out=wb[:, hh * kh + q0:hh * kh + q0 + step, :], in_=wst[:, q0:q0 + step, :]),
                         reads=[b_wst], writes=[b_wb])
        return wb, b_wb

    def run_blocks(items):
        if not items:
            return
        nxt = load_w(items[0][0])
        for k, (w_ap, fn) in enumerate(items):
            cur = nxt
            if k + 1 < len(items):
                nxt = load_w(items[k + 1][0])
            fn(*cur)

    def mm_fm(wb, b_wb, m, src, b_src, c0, ncols, kcn=KC):
        ps, b_ps = PS.next()
        for kc in range(kcn):
            S.op("pe", lambda e: e.matmul(ps[:, 0:ncols], lhsT=wb[:, kc, m * 128:(m + 1) * 128],
                                          rhs=src[:, kc, c0:c0 + ncols], start=(kc == 0), stop=(kc == kcn - 1)),
                 reads=[b_wb, b_src], writes=[b_ps], signal=(kc == kcn - 1))
        return ps, b_ps

    def mm_tm(wb, b_wb, src, b_src, c0, kcn=KC, ncols=TT):
        ps, b_ps = PS.next()
        for kc in range(kcn):
            S.op("pe", lambda e: e.matmul(ps[:, 0:ncols], lhsT=src[:, kc, c0:c0 + 128], rhs=wb[:, kc, 0:ncols],
                                          start=(kc == 0), stop=(kc == kcn - 1)),
                 reads=[b_wb, b_src], writes=[b_ps], signal=(kc == kcn - 1))
        return ps, b_ps

    def attn_proj(Xc, bXc, li, j):
        Wi = attn_w_in[j]
        hT, b_hT = W["hT"]
        for g, d in enumerate(DIL):
            Lq = T // d
            nst = T // ST
            for s in range(nst):
                per = Lq // ST if Lq >= ST else 0
                blocks = []
                if Lq >= ST:
                    rho, pos0 = s // per, (s % per) * ST
                    blocks = [(rho, pos0, ST)]
                else:
                    nr = ST // Lq
                    blocks = [(s * nr + r, 0, Lq) for r in range(nr)]
                set_gain(li) if (g == 0 and s == 0) else None
                col = 0
                for (rho, pos0, npos) in blocks:
                    for bb in range(npos // 128):
                        t0 = rho + d * (pos0 + bb * 128)
                        pp0 = pos0 + bb * 128
                        rows = Xc.rearrange("(i r) c -> r i c", r=d)[rho, pp0:pp0 + 128, :]
                        prep_block(rows, bXc, hT, b_hT, col)
                        col += 128
                pendB = [None]

                def flushB():
                    if pendB[0] is not None:
                        f = pendB[0]
                        pendB[0] = None
                        f()

                items = []
                for which in range(2):
                  for hb in range(HPG // 4):
                    def qk_block(wb, b_wb, which=which, hb=hb):
                        dstT = QT[g] if which == 0 else KTs[g]
                        for m in range(4):
                            h = hb * 4 + m
                            for n in range(ST // TT):
                                ps, b_ps = mm_fm(wb, b_wb, m, hT, b_hT, n * TT, TT)
                                sq, b_sq = W["ev_ring"].next()
                                S.op("act", lambda e: e.activation(out=sq[:], in_=ps[:], func=AF.Square),
                                     reads=[b_ps], writes=[b_sq])
                                qr, b_qr = W["qr_ring"].next()
                                S.op("dve", lambda e: e.tensor_copy(out=qr[:], in_=ps[:]), reads=[b_ps, b_sq], writes=[b_qr])
                                flushB()

                                def partB(ps=qr, b_ps=b_qr, sq=sq, b_sq=b_sq, h=h, n=n, which=which, dstT=dstT):
                                    ps2, b_ps2 = PS.next()
                                    S.op("pe", lambda e: e.matmul(ps2[:], lhsT=ones_bf[:], rhs=sq[:], start=True, stop=True),
                                         reads=[b_ones, b_sq], writes=[b_ps2])
                                    rs, b_rs = W["evf_ring"].next()
                                    S.op("act", lambda e: e.activation(out=rs[:], in_=ps2[:], func=AF.Sqrt, bias=eps128[:, 0:1], scale=1.0 / 128),
                                         reads=[b_ps2, b_eps], writes=[b_rs])
                                    S.op("dve", lambda e: e.reciprocal(out=rs[:], in_=rs[:]), reads=[b_rs], writes=[b_rs])
                                    o, b_o = W["o_ring"].next()
                                    S.op("dve", lambda e: e.scalar_tensor_tensor(out=o[:], in0=ps[:], scalar=qkg[:, 2 * j + which:2 * j + which + 1],
                                                                                 in1=rs[:], op0=ALU.mult, op1=ALU.mult),
                                         reads=[b_ps, b_rs, b_qkg], writes=[b_o])
                                    c = n * TT
                                    cc = 0
                                    for (rho, pos0, npos) in blocks:
                                        lo, hi = max(c, cc), min(c + TT, cc + npos)
                                        if lo < hi:
                                            S.store(dstT[h, :, rho, pos0 + (lo - cc):pos0 + (hi - cc)], o[:, lo - c:hi - c],
                                                    reads=[b_o], writes=[bQKV], track=b_o)
                                            if which == 1:
                                                if pos0 == 0 and lo == cc:
                                                    S.store(HKin[g][(h * 2) * 128:(h * 2 + 1) * 128, rho * 64:(rho + 1) * 64],
                                                            o[:, lo - c:lo - c + 64], reads=[b_o], writes=[bHin[g]], track=b_o)
                                                if pos0 + npos == Lq and hi == cc + npos:
                                                    S.store(HKin[g][(h * 2 + 1) * 128:(h * 2 + 2) * 128, rho * 64:(rho + 1) * 64],
                                                            o[:, hi - c - 64:hi - c], reads=[b_o], writes=[bHin[g]], track=b_o)
                                        cc += npos
                                pendB[0] = partB
                    f0 = which * cfg.QKV + (g * HPG + hb * 4) * 128
                    items.append((Wi[:, f0:f0 + TT], qk_block))
                for hb in range(HPG // 4):
                  def v_block(wb, b_wb, hb=hb):
                    flushB()
                    col = 0
                    for (rho, pos0, npos) in blocks:
                        for bb in range(npos // 128):
                            ps, b_ps = mm_tm(wb, b_wb, hT, b_hT, col)
                            o, b_o = W["ev_ring"].next()
                            S.op("act", lambda e: e.activation(out=o[:], in_=ps[:], func=AF.Copy), reads=[b_ps], writes=[b_o])
                            r0 = pos0 + bb * 128
                            S.store(VS[g][rho, r0:r0 + 128, hb * TT:(hb + 1) * TT], o[:], reads=[b_o], writes=[bQKV], track=b_o)
                            hv = HVin[g].rearrange("(h s k) (r c) -> k h s r c", s=2, k=64, c=128)
                            if r0 == 0:
                                S.store(hv[:, hb * 4:(hb + 1) * 4, 0, rho, :], o[0:64, :].rearrange("k (h c) -> k h c", c=128),
                                      reads=[b_o], writes=[bHin[g]], track=b_o)
                            if r0 + 128 == Lq:
                                S.store(hv[:, hb * 4:(hb + 1) * 4, 1, rho, :], o[64:128, :].rearrange("k (h c) -> k h c", c=128),
                                      reads=[b_o], writes=[bHin[g]], track=b_o)
                            col += 128
                  f0 = 2 * cfg.QKV + (g * HPG + hb * 4) * 128
                  items.append((Wi[:, f0:f0 + TT], v_block))
                run_blocks(items)
                flushB()
            if g > 0:
                halo_exchange_g(j, g - 1)
        for s in range(T // ST):
            for bb in range(ST // 128):
                t0 = s * ST + bb * 128
                prep_block(Xc[t0:t0 + 128, :], bXc, hT, b_hT, bb * 128)
            items = []
            for fb in range(cfg.AW // TT):
                def g_block(wb, b_wb, fb=fb, s=s):
                    for m in range(4):
                        for n in range(ST // TT):
                            ps, b_ps = mm_fm(wb, b_wb, m, hT, b_hT, n * TT, TT)
                            o, b_o = W["ev_ring"].next()
                            S.op("act", lambda e: e.activation(out=o[:], in_=ps[:], func=AF.Silu), reads=[b_ps], writes=[b_o])
                            fr = fb * TT + m * 128
                            S.store(SGT[fr:fr + 128, s * ST + n * TT:s * ST + (n + 1) * TT], o[:], reads=[b_o], writes=[bSGT], track=b_o)
                f0 = 3 * cfg.QKV + fb * TT
                items.append((Wi[:, f0:f0 + TT], g_block))
            run_blocks(items)
            if s == 0:
                halo_exchange_g(j, 2)

    def halo_exchange_g(ai, g):
        d = DIL[g]
        S.flush_stores()
        ag_chunks(HKin[g], HKout[g][ai], HPG * 2 * 128, d * 64, [bHin[g]], [bHout])
        ag_chunks(HVin[g], HVout[g][ai], HPG * 2 * 64, d * 128, [bHin[g]], [bHout])

    def attn_core(ai):
        qs_ring = Ring([sbuf([128, ST], BF16, "qs") for _ in range(2)])
        ks_ring = Ring([sbuf([128, 2 * ST], BF16, "ks") for _ in range(2)])
        vs_ring = Ring([sbuf([128, 32, 128], BF16, "vs") for _ in range(2)])
        mt_ring = Ring([sbuf([128, 3, 384], F32, "mt") for _ in range(2)])
        kh_ring = Ring([sbuf([128, 1024], BF16, "kh") for _ in range(2)])
        vh_ring = Ring([sbuf([128, 2048], BF16, "vh") for _ in range(2)])

        def khalo(dst, g, d, col, b_ks):
            kh, b_kh = kh_ring.next()
            S.idma(kh[:, 0:d * 64], HKout[g][ai][:, :], idxk[:, g, col:col + 1], reads=[bHout, b_idxk], writes=[b_kh], track=b_kh)
            S.op("act", lambda e: e.activation(out=dst, in_=kh[:, 0:d * 64].rearrange("p (r l) -> p r l", l=64), func=AF.Copy),
                 reads=[b_kh], writes=[b_ks])

        def vhalo(dst, g, d, col, b_vs):
            vh, b_vh = vh_ring.next()
            S.idma(vh[0:64, 0:d * 128], HVout[g][ai][:, :], idxv[0:64, g, col:col + 1], reads=[bHout, b_idxv], writes=[b_vh], track=b_vh)
            S.op("act", lambda e: e.activation(out=dst, in_=vh[0:64, 0:d * 128].rearrange("k (r c) -> k r c", c=128), func=AF.Copy),
                 reads=[b_vh], writes=[b_vs])

        acc, b_acc = sbuf([128, 2, ST], F32, "acc")
        ex_ring = Ring([sbuf([128, 256], F32, "ex") for _ in range(4)])
        pt_ring = Ring([sbuf([128, 256], BF16, "pT") for _ in range(4)])
        sg_ring = Ring([sbuf([128, ST], BF16, "sg") for _ in range(2)])
        yo_ring = Ring([sbuf([128, ST], BF16, "yo") for _ in range(2)])
        tmp, b_tmp = sbuf([128, ST], F32, "tmp")
        nst = T // ST
        for s_ in range(nst):
            T0 = s_ * ST
            for h in range(HPG):
                mt, b_mt = mt_ring.next()
                S.dma("sp", mt[:], mtab_in.rearrange("p (g h) c -> p h g c", g=3)[:, h, :, :], writes=[b_mt], track=b_mt)
                hc = slice(h * 128, (h + 1) * 128)
                for g, d in enumerate(DIL):
                    Lq = ST // d
                    nqb = Lq // 128
                    P0 = T0 // d
                    Wd = Lq + 128
                    qs, b_qs = qs_ring.next()
                    ks, b_ks = ks_ring.next()
                    vs, b_vs = vs_ring.next()
                    S.dma("sp", qs[:, 0:ST].rearrange("p (r l) -> p r l", r=d), QT[g][h, :, :, P0:P0 + Lq],
                          reads=[bQKV], writes=[b_qs], track=b_qs)
                    ks3 = ks[:, 0:d * Wd].rearrange("p (r w) -> p r w", r=d)
                    S.dma("sp", ks3[:, :, 64:Lq], KTs[g][h, :, :, P0:P0 + Lq - 64], reads=[bQKV], writes=[b_ks], track=b_ks)
                    S.dma("sp", ks3[:, :, Lq + 64:Lq + 128], KTs[g][h, :, :, P0 + Lq - 64:P0 + Lq], reads=[bQKV], writes=[b_ks], track=b_ks)
                    hk = HKout[g][ai].rearrange("n (r l) -> n r l", l=64)
                    if s_ > 0:
                        S.dma("sp", ks3[:, :, 0:64], KTs[g][h, :, :, P0 - 64:P0], reads=[bQKV], writes=[b_ks], track=b_ks)
                    else:
                        khalo(ks3[:, :, 0:64], g, d, h, b_ks)
                    if s_ < nst - 1:
                        S.dma("sp", ks3[:, :, Lq:Lq + 64], KTs[g][h, :, :, P0 + Lq:P0 + Lq + 64], reads=[bQKV], writes=[b_ks], track=b_ks)
                    else:
                        khalo(ks3[:, :, Lq:Lq + 64], g, d, HPG + h, b_ks)
                    vs4 = vs[:, 0:d * (nqb + 1), :].rearrange("k (r j) c -> k r j c", j=nqb + 1)
                    if nqb > 1:
                        for rho in range(d):
                            S.dma("sp", vs4[:, rho, 1:nqb, :],
                                  VS[g][rho, P0 + 64:P0 + 64 + 128 * (nqb - 1), hc].rearrange("(j k) c -> k j c", k=128),
                                  reads=[bQKV], writes=[b_vs], track=b_vs)
                    S.dma("sp", vs4[64:128, :, 0, :], VS[g][:, P0:P0 + 64, hc].rearrange("r k c -> k r c"),
                          reads=[bQKV], writes=[b_vs], track=b_vs)
                    S.dma("sp", vs4[64:128, :, nqb, :], VS[g][:, P0 + Lq - 64:P0 + Lq, hc].rearrange("r k c -> k r c"),
                          reads=[bQKV], writes=[b_vs], track=b_vs)
                    hvv = HVout[g][ai].rearrange("n (r c) -> n r c", c=128)
                    if s_ > 0:
                        S.dma("sp", vs4[0:64, :, 0, :], VS[g][:, P0 - 64:P0, hc].rearrange("r k c -> k r c"),
                              reads=[bQKV], writes=[b_vs], track=b_vs)
                    else:
                        vhalo(vs4[0:64, :, 0, :], g, d, h, b_vs)
                    if s_ < nst - 1:
                        S.dma("sp", vs4[0:64, :, nqb, :], VS[g][:, P0 + Lq:P0 + Lq + 64, hc].rearrange("r k c -> k r c"),
                              reads=[bQKV], writes=[b_vs], track=b_vs)
                    else:
                        vhalo(vs4[0:64, :, nqb, :], g, d, HPG + h, b_vs)
                    def stageA(rho, qb):
                        qcol = rho * Lq + qb * 128
                        sc, b_sc = PS.next()
                        for e in range(2):
                            kcol = rho * Wd + (qb + e) * 128
                            S.op("pe", lambda en: en.matmul(sc[:, e * 128:(e + 1) * 128], lhsT=ks[:, kcol:kcol + 128],
                                                           rhs=qs[:, qcol:qcol + 128], start=True, stop=True),
                                 reads=[b_ks, b_qs], writes=[b_sc], signal=(e == 1))
                        ex, b_ex = ex_ring.next()
                        S.op("act", lambda en: en.activation(out=ex[:], in_=sc[:, 0:256], func=AF.Exp, scale=128.0 ** -0.5),
                             reads=[b_sc], writes=[b_ex])
                        pT, b_pT = pt_ring.next()
                        for e in range(2):
                            jcol = cfg.NKB_OFF[g] + P0 // 128 + qb + e
                            swapped = (qb + e == nqb)
                            vtab = vts if swapped else vtn
                            mv = 2 if swapped else e
                            S.op("dve", lambda en: en.scalar_tensor_tensor(out=pT[:, e * 128:(e + 1) * 128], in0=ex[:, e * 128:(e + 1) * 128],
                                                                          scalar=vtab[:, jcol:jcol + 1], in1=mt[:, g, mv * 128:(mv + 1) * 128],
                                                                          op0=ALU.mult, op1=ALU.mult),
                                 reads=[b_ex, b_vt, b_vts, b_mt], writes=[b_pT])
                        return pT, b_pT

                    def stageB(rho, qb, pT, b_pT):
                        nd, b_nd = PS.next()
                        for e in range(2):
                            vb = rho * (nqb + 1) + qb + e
                            S.op("pe", lambda en: en.matmul(nd[:, 0:128], lhsT=vs[:, vb, :], rhs=pT[:, e * 128:(e + 1) * 128],
                                                           start=(e == 0), stop=(e == 1)),
                                 reads=[b_vs, b_pT], writes=[b_nd], signal=False)
                        for e in range(2):
                            S.op("pe", lambda en: en.matmul(nd[:, 128:256], lhsT=ones_bf[:], rhs=pT[:, e * 128:(e + 1) * 128],
                                                           start=(e == 0), stop=(e == 1)),
                                 reads=[b_ones, b_pT], writes=[b_nd], signal=(e == 1))
                        dst = acc[:, :, :].rearrange("p a (i r) -> p a r i", r=d)[:, :, rho, qb * 128:(qb + 1) * 128]
                        src = nd[:, 0:256].rearrange("p (a b) -> p a b", a=2)
                        if g == 0:
                            S.op("act", lambda en: en.activation(out=dst, in_=src, func=AF.Copy), reads=[b_nd], writes=[b_acc])
                        else:
                            S.op("dve", lambda en: en.tensor_tensor(out=dst, in0=dst, in1=src, op=ALU.add),
                                 reads=[b_nd, b_acc], writes=[b_acc])

                    blks = [(rho, qb) for rho in range(d) for qb in range(nqb)]
                    cur = stageA(*blks[0])
                    for bi, (rho, qb) in enumerate(blks):
                        nxtp = stageA(*blks[bi + 1]) if bi + 1 < len(blks) else None
                        stageB(rho, qb, *cur)
                        cur = nxtp
                sg, b_sg = sg_ring.next()
                S.dma("sp", sg[:], SGT[h * 128:(h + 1) * 128, T0:T0 + ST], reads=[bSGT], writes=[b_sg], track=b_sg)
                S.op("dve", lambda en: en.tensor_scalar(out=tmp[:], in0=acc[:, 1, :], scalar1=1e-30, scalar2=None, op0=ALU.add),
                     reads=[b_acc], writes=[b_tmp])
                S.op("dve", lambda en: en.reciprocal(out=tmp[:], in_=tmp[:]), reads=[b_tmp], writes=[b_tmp])
                S.op("dve", lambda en: en.tensor_tensor(out=tmp[:], in0=tmp[:], in1=acc[:, 0, :], op=ALU.mult),
                     reads=[b_tmp, b_acc], writes=[b_tmp])
                yo, b_yo = yo_ring.next()
                S.op("dve", lambda en: en.tensor_tensor(out=yo[:], in0=tmp[:], in1=sg[:], op=ALU.mult),
                     reads=[b_tmp, b_sg], writes=[b_yo])
                S.store(YT[h * 128:(h + 1) * 128, T0:T0 + ST], yo[:], reads=[b_yo], writes=[bYT], track=b_yo)

    rr = [0]

    def load_w_resident(dst, b_dst, W_rows_ap, kcn, ncols):
        for kc0 in range(0, kcn, KC // 2 if KC >= 2 else 1):
            kn = min(KC // 2 if KC >= 2 else 1, kcn - kc0)
            for c0 in range(0, ncols, TT):
                wst, b_wst = W["wst_ring"].next()
                S.dma("sp", wst[:, 0:kn, :], W_rows_ap[kc0 * 128:(kc0 + kn) * 128, c0:c0 + TT].rearrange("(kc p) f -> p kc f", p=128),
                      writes=[b_wst], track=b_wst)
                rr[0] += 1
                if rr[0] % 2 == 0:
                    S.op("act", lambda e: e.activation(out=dst[:, kc0:kc0 + kn, c0:c0 + TT], in_=wst[:, 0:kn, :], func=AF.Copy),
                         reads=[b_wst], writes=[b_dst])
                else:
                    S.op("dve", lambda e: e.tensor_copy(out=dst[:, kc0:kc0 + kn, c0:c0 + TT], in_=wst[:, 0:kn, :]),
                         reads=[b_wst], writes=[b_dst])

    def wout_pass(Xsrc, bXsrc, Xdst, bXdst, W_rows_ap, yT_rows0, wres, b_wres):
        load_w_resident(wres, b_wres, W_rows_ap, KC, D)
        yt_ring = Ring([sbuf([128, KC, TT], BF16, "yt") for _ in range(2)])
        for tb in range(T // 128):
            t0 = tb * 128
            if tb % 4 == 0:
                ytt, b_yt = yt_ring.next()
                S.dma("sp", ytt[:], YT[yT_rows0:yT_rows0 + D, t0:t0 + TT].rearrange("(kc p) t -> p kc t", p=128),
                      reads=[bYT], writes=[b_yt], track=b_yt)
            yt = ytt[:, :, (tb % 4) * 128:(tb % 4 + 1) * 128]
            xt, b_xt = W["xt_ring"].next()
            S.dma("sp", xt[:], Xsrc[t0:t0 + 128, :], reads=[bXsrc], writes=[b_xt], track=b_xt)
            for fb in range(D // TT):
                ps, b_ps = PS.next()
                for kc in range(KC):
                    S.op("pe", lambda e: e.matmul(ps[:], lhsT=yt[:, kc, :], rhs=wres[:, kc, fb * TT:(fb + 1) * TT],
                                                  start=(kc == 0), stop=(kc == KC - 1)),
                         reads=[b_yt, b_wres], writes=[b_ps], signal=(kc == KC - 1))
                S.op("dve", lambda e: e.tensor_tensor(out=xt[:, fb * TT:(fb + 1) * TT], in0=ps[:], in1=xt[:, fb * TT:(fb + 1) * TT], op=ALU.add),
                     reads=[b_ps, b_xt], writes=[b_xt])
            S.store(Xdst[t0:t0 + 128, :], xt[:], reads=[b_xt], writes=[bXdst], track=b_xt)

    def ple_pass(Xsrc, bXsrc, Xdst, bXdst, li, wres, b_wres, wp, b_wp):
        set_gain(cfg.DEPTH + li)
        load_w_resident(wres, b_wres, ple_gate[li], KC, D)
        PK = cfg.PLE // 128
        load_w_resident(wp, b_wp, ple_proj[li], PK, D)
        h2_ring = Ring([sbuf([128, KC, 128], BF16, "h2") for _ in range(2)])
        pin_ring = Ring([sbuf([128, cfg.PLE], F32, "pin") for _ in range(2)])
        pb_ring = Ring([sbuf([128, cfg.PLE], BF16, "pb") for _ in range(2)])
        pT_ring = Ring([sbuf([128, PK, 128], BF16, "ppT") for _ in range(2)])
        sig_ring = Ring([sbuf([128, TT], F32, "sig") for _ in range(2)])
        xo_ring = Ring([sbuf([128, D], F32, "xo") for _ in range(2)])
        for tb in range(T // 128):
            t0 = tb * 128
            h2, b_h2 = h2_ring.next()
            prep_block(Xsrc[t0:t0 + 128, :], bXsrc, h2, b_h2, 0)
            xo, b_xo = xo_ring.next()
            S.dma("sp", xo[:], Xsrc[t0:t0 + 128, :], reads=[bXsrc], writes=[b_xo], track=b_xo)
            pin, b_pin = pin_ring.next()
            S.dma("sp", pin[:], p_in[li, t0:t0 + 128, :], writes=[b_pin], track=b_pin)
            pb, b_pb = pb_ring.next()
            S.op("act", lambda e: e.activation(out=pb[:], in_=pin[:], func=AF.Copy), reads=[b_pin], writes=[b_pb])
            ptp, b_ptp = PTR.next()
            for k in range(PK):
                S.op("pe", lambda e: e.transpose(out=ptp[:, k * 128:(k + 1) * 128], in_=pb[:, k * 128:(k + 1) * 128], identity=ident[:]),
                     reads=[b_pb, b_ident], writes=[b_ptp], signal=(k == PK - 1))
            ppT, b_ppT = pT_ring.next()
            S.op("act", lambda e: e.activation(out=ppT[:], in_=ptp[:, 0:PK * 128].rearrange("p (k c) -> p k c", k=PK), func=AF.Copy),
                 reads=[b_ptp], writes=[b_ppT])
            for fb in range(D // TT):
                ps, b_ps = PS.next()
                for kc in range(KC):
                    S.op("pe", lambda e: e.matmul(ps[:], lhsT=h2[:, kc, :], rhs=wres[:, kc, fb * TT:(fb + 1) * TT],
                                                  start=(kc == 0), stop=(kc == KC - 1)),
                         reads=[b_h2, b_wres], writes=[b_ps], signal=(kc == KC - 1))
                sig, b_sig = sig_ring.next()
                S.op("act", lambda e: e.activation(out=sig[:], in_=ps[:], func=AF.Sigmoid), reads=[b_ps], writes=[b_sig])
                ps2, b_ps2 = PS.next()
                for k in range(PK):
                    S.op("pe", lambda e: e.matmul(ps2[:], lhsT=ppT[:, k, :], rhs=wp[:, k, fb * TT:(fb + 1) * TT],
                                                  start=(k == 0), stop=(k == PK - 1)),
                         reads=[b_ppT, b_wp], writes=[b_ps2], signal=(k == PK - 1))
                S.op("dve", lambda e: e.tensor_tensor(out=sig[:], in0=ps2[:], in1=sig[:], op=ALU.mult),
                     reads=[b_ps2, b_sig], writes=[b_sig])
                S.op("dve", lambda e: e.tensor_tensor(out=xo[:, fb * TT:(fb + 1) * TT], in0=xo[:, fb * TT:(fb + 1) * TT], in1=sig[:], op=ALU.add),
                     reads=[b_sig, b_xo], writes=[b_xo])
            S.store(Xdst[t0:t0 + 128, :], xo[:], reads=[b_xo], writes=[bXdst], track=b_xo)

    def fnet_proj(Xc, bXc, li, j):
        Wi = fnet_w_in[j]
        hT, b_hT = W["hT"]
        set_gain(li)
        for s in range(T // ST):
            for bb in range(ST // 128):
                t0 = s * ST + bb * 128
                prep_block(Xc[t0:t0 + 128, :], bXc, hT, b_hT, bb * 128)
            items = []
            for fb in range(cfg.FW // TT):
                def u_block(wb, b_wb, fb=fb, s=s):
                    for bb in range(ST // 128):
                        ps, b_ps = mm_tm(wb, b_wb, hT, b_hT, bb * 128)
                        o, b_o = W["ev_ring"].next()
                        S.op("act", lambda e: e.activation(out=o[:], in_=ps[:], func=AF.Copy), reads=[b_ps], writes=[b_o])
                        t0 = s * ST + bb * 128
                        uv = UUloc.rearrange("(c t) e -> t c e", t=T)
                        S.store(uv[t0:t0 + 128, fb * 8:(fb + 1) * 8, :], o[:].rearrange("t (c e) -> t c e", e=64),
                                reads=[b_o], writes=[bUU], track=b_o)
                items.append((Wi[:, fb * TT:(fb + 1) * TT], u_block))
            run_blocks(items)
        S.flush_stores()
        ag_chunks(UUloc.rearrange("(n b) e -> n (b e)", b=128), UUall[j].rearrange("(n b) e -> n (b e)", b=128),
                  PL * (T // 128), 128 * 64, [bUU], [bUUall])
        for s in range(T // ST):
            for bb in range(ST // 128):
                t0 = s * ST + bb * 128
                prep_block(Xc[t0:t0 + 128, :], bXc, hT, b_hT, bb * 128)
            items = []
            for fb in range(cfg.FW // TT):
                def fg_block(wb, b_wb, fb=fb, s=s):
                    for m in range(4):
                        for n in range(ST // TT):
                            ps, b_ps = mm_fm(wb, b_wb, m, hT, b_hT, n * TT, TT)
                            o, b_o = W["ev_ring"].next()
                            S.op("act", lambda e: e.activation(out=o[:], in_=ps[:], func=AF.Silu), reads=[b_ps], writes=[b_o])
                            fr = fb * TT + m * 128
                            S.store(SGT[fr:fr + 128, s * ST + n * TT:s * ST + (n + 1) * TT], o[:], reads=[b_o], writes=[bSGT], track=b_o)
                items.append((Wi[:, cfg.FW + fb * TT:cfg.FW + (fb + 1) * TT], fg_block))
            run_blocks(items)

    def fnet_dft(fi, dA, b_dA, dct, b_dct, dst_, b_dst_, dB, b_dB, dC3, b_dC3):
        AR = T // 128
        NA = 4 * AR
        CB = 64
        xx, b_xx = sbuf([128, 128, CB], BF16, "dx")
        y2, b_y2 = sbuf([128, 2, 2, 128, CB], BF16, "dy2")
        zs, b_zs = sbuf([64, 2, TS], BF16, "dzs")
        t_ring = Ring([sbuf([128, 2, 128], F32, "dt") for _ in range(4)])
        uall = UUall[fi].rearrange("(n b) e -> n (b e)", b=128)
        zt_ring = Ring([sbuf([128, 2, 2, TT], BF16, "zt") for _ in range(2)])

        def stage3(gi):
            for n in range(TS // TT):
                zt, b_zt = zt_ring.next()
                for kch in range(2):
                    r0 = gi * 256 + kch * 128
                    S.dma("sp", zt[:, kch, :, :], ZT[r0:r0 + 128, :, n * TT:(n + 1) * TT], reads=[bZT], writes=[b_zt], track=b_zt)
                for cob in range(2):
                    ps, b_ps = PS.next()
                    k = 0
                    for kch in range(2):
                        for r in range(2):
                            S.op("pe", lambda e: e.matmul(ps[:], lhsT=dC3[:, r, kch, cob * 128:(cob + 1) * 128], rhs=zt[:, kch, r, :],
                                                          start=(k == 0), stop=(k == 3)),
                                 reads=[b_dC3, b_zt], writes=[b_ps], signal=(k == 3))
                            k += 1
                    o, b_o = W["ev_ring"].next()
                    S.op("act", lambda e: e.activation(out=o[:], in_=ps[:], func=AF.Copy), reads=[b_ps], writes=[b_o])
                    q = (n * TT) // T
                    tl = (n * TT) % T
                    row = q * CHL + gi * 256 + cob * 128
                    S.store(MXloc[row:row + 128, tl:tl + TT], o[:], reads=[b_o], writes=[bMX], track=b_o)

        for i in range(NCBO):
            S.idma(xx[:].rearrange("a b c -> a (b c)"), uall, idxu[:, i:i + 1], reads=[bUUall, b_idxu], writes=[b_xx], track=b_xx)
            for c in range(CB):
                ps, b_ps = PS.next()
                S.op("pe", lambda e: e.matmul(ps[:], lhsT=xx[:, :, c], rhs=dA[:], start=True, stop=True),
                     reads=[b_xx, b_dA], writes=[b_ps])
                pv = ps[:].rearrange("p (q r j) -> p q r j", q=2, r=2)
                re1, im1 = pv[:, :, 0, :], pv[:, :, 1, :]
                ta, b_ta = t_ring.next()
                tb_, b_tb = t_ring.next()
                S.op("dve", lambda e: e.tensor_tensor(out=ta[:], in0=re1, in1=dct[:], op=ALU.mult), reads=[b_ps, b_dct], writes=[b_ta])
                S.op("dve", lambda e: e.tensor_tensor(out=tb_[:], in0=im1, in1=dst_[:], op=ALU.mult), reads=[b_ps, b_dst_], writes=[b_tb])
                S.op("dve", lambda e: e.tensor_tensor(out=y2[:, :, 0, :, c], in0=ta[:], in1=tb_[:], op=ALU.add),
                     reads=[b_ta, b_tb], writes=[b_y2])
                tc_, b_tc = t_ring.next()
                td, b_td = t_ring.next()
                S.op("dve", lambda e: e.tensor_tensor(out=tc_[:], in0=im1, in1=dct[:], op=ALU.mult), reads=[b_ps, b_dct], writes=[b_tc])
                S.op("dve", lambda e: e.tensor_tensor(out=td[:], in0=re1, in1=dst_[:], op=ALU.mult), reads=[b_ps, b_dst_], writes=[b_td])
                S.op("dve", lambda e: e.tensor_tensor(out=y2[:, :, 1, :, c], in0=tc_[:], in1=td[:], op=ALU.subtract),
                     reads=[b_tc, b_td], writes=[b_y2])
            for jj in range(128):
                ps, b_ps = PS.next()
                k = 0
                for sq in range(2):
                    for r in range(2):
                        S.op("pe", lambda e: e.matmul(ps[0:CB, 0:256], lhsT=y2[:, sq, r, jj, :], rhs=dB[:, sq, r, :],
                                                      start=(k == 0), stop=(k == 3)),
                             reads=[b_y2, b_dB], writes=[b_ps], signal=(k == 3))
                        k += 1
                src = ps[0:CB, 0:256].rearrange("p (r d) -> p r d", r=2)[:, :, 0:NA]
                dstz = zs[:, :, :].rearrange("c r (d j) -> c r j d", j=128)[:, :, jj, 0:NA]
                if jj % 2 == 0:
                    S.op("act", lambda e: e.activation(out=dstz, in_=src, func=AF.Copy), reads=[b_ps], writes=[b_zs])
                else:
                    S.op("dve", lambda e: e.tensor_copy(out=dstz, in_=src), reads=[b_ps], writes=[b_zs])
            S.store(ZT[i * CB:(i + 1) * CB, :, :], zs[:], reads=[b_zs], writes=[bZT], track=b_zs, lag=0)
            if (i + 1) % (256 // CB) == 0:
                gi = i // (256 // CB)
                stage3(gi)
                S.flush_stores()
                rpc = _rpc(4 * CHL, T)
                assert rpc <= 256
                ks = sorted({(q * CHL + gi * 256 + off) // rpc for q in range(4) for off in (0, 128)})
                ag_chunks(MXloc, MXall[fi], 4 * CHL, T, [bMX], [bMXall], ks=ks)

    def fnet_gate(fi):
        mx_ring = Ring([sbuf([128, T], BF16, "mx") for _ in range(2)])
        sg_ring = Ring([sbuf([128, T], BF16, "sgf") for _ in range(2)])
        yo_ring = Ring([sbuf([128, T], BF16, "yof") for _ in range(2)])
        for gb in range(cfg.FW // 128):
            mx, b_mx = mx_ring.next()
            S.idma(mx[:], MXall[fi][:, :], idxm[:, gb:gb + 1], reads=[bMXall, b_idxm], writes=[b_mx], track=b_mx)
            sg, b_sg = sg_ring.next()
            S.dma("sp", sg[:], SGT[gb * 128:(gb + 1) * 128, :], reads=[bSGT], writes=[b_sg], track=b_sg)
            yo, b_yo = yo_ring.next()
            S.op("dve", lambda e: e.tensor_tensor(out=yo[:], in0=mx[:], in1=sg[:], op=ALU.mult), reads=[b_mx, b_sg], writes=[b_yo])
            S.dma("sp", YT[gb * 128:(gb + 1) * 128, :], yo[:], reads=[b_yo], writes=[bYT], track=b_yo)

    cur, bcur = x_in, Buf("xin", True)
    pp = [(XA, bXA), (XB, bXB)]
    ppi = [0]

    def nxt(final=False):
        if final:
            return y_out, bY
        r = pp[ppi[0] % 2]
        ppi[0] += 1
        return r

    def mix_phase(src, bsrc, dst, bdst, W_rows, yrow0):
        with Phase():
            alloc_prep()
            wres, b_wres = sbuf([128, KC, D], BF16, "wres")
            wout_pass(src, bsrc, dst, bdst, W_rows, yrow0, wres, b_wres)

    try:
        for li in range(cfg.DEPTH):
            j = li // 2
            if li % 2 == 0:
                with Phase():
                    alloc_proj()
                    attn_proj(cur, bcur, li, j)
                with Phase():
                    attn_core(j)
                d1, bd1 = nxt()
                mix_phase(cur, bcur, d1, bd1, attn_w_out[j], 0)
            else:
                with Phase():
                    alloc_proj()
                    fnet_proj(cur, bcur, li, j)
                with Phase():
                    alloc_proj_ev()
                    dA, b_dA = sbuf([128, 512], BF16, "dA")
                    S.dma("sp", dA[:], dftA_in[:, :], writes=[b_dA], track=b_dA)
                    dct, b_dct = sbuf([128, 2, 128], F32, "dct")
                    S.dma("sp", dct[:], dftct_in[:, :, :], writes=[b_dct], track=b_dct)
                    dst_, b_dst_ = sbuf([128, 2, 128], F32, "dst")
                    S.dma("sp", dst_[:], dftst_in[:, :, :], writes=[b_dst_], track=b_dst_)
                    dB, b_dB = sbuf([128, 2, 2, 256], BF16, "dB")
                    S.dma("sp", dB[:], dftB_in[:, :, :, :], writes=[b_dB], track=b_dB)
                    dC3, b_dC3 = sbuf([128, 2, 2, 256], BF16, "dC3")
                    S.dma("sp", dC3[:], dftC3_in[:, :, :, :], writes=[b_dC3], track=b_dC3)
                    fnet_dft(j, dA, b_dA, dct, b_dct, dst_, b_dst_, dB, b_dB, dC3, b_dC3)
                with Phase():
                    fnet_gate(j)
                d0, bd0 = nxt()
                mix_phase(cur, bcur, d0, bd0, fnet_w_out[j][0:D, :], 0)
                d1, bd1 = nxt()
                mix_phase(d0, bd0, d1, bd1, fnet_w_out[j][D:2 * D, :], D)
            d2, bd2 = nxt(final=(li == cfg.DEPTH - 1))
            with Phase():
                alloc_prep()
                wres, b_wres = sbuf([128, KC, D], BF16, "wres")
                wp, b_wp = sbuf([128, max(cfg.PLE // 128, 1), D], BF16, "wp")
                ple_pass(d1, bd1, d2, bd2, li, wres, b_wres, wp, b_wp)
            cur, bcur = d2, bd2
    except _StopBuild:
        pass
    S.barrier()
    nc._sched_ninst = S.ninst
    return nc


_W_NAMES = ["norm_in", "attn_w_in", "attn_q_norm", "attn_k_norm", "attn_w_out", "fnet_w_in", "fnet_w_out",
            "ple_proj", "ple_gate", "ple_norm"]


def run_groups(cfg, groups, weights):
    nc = build(cfg)
    T = cfg.T
    in_maps = []
    wts = {k: np.ascontiguousarray(weights[k], dtype=np.float32) for k in _W_NAMES}
    for c in range(8):
        g, r = c // 4, c % 4
        x, p, nseq = groups[g]
        m = {"x": np.ascontiguousarray(x[r * T:(r + 1) * T]), "p": np.ascontiguousarray(p[:, r * T:(r + 1) * T])}
        m.update(wts)
        m.update(host_tables(cfg, r, nseq))
        in_maps.append(m)
    res = run_bass_kernel_spmd(nc, in_maps, core_ids=list(range(8)))
    return [np.concatenate([np.asarray(res.results[g * 4 + r]["y"]) for r in range(4)], axis=0) for g in range(2)]


def kernel(x_prompt, x_sample, p_prompt, p_sample, **weights):
    cfg = Cfg(T=4096)
    x_prompt = np.asarray(x_prompt, np.float32)
    x_sample = np.asarray(x_sample, np.float32)
    p_prompt = np.asarray(p_prompt, np.float32)
    p_sample = np.asarray(p_sample, np.float32)
    B, SQ, D = x_prompt.shape
    gA = (x_prompt.reshape(B * SQ, D), p_prompt.reshape(p_prompt.shape[0], B * SQ, -1), B)
    gB = (x_sample[0], p_sample[:, 0], 1)
    ys = run_groups(cfg, [gA, gB], weights)
    y_prompt = ys[0].reshape(B, SQ, D).astype(np.float32)
    y_sample = ys[1][None].astype(np.float32)
    return (y_prompt, y_sample)
```

```python
from contextlib import ExitStack
import numpy as np
import ml_dtypes
import concourse.bass as bass
import concourse.mybir as mybir
from concourse.bass_utils import run_bass_kernel_spmd

F32 = mybir.dt.float32
BF16 = mybir.dt.bfloat16
I32 = mybir.dt.int32
AF = mybir.ActivationFunctionType
ALU = mybir.AluOpType
EPS = 1e-6
DIL = (1, 4, 16)
ST = 2048
TT = 512


class Buf:
    __slots__ = ("name", "w", "r", "sem", "cnt", "multi")

    def __init__(self, name, multi=False):
        self.name = name
        self.w = [] if multi else None
        self.r = []
        self.sem = None
        self.cnt = 0
        self.multi = multi


def _prune(deps):
    best = {}
    for d in deps:
        k = id(d[0])
        if k not in best or best[k][1] < d[1]:
            best[k] = d
    return list(best.values())


class Sched:
    def __init__(self, nc):
        self.nc = nc
        self.E = {}
        for name, e in [("pe", nc.tensor), ("act", nc.scalar), ("dve", nc.vector),
                        ("pool", nc.gpsimd), ("sp", nc.sync)]:
            self.E[name] = dict(eng=e, sem=nc.alloc_semaphore("e_" + name), cnt=0, waited={})
        self.ninst = 0
        self.pending = []
        self.free_sems = []
        self.all_dma = {}

    def acquire(self, b):
        if self.free_sems:
            b.sem, b.cnt = self.free_sems.pop()
        else:
            b.sem, b.cnt = self.nc.alloc_semaphore("d_" + b.name), 0
        self.all_dma[id(b.sem)] = [b.sem, b.cnt]

    def release(self, bufs):
        for b in bufs:
            if b.sem is not None:
                self.free_sems.append((b.sem, b.cnt))
                b.sem = None

    def barrier(self):
        self.flush_stores()
        for en, E in self.E.items():
            deps = [(F["sem"], F["cnt"], fn) for fn, F in self.E.items() if fn != en and F["cnt"] > 0]
            deps += [(sem, cnt, "dma") for sem, cnt in self.all_dma.values() if cnt > 0]
            self._wait(en, deps)

    def _wait(self, en, deps):
        E = self.E[en]
        need = {}
        for d in deps:
            if d is None:
                continue
            sem, val, owner = d
            if owner == en:
                continue
            k = id(sem)
            if k not in need or need[k][1] < val:
                need[k] = (sem, val)
        for k, (sem, val) in need.items():
            if E["waited"].get(k, 0) >= val:
                continue
            E["eng"].wait_ge(sem, val)
            E["waited"][k] = val
            self.ninst += 1

    @staticmethod
    def _deps(reads, writes):
        deps = []
        for b in reads:
            if b.multi:
                deps.extend(b.w)
            else:
                deps.append(b.w)
        for b in writes:
            if b.multi:
                deps.extend(b.w)
            else:
                deps.append(b.w)
            deps.extend(b.r)
        return deps

    @staticmethod
    def _update(dep, reads, writes):
        for b in writes:
            if b.multi:
                b.w.append(dep)
                if len(b.w) > 12:
                    b.w = _prune(b.w)
            else:
                b.w = dep
            b.r = []
        for b in reads:
            b.r.append(dep)
            if len(b.r) > 12:
                b.r = _prune(b.r)

    def op(self, en, fn, reads=(), writes=(), signal=True):
        E = self.E[en]
        self._wait(en, self._deps(reads, writes))
        ins = fn(E["eng"])
        self.ninst += 1
        dep = (E["sem"], E["cnt"] + 1, en)
        if signal:
            ins.then_inc(E["sem"], 1)
            E["cnt"] += 1
        self._update(dep, reads, writes)
        return ins

    def dma(self, qn, out_ap, in_ap, reads=(), writes=(), track=None, strict=False):
        E = self.E[qn]
        self._wait(qn, self._deps(reads, writes))
        tb = track
        if tb.sem is None:
            self.acquire(tb)
        ins = E["eng"].dma_start(out=out_ap, in_=in_ap)
        ins.then_inc(tb.sem, 16)
        tb.cnt += 16
        self.all_dma[id(tb.sem)][1] = tb.cnt
        self.ninst += 1
        dep = (tb.sem, tb.cnt, "dma")
        self._update(dep, reads, writes)
        return ins

    def collective(self, in_ap, out_ap, groups, reads=(), writes=()):
        E = self.E["pool"]
        self._wait("pool", self._deps(reads, writes))
        if getattr(self, "cc_sem", None) is None:
            self.cc_sem = self.nc.alloc_semaphore("cc_sem")
            self.cc_cnt = 0
        ins = E["eng"].collective_compute("AllGather", ALU.bypass, replica_groups=groups, ins=[in_ap], outs=[out_ap])
        ins.then_inc(self.cc_sem, 1)
        self.cc_cnt += 1
        self.all_dma[id(self.cc_sem)] = [self.cc_sem, self.cc_cnt]
        self.ninst += 1
        self._update((self.cc_sem, self.cc_cnt, "cc"), reads, writes)

    def idma(self, out_ap, in_ap, idx_ap, reads=(), writes=(), track=None):
        E = self.E["pool"]
        self._wait("pool", self._deps(reads, writes))
        tb = track
        if tb.sem is None:
            self.acquire(tb)
        ins = E["eng"].indirect_dma_start(out=out_ap, out_offset=None, in_=in_ap,
                                          in_offset=bass.IndirectOffsetOnAxis(ap=idx_ap, axis=0))
        ins.then_inc(tb.sem, 16)
        tb.cnt += 16
        self.all_dma[id(tb.sem)][1] = tb.cnt
        self.ninst += 1
        self._update((tb.sem, tb.cnt, "dma"), reads, writes)

    def store(self, out_ap, in_ap, reads=(), writes=(), track=None, lag=1):
        self.pending.append((out_ap, in_ap, reads, writes, track))
        while len(self.pending) > lag:
            self._emit_store()

    def _emit_store(self):
        out_ap, in_ap, reads, writes, track = self.pending.pop(0)
        self.dma("sp", out_ap, in_ap, reads=reads, writes=writes, track=track, strict=True)

    def flush_stores(self):
        while self.pending:
            self._emit_store()

    def wait_all(self, en, bufs):
        deps = []
        for b in bufs:
            deps.extend(b.w if b.multi else [b.w])
            deps.extend(b.r)
        self._wait(en, deps)


class _StopBuild(Exception):
    pass


class Ring:
    def __init__(self, items):
        self.items = items
        self.i = 0

    def next(self):
        it = self.items[self.i % len(self.items)]
        self.i += 1
        return it


class Cfg:
    def __init__(self, T=16384, D=2048, HPG=16, FG=16, PLE=256, DEPTH=4):
        self.T, self.D, self.HPG, self.FG, self.PLE, self.DEPTH = T, D, HPG, FG, PLE, DEPTH
        self.KC = D // 128
        self.NH = 3 * HPG
        self.QKV = self.NH * 128
        self.AW = HPG * 128
        self.AIN = 3 * self.QKV + self.AW
        self.FW = 2 * D
        self.FGD = self.FW // FG
        assert self.FGD == 256 and T % ST == 0
        self.NATT = (DEPTH + 1) // 2
        self.NFN = max(DEPTH // 2, 1)
        self.R = 4
        self.TS = 4 * T
        self.PL = self.FW // 64
        self.NCBO = self.PL // 4
        self.CHL = self.FW // 4
        assert self.CHL % 256 == 0
        self.NKB = [T // d // 128 + 1 for d in DIL]
        self.NKB_OFF = [0, self.NKB[0], self.NKB[0] + self.NKB[1]]
        self.NKBT = sum(self.NKB)


CH_EL = 512 * 1024


def _rpc(nr, rl):
    return max(1, min(nr, CH_EL // rl))


def _grow(r_loc, rank, nr, rl):
    rpc = _rpc(nr, rl)
    return ((r_loc // rpc) * 4 + rank) * rpc + (r_loc % rpc)


def host_tables(cfg, r, nseq):
    T, TS = cfg.T, cfg.TS
    L = TS // nseq
    bf = ml_dtypes.bfloat16
    tabs = {}
    seq_of_core = (r * T) // L
    vtn = np.zeros((128, cfg.NKBT), np.float32)
    kk = np.arange(128)[:, None]
    for g, d in enumerate(DIL):
        Lq = T // d
        j = np.arange(cfg.NKB[g])[None, :]
        X = r * Lq + 128 * j - 64 + kk
        vtn[:, cfg.NKB_OFF[g]:cfg.NKB_OFF[g] + cfg.NKB[g]] = ((X >= seq_of_core * (L // d)) & (X < (seq_of_core + 1) * (L // d))).astype(np.float32)
    tabs["vtn"] = vtn
    tabs["vts"] = np.ascontiguousarray(np.roll(vtn, 64, axis=0))
    slopes = np.exp2(-8.0 * np.arange(1, cfg.NH + 1, dtype=np.float64) / cfg.NH)
    m = np.zeros((128, cfg.NH, 3, 128), np.float64)
    qq = np.arange(128)[None, :]
    for g, d in enumerate(DIL):
        for h in range(cfg.HPG):
            for e in range(2):
                rel = 128 * e - 64 + kk - qq
                m[:, g * cfg.HPG + h, e, :] = np.where(np.abs(rel) <= 64,
                                                       np.exp(-slopes[g * cfg.HPG + h] * d * np.abs(rel)), 0.0)
            m[:, g * cfg.HPG + h, 2, :] = np.roll(m[:, g * cfg.HPG + h, 1, :], 64, axis=0)
    tabs["mtab"] = m.reshape(128, cfg.NH, 384).astype(np.float32)
    H = cfg.HPG
    p = np.arange(128)
    idxk = np.zeros((128, 3, 2 * H), np.int32)
    idxv = np.zeros((128, 3, 2 * H), np.int32)
    for g, d in enumerate(DIL):
        for side in range(2):
            nb = min(max(r - 1 if side == 0 else r + 1, 0), 3)
            src = 1 - side
            for h in range(H):
                idxk[:, g, side * H + h] = _grow((h * 2 + src) * 128 + p, nb, H * 2 * 128, d * 64)
                idxv[:, g, side * H + h] = _grow((h * 2 + src) * 64 + (p % 64), nb, H * 2 * 64, d * 128)
    tabs["idxk"] = idxk
    tabs["idxv"] = idxv
    AR = T // 128
    a = np.arange(128)
    rk = np.minimum(a // AR, 3)
    idxu = np.zeros((128, cfg.NCBO), np.int32)
    for i in range(cfg.NCBO):
        cb = r * cfg.NCBO + i
        idxu[:, i] = np.where(a < 4 * AR, _grow(cb * AR + (a % AR), rk, cfg.PL * AR, 128 * 64), 0)
    tabs["idxu"] = idxu
    ngb = cfg.FW // 128
    idxm = np.zeros((128, ngb), np.int32)
    per = cfg.CHL // 128
    for gb in range(ngb):
        idxm[:, gb] = _grow(r * cfg.CHL + (gb % per) * 128 + p, gb // per, 4 * cfg.CHL, T)
    tabs["idxm"] = idxm
    N = L
    N1 = N // 128
    aa = np.arange(128)[:, None]
    jj = np.arange(128)[None, :]
    c = jj % N1
    e = jj // N1
    dA = np.zeros((128, 2, 2, 128), np.float64)
    dB = np.zeros((128, 2, 2, 256), np.float64)
    b = np.arange(128)[:, None]
    dd = np.arange(128)[None, :]
    for sq in range(nseq):
        a0 = sq * N1
        ang = 2 * np.pi * ((aa - a0) * c) / N1
        ina = (aa >= a0) & (aa < a0 + N1)
        dA[:, sq, 0, :] = np.where(ina, np.cos(ang), 0.0)
        dA[:, sq, 1, :] = np.where(ina, -np.sin(ang), 0.0)
        angb = 2 * np.pi * (b * (dd - a0)) / N1
        ind = (dd >= a0) & (dd < a0 + N1)
        C2 = np.where(ind, np.cos(angb), 0.0)
        S2 = np.where(ind, np.sin(angb), 0.0)
        dB[:, sq, 0, :] = np.concatenate([C2, -S2], axis=1)
        dB[:, sq, 1, :] = np.concatenate([S2, C2], axis=1)
    tabs["dftA"] = dA.reshape(128, 512).astype(bf)
    tabs["dftB"] = dB.astype(bf)
    angt = 2 * np.pi * (b * c / N + b * e / 128.0)
    ct = np.cos(angt)
    st = np.sin(angt)
    tabs["dftct"] = np.stack([ct, ct], axis=1).astype(np.float32)
    tabs["dftst"] = np.stack([st, st], axis=1).astype(np.float32)
    ci = np.arange(256)[:, None]
    co = np.arange(256)[None, :]
    ang3 = 2 * np.pi * (ci * co) / 256.0
    sc = 1.0 / np.sqrt(float(N) * 256.0)
    C3 = (np.cos(ang3) * sc).reshape(2, 128, 256).transpose(1, 0, 2)
    S3 = (np.sin(ang3) * sc).reshape(2, 128, 256).transpose(1, 0, 2)
    tabs["dftC3"] = np.ascontiguousarray(np.stack([C3, S3], axis=1)).astype(bf)
    tabs["ident"] = np.eye(128, dtype=np.float32).astype(bf)
    return tabs


def build(cfg):
    T, D, KC, HPG, NH = cfg.T, cfg.D, cfg.KC, cfg.HPG, cfg.NH
    nc = bass.Bass("TRN2", target_bir_lowering=False, num_devices=8)
    S = Sched(nc)
    GROUPS = [[0, 1, 2, 3], [4, 5, 6, 7]]
    TS, PL, NCBO, CHL = cfg.TS, cfg.PL, cfg.NCBO, cfg.CHL

    def din(name, shape, dt=F32):
        return nc.dram_tensor(name, list(shape), dt, kind="ExternalInput").ap()

    x_in = din("x", [T, D])
    p_in = din("p", [cfg.DEPTH, T, cfg.PLE])
    norm_in = din("norm_in", [cfg.DEPTH, D])
    attn_w_in = din("attn_w_in", [cfg.NATT, D, cfg.AIN])
    qn_in = din("attn_q_norm", [cfg.NATT, 128])
    kn_in = din("attn_k_norm", [cfg.NATT, 128])
    attn_w_out = din("attn_w_out", [cfg.NATT, cfg.AW, D])
    fnet_w_in = din("fnet_w_in", [cfg.NFN, D, 2 * cfg.FW])
    fnet_w_out = din("fnet_w_out", [cfg.NFN, cfg.FW, D])
    ple_proj = din("ple_proj", [cfg.DEPTH, cfg.PLE, D])
    ple_gate = din("ple_gate", [cfg.DEPTH, D, D])
    ple_norm = din("ple_norm", [cfg.DEPTH, D])
    vtn_in = din("vtn", [128, cfg.NKBT])
    vts_in = din("vts", [128, cfg.NKBT])
    mtab_in = din("mtab", [128, NH, 384])
    idxk_in = din("idxk", [128, 3, 2 * HPG], I32)
    idxv_in = din("idxv", [128, 3, 2 * HPG], I32)
    idxu_in = din("idxu", [128, NCBO], I32)
    idxm_in = din("idxm", [128, cfg.FW // 128], I32)
    dftA_in = din("dftA", [128, 512], BF16)
    dftct_in = din("dftct", [128, 2, 128])
    dftst_in = din("dftst", [128, 2, 128])
    dftB_in = din("dftB", [128, 2, 2, 256], BF16)
    dftC3_in = din("dftC3", [128, 2, 2, 256], BF16)
    ident_in = din("ident", [128, 128], BF16)
    y_out = nc.dram_tensor("y", [T, D], F32, kind="ExternalOutput").ap()

    all_inputs = [x_in, p_in, norm_in, attn_w_in, qn_in, kn_in, attn_w_out, fnet_w_in, fnet_w_out, ple_proj, ple_gate,
                  ple_norm, vtn_in, vts_in, mtab_in, idxk_in, idxv_in, idxu_in, idxm_in, dftA_in, dftct_in, dftst_in,
                  dftB_in, dftC3_in, ident_in]

    def touch_inputs():
        tiles = {}
        for ap in all_inputs:
            a = ap
            while len(a.shape) > 2:
                a = a[0]
            n = min(16, a.shape[-1])
            key = str(ap.dtype)
            if key not in tiles:
                tiles[key] = (nc.alloc_sbuf_tensor("touch_" + str(len(tiles)), [1, 16], ap.dtype), Buf("touch" + str(len(tiles))))
            t, b = tiles[key]
            S.dma("sp", t[0:1, 0:n], a[0:1, 0:n], writes=[b], track=b)

    def dscr(name, shape, dt):
        return nc.dram_tensor(name, list(shape), dt, kind="Internal").ap()

    XA = dscr("XA", [T, D], F32)
    XB = dscr("XB", [T, D], F32)
    bXA, bXB, bY = Buf("XA", True), Buf("XB", True), Buf("Y", True)
    QT, KTs, VS = [], [], []
    HKin, HKout, HVin, HVout = [], [], [], []
    for g, d in enumerate(DIL):
        LP = T // d
        QT.append(dscr(f"QT{g}", [HPG, 128, d, LP], BF16))
        KTs.append(dscr(f"KT{g}", [HPG, 128, d, LP], BF16))
        VS.append(dscr(f"VS{g}", [d, LP, HPG * 128], BF16))
        HKin.append(dscr(f"HKin{g}", [HPG * 2 * 128, d * 64], BF16))
        HVin.append(dscr(f"HVin{g}", [HPG * 2 * 64, d * 128], BF16))
        HKout.append([dscr(f"HKout{g}_{a}", [4 * HPG * 2 * 128, d * 64], BF16) for a in range(cfg.NATT)])
        HVout.append([dscr(f"HVout{g}_{a}", [4 * HPG * 2 * 64, d * 128], BF16) for a in range(cfg.NATT)])
    bQKV = Buf("QKV", True)
    bHin, bHout = [Buf(f"Hin{g}", True) for g in range(3)], Buf("Hout", True)
    SGT = dscr("SGT", [cfg.FW, T], BF16)
    bSGT = Buf("SGT", True)
    YT = dscr("YT", [cfg.FW, T], BF16)
    bYT = Buf("YT", True)
    UUloc = dscr("UUloc", [PL * T, 64], BF16)
    UUall = [dscr(f"UUall{a}", [4 * PL * T, 64], BF16) for a in range(max(cfg.NFN, 1))]
    bUU, bUUall = Buf("UU", True), Buf("UUall", True)
    ZT = dscr("ZT", [CHL, 2, TS], BF16)
    bZT = Buf("ZT", True)
    MXloc = dscr("MXloc", [4 * CHL, T], BF16)
    MXall = [dscr(f"MXall{a}", [16 * CHL, T], BF16) for a in range(max(cfg.NFN, 1))]
    bMX, bMXall = Buf("MX", True), Buf("MXall", True)

    sb_i = [0]
    nph = [0]
    phase = [None]

    def sbuf(shape, dt, name=None):
        sb_i[0] += 1
        nm = f"{name or 't'}_{sb_i[0]}"
        b = Buf(nm)
        if phase[0] is None:
            return nc.alloc_sbuf_tensor(nm, list(shape), dt), b
        t = phase[0][0].enter_context(nc.sbuf_tensor(nm, list(shape), dt))
        phase[0][1].append(b)
        return t, b

    class Phase:
        def __enter__(self):
            self.es = ExitStack()
            phase[0] = (self.es, [])
            return self

        def __exit__(self, *a):
            S.barrier()
            S.release(phase[0][1])
            self.es.close()
            phase[0] = None
            if a[0] is None:
                nph[0] += 1
                if nph[0] == getattr(cfg, "stop_after", -1):
                    raise _StopBuild()
            return False

    PS = Ring([(nc.alloc_psum_tensor(f"ps{i}", [128, 512], F32), Buf(f"ps{i}")) for i in range(6)])
    PTR = Ring([(nc.alloc_psum_tensor(f"pt{i}", [128, 1024], BF16), Buf(f"pt{i}")) for i in range(2)])

    touch_inputs()
    ident, b_ident = sbuf([128, 128], BF16, "ident")
    S.dma("sp", ident[:], ident_in[:, :], writes=[b_ident], track=b_ident)
    ones_bf, b_ones = sbuf([128, 128], BF16, "ones")
    S.op("dve", lambda e: e.memset(ones_bf[:], 1.0), writes=[b_ones])
    vtn, b_vt = sbuf([128, cfg.NKBT], F32, "vtn")
    S.dma("sp", vtn[:], vtn_in[:, :], writes=[b_vt], track=b_vt)
    vts, b_vts = sbuf([128, cfg.NKBT], F32, "vts")
    S.dma("sp", vts[:], vts_in[:, :], writes=[b_vts], track=b_vts)
    idxk, b_idxk = sbuf([128, 3, 2 * HPG], I32, "idxk")
    S.dma("sp", idxk[:], idxk_in[:, :, :], writes=[b_idxk], track=b_idxk)
    idxv, b_idxv = sbuf([128, 3, 2 * HPG], I32, "idxv")
    S.dma("sp", idxv[:], idxv_in[:, :, :], writes=[b_idxv], track=b_idxv)

    def ag_chunks(in_t, out_t, nr, rl, reads, writes):
        rpc = _rpc(nr, rl)
        for k in range(nr // rpc):
            ic = in_t[k * rpc:(k + 1) * rpc, :]
            oc = out_t[k * 4 * rpc:(k + 1) * 4 * rpc, :]
            if rl < 512 and (rpc * rl) % 512 == 0 and rpc % (512 // rl) == 0:
                b = 512 // rl
                ic = ic.rearrange("(a b) e -> a (b e)", b=b)
                oc = oc.rearrange("(a b) e -> a (b e)", b=b)
            elif rl > 512 and rl % 512 == 0:
                ic = ic.rearrange("a (b e) -> (a b) e", e=512)
                oc = oc.rearrange("a (b e) -> (a b) e", e=512)
            S.collective(ic, oc, GROUPS, reads=reads, writes=writes)
    idxu, b_idxu = sbuf([128, NCBO], I32, "idxu")
    S.dma("sp", idxu[:], idxu_in[:, :], writes=[b_idxu], track=b_idxu)
    idxm, b_idxm = sbuf([128, cfg.FW // 128], I32, "idxm")
    S.dma("sp", idxm[:], idxm_in[:, :], writes=[b_idxm], track=b_idxm)
    NV = 2 * cfg.DEPTH
    gvec, b_gvec = sbuf([128, NV, KC], F32, "gvec")
    with nc.allow_non_contiguous_dma(reason="tiny gain vectors"):
        for i in range(cfg.DEPTH):
            S.dma("sp", gvec[:, i, :], norm_in[i].rearrange("(kc p) -> p kc", p=128), writes=[b_gvec], track=b_gvec)
            S.dma("sp", gvec[:, cfg.DEPTH + i, :], ple_norm[i].rearrange("(kc p) -> p kc", p=128),
                  writes=[b_gvec], track=b_gvec)
        qkg, b_qkg = sbuf([128, 2 * cfg.NATT], F32, "qkg")
        for j in range(cfg.NATT):
            S.dma("sp", qkg[:, 2 * j:2 * j + 1], qn_in[j].rearrange("(p o) -> p o", o=1), writes=[b_qkg], track=b_qkg)
            S.dma("sp", qkg[:, 2 * j + 1:2 * j + 2], kn_in[j].rearrange("(p o) -> p o", o=1), writes=[b_qkg], track=b_qkg)
    eps128, b_eps = sbuf([128, 1], F32, "eps128")
    S.op("dve", lambda e: e.memset(eps128[:], EPS), writes=[b_eps])
    gfull, b_gfull = sbuf([128, KC, 128], F32, "gfull")
    onesf, b_onesf = sbuf([128, 128], F32, "onesf")
    S.op("dve", lambda e: e.memset(onesf[:], 1.0), writes=[b_onesf])

    def set_gain(vi):
        for kc in range(KC):
            S.op("dve", lambda e: e.tensor_scalar(out=gfull[:, kc, :], in0=onesf[:], scalar1=gvec[:, vi, kc:kc + 1],
                                                  scalar2=float(D) ** 0.5, op0=ALU.mult, op1=ALU.mult),
                 reads=[b_onesf, b_gvec], writes=[b_gfull])

    stat_ring = Ring([sbuf([128, 4], F32, "stat") for _ in range(4)])
    W = {}

    def alloc_prep():
        W["xt_ring"] = Ring([sbuf([128, D], F32, "xt") for _ in range(2)])
        W["xs_ring"] = Ring([sbuf([128, D], BF16, "xs") for _ in range(2)])
        W["xs_hi"] = {}
        W["junk"] = sbuf([128, D], BF16, "junk")
        W["wst_ring"] = Ring([sbuf([128, max(KC // 2, 1), TT], F32, "wst") for _ in range(2)])

    def alloc_proj_ev():
        W["ev_ring"] = Ring([sbuf([128, TT], BF16, "ev") for _ in range(3)])

    def alloc_proj():
        alloc_prep()
        W["hT"] = sbuf([128, KC, ST], BF16, "hT")
        W["wb_ring"] = Ring([sbuf([128, KC, TT], BF16, "wb") for _ in range(2)])
        W["ev_ring"] = Ring([sbuf([128, TT], BF16, "ev") for _ in range(3)])
        W["evf_ring"] = Ring([sbuf([128, TT], F32, "evf") for _ in range(3)])
        W["o_ring"] = Ring([sbuf([128, TT], BF16, "o") for _ in range(3)])
        W["qr_ring"] = Ring([sbuf([128, TT], F32, "qr") for _ in range(3)])

    def rstd_from_ssq(ssq_ap, b_ssq, n, out_ap, b_out):
        S.op("dve", lambda e: e.tensor_scalar(out=out_ap, in0=ssq_ap, scalar1=float(n) * EPS, scalar2=None,
                                              op0=ALU.add), reads=[b_ssq], writes=[b_out])
        S.op("act", lambda e: e.activation(out=out_ap, in_=out_ap, func=AF.Sqrt), reads=[b_out], writes=[b_out])
        S.op("dve", lambda e: e.reciprocal(out=out_ap, in_=out_ap), reads=[b_out], writes=[b_out])

    def prep_block(src_rows_ap, b_src, dst, b_dst, col0, kcn=KC):
        xt, b_xt = W["xt_ring"].next()
        S.dma("sp", xt[:], src_rows_ap, reads=[b_src], writes=[b_xt], track=b_xt)
        st, b_st = stat_ring.next()
        junk, b_junk = W["junk"]
        S.op("act", lambda e: e.activation(out=junk[:], in_=xt[:], func=AF.Square, accum_out=st[:, 0:1]),
             reads=[b_xt], writes=[b_junk, b_st])
        rstd_from_ssq(st[:, 0:1], b_st, D, st[:, 1:2], b_st)
        xs, b_xs = W["xs_ring"].next()
        b_xh = W["xs_hi"].setdefault(id(b_xs), Buf("xs_hi"))
        Hh = D // 2
        S.op("act", lambda e: e.activation(out=xs[:, 0:Hh], in_=xt[:, 0:Hh], func=AF.Copy, scale=st[:, 1:2]),
             reads=[b_xt, b_st], writes=[b_xs])
        S.op("dve", lambda e: e.tensor_scalar(out=xs[:, Hh:D], in0=xt[:, Hh:D], scalar1=st[:, 1:2], scalar2=None, op0=ALU.mult),
             reads=[b_xt, b_st], writes=[b_xh])
        for half in range(KC // 8 if KC >= 8 else 1):
            n = min(8, KC)
            pt, b_pt = PTR.next()
            for k in range(n):
                kc = half * 8 + k
                S.op("pe", lambda e: e.transpose(out=pt[:, k * 128:(k + 1) * 128], in_=xs[:, kc * 128:(kc + 1) * 128],
                                                 identity=ident[:]),
                     reads=[b_xs, b_xh, b_ident], writes=[b_pt], signal=(k == n - 1))
            S.op("dve", lambda e: e.tensor_tensor(out=dst[:, half * 8:half * 8 + n, col0:col0 + 128],
                                                  in0=pt[:, 0:n * 128].rearrange("p (k c) -> p k c", k=n),
                                                  in1=gfull[:, half * 8:half * 8 + n, :], op=ALU.mult),
                 reads=[b_pt, b_gfull], writes=[b_dst])

    def load_w(w_ap_cols, b_w=None):
        wb, b_wb = W["wb_ring"].next()
        nh = 2 if KC >= 2 else 1
        kh = KC // nh
        for hh in range(nh):
            wst, b_wst = W["wst_ring"].next()
            S.dma("sp", wst[:, 0:kh, :], w_ap_cols[hh * kh * 128:(hh + 1) * kh * 128, :].rearrange("(kc p) f -> p kc f", p=128),
                  writes=[b_wst], track=b_wst)
            step = 2 if kh >= 2 else 1
            for q0 in range(0, kh, step):
                if (q0 // step) % 2 == 0:
                    S.op("act", lambda e: e.activation(out=wb[:, hh * kh + q0:hh * kh + q0 + step, :], in_=wst[:, q0:q0 + step, :], func=AF.Copy),
                         reads=[b_wst], writes=[b_wb])
                else:
                    S.op("dve", lambda e: e.tensor_copy(out=wb[:, hh * kh + q0:hh * kh + q0 + step, :], in_=wst[:, q0:q0 + step, :]),
                         reads=[b_wst], writes=[b_wb])
        return wb, b_wb

    def run_blocks(items):
        if not items:
            return
        nxt = load_w(items[0][0])
        for k, (w_ap, fn) in enumerate(items):
            cur = nxt
            if k + 1 < len(items):
                nxt = load_w(items[k + 1][0])
            fn(*cur)

    def mm_fm(wb, b_wb, m, src, b_src, c0, ncols, kcn=KC):
        ps, b_ps = PS.next()
        for kc in range(kcn):
            S.op("pe", lambda e: e.matmul(ps[:, 0:ncols], lhsT=wb[:, kc, m * 128:(m + 1) * 128],
                                          rhs=src[:, kc, c0:c0 + ncols], start=(kc == 0), stop=(kc == kcn - 1)),
                 reads=[b_wb, b_src], writes=[b_ps], signal=(kc == kcn - 1))
        return ps, b_ps

    def mm_tm(wb, b_wb, src, b_src, c0, kcn=KC, ncols=TT):
        ps, b_ps = PS.next()
        for kc in range(kcn):
            S.op("pe", lambda e: e.matmul(ps[:, 0:ncols], lhsT=src[:, kc, c0:c0 + 128], rhs=wb[:, kc, 0:ncols],
                                          start=(kc == 0), stop=(kc == kcn - 1)),
                 reads=[b_wb, b_src], writes=[b_ps], signal=(kc == kcn - 1))
        return ps, b_ps

    def attn_proj(Xc, bXc, li, j):
        Wi = attn_w_in[j]
        hT, b_hT = W["hT"]
        for g, d in enumerate(DIL):
            Lq = T // d
            nst = T // ST
            for s in range(nst):
                per = Lq // ST if Lq >= ST else 0
                blocks = []
                if Lq >= ST:
                    rho, pos0 = s // per, (s % per) * ST
                    blocks = [(rho, pos0, ST)]
                else:
                    nr = ST // Lq
                    blocks = [(s * nr + r, 0, Lq) for r in range(nr)]
                set_gain(li) if (g == 0 and s == 0) else None
                col = 0
                for (rho, pos0, npos) in blocks:
                    for bb in range(npos // 128):
                        t0 = rho + d * (pos0 + bb * 128)
                        pp0 = pos0 + bb * 128
                        rows = Xc.rearrange("(i r) c -> r i c", r=d)[rho, pp0:pp0 + 128, :]
                        prep_block(rows, bXc, hT, b_hT, col)
                        col += 128
                pendB = [None]

                def flushB():
                    if pendB[0] is not None:
                        f = pendB[0]
                        pendB[0] = None
                        f()

                items = []
                for which in range(2):
                  for hb in range(HPG // 4):
                    def qk_block(wb, b_wb, which=which, hb=hb):
                        dstT = QT[g] if which == 0 else KTs[g]
                        for m in range(4):
                            h = hb * 4 + m
                            for n in range(ST // TT):
                                ps, b_ps = mm_fm(wb, b_wb, m, hT, b_hT, n * TT, TT)
                                sq, b_sq = W["ev_ring"].next()
                                S.op("act", lambda e: e.activation(out=sq[:], in_=ps[:], func=AF.Square),
                                     reads=[b_ps], writes=[b_sq])
                                qr, b_qr = W["qr_ring"].next()
                                S.op("dve", lambda e: e.tensor_copy(out=qr[:], in_=ps[:]), reads=[b_ps, b_sq], writes=[b_qr])
                                flushB()

                                def partB(ps=qr, b_ps=b_qr, sq=sq, b_sq=b_sq, h=h, n=n, which=which, dstT=dstT):
                                    ps2, b_ps2 = PS.next()
                                    S.op("pe", lambda e: e.matmul(ps2[:], lhsT=ones_bf[:], rhs=sq[:], start=True, stop=True),
                                         reads=[b_ones, b_sq], writes=[b_ps2])
                                    rs, b_rs = W["evf_ring"].next()
                                    S.op("act", lambda e: e.activation(out=rs[:], in_=ps2[:], func=AF.Sqrt, bias=eps128[:, 0:1], scale=1.0 / 128),
                                         reads=[b_ps2, b_eps], writes=[b_rs])
                                    S.op("dve", lambda e: e.reciprocal(out=rs[:], in_=rs[:]), reads=[b_rs], writes=[b_rs])
                                    o, b_o = W["o_ring"].next()
                                    S.op("dve", lambda e: e.scalar_tensor_tensor(out=o[:], in0=ps[:], scalar=qkg[:, 2 * j + which:2 * j + which + 1],
                                                                                 in1=rs[:], op0=ALU.mult, op1=ALU.mult),
                                         reads=[b_ps, b_rs, b_qkg], writes=[b_o])
                                    c = n * TT
                                    cc = 0
                                    for (rho, pos0, npos) in blocks:
                                        lo, hi = max(c, cc), min(c + TT, cc + npos)
                                        if lo < hi:
                                            S.store(dstT[h, :, rho, pos0 + (lo - cc):pos0 + (hi - cc)], o[:, lo - c:hi - c],
                                                    reads=[b_o], writes=[bQKV], track=b_o)
                                            if which == 1:
                                                if pos0 == 0 and lo == cc:
                                                    S.store(HKin[g][(h * 2) * 128:(h * 2 + 1) * 128, rho * 64:(rho + 1) * 64],
                                                            o[:, lo - c:lo - c + 64], reads=[b_o], writes=[bHin[g]], track=b_o)
                                                if pos0 + npos == Lq and hi == cc + npos:
                                                    S.store(HKin[g][(h * 2 + 1) * 128:(h * 2 + 2) * 128, rho * 64:(rho + 1) * 64],
                                                            o[:, hi - c - 64:hi - c], reads=[b_o], writes=[bHin[g]], track=b_o)
                                        cc += npos
                                pendB[0] = partB
                    f0 = which * cfg.QKV + (g * HPG + hb * 4) * 128
                    items.append((Wi[:, f0:f0 + TT], qk_block))
                for hb in range(HPG // 4):
                  def v_block(wb, b_wb, hb=hb):
                    flushB()
                    col = 0
                    for (rho, pos0, npos) in blocks:
                        for bb in range(npos // 128):
                            ps, b_ps = mm_tm(wb, b_wb, hT, b_hT, col)
                            o, b_o = W["ev_ring"].next()
                            S.op("act", lambda e: e.activation(out=o[:], in_=ps[:], func=AF.Copy), reads=[b_ps], writes=[b_o])
                            r0 = pos0 + bb * 128
                            S.store(VS[g][rho, r0:r0 + 128, hb * TT:(hb + 1) * TT], o[:], reads=[b_o], writes=[bQKV], track=b_o)
                            hv = HVin[g].rearrange("(h s k) (r c) -> k h s r c", s=2, k=64, c=128)
                            if r0 == 0:
                                S.store(hv[:, hb * 4:(hb + 1) * 4, 0, rho, :], o[0:64, :].rearrange("k (h c) -> k h c", c=128),
                                      reads=[b_o], writes=[bHin[g]], track=b_o)
                            if r0 + 128 == Lq:
                                S.store(hv[:, hb * 4:(hb + 1) * 4, 1, rho, :], o[64:128, :].rearrange("k (h c) -> k h c", c=128),
                                      reads=[b_o], writes=[bHin[g]], track=b_o)
                            col += 128
                  f0 = 2 * cfg.QKV + (g * HPG + hb * 4) * 128
                  items.append((Wi[:, f0:f0 + TT], v_block))
                run_blocks(items)
                flushB()
            if g > 0:
                halo_exchange_g(j, g - 1)
        for s in range(T // ST):
            for bb in range(ST // 128):
                t0 = s * ST + bb * 128
                prep_block(Xc[t0:t0 + 128, :], bXc, hT, b_hT, bb * 128)
            items = []
            for fb in range(cfg.AW // TT):
                def g_block(wb, b_wb, fb=fb, s=s):
                    for m in range(4):
                        for n in range(ST // TT):
                            ps, b_ps = mm_fm(wb, b_wb, m, hT, b_hT, n * TT, TT)
                            o, b_o = W["ev_ring"].next()
                            S.op("act", lambda e: e.activation(out=o[:], in_=ps[:], func=AF.Silu), reads=[b_ps], writes=[b_o])
                            fr = fb * TT + m * 128
                            S.store(SGT[fr:fr + 128, s * ST + n * TT:s * ST + (n + 1) * TT], o[:], reads=[b_o], writes=[bSGT], track=b_o)
                f0 = 3 * cfg.QKV + fb * TT
                items.append((Wi[:, f0:f0 + TT], g_block))
            run_blocks(items)
            if s == 0:
                halo_exchange_g(j, 2)

    def halo_exchange_g(ai, g):
        d = DIL[g]
        S.flush_stores()
        ag_chunks(HKin[g], HKout[g][ai], HPG * 2 * 128, d * 64, [bHin[g]], [bHout])
        ag_chunks(HVin[g], HVout[g][ai], HPG * 2 * 64, d * 128, [bHin[g]], [bHout])

    def attn_core(ai):
        qs_ring = Ring([sbuf([128, ST], BF16, "qs") for _ in range(2)])
        ks_ring = Ring([sbuf([128, 2 * ST], BF16, "ks") for _ in range(2)])
        vs_ring = Ring([sbuf([128, 32, 128], BF16, "vs") for _ in range(2)])
        mt_ring = Ring([sbuf([128, 3, 384], F32, "mt") for _ in range(2)])
        kh_ring = Ring([sbuf([128, 1024], BF16, "kh") for _ in range(2)])
        vh_ring = Ring([sbuf([128, 2048], BF16, "vh") for _ in range(2)])

        def khalo(dst, g, d, col, b_ks):
            kh, b_kh = kh_ring.next()
            S.idma(kh[:, 0:d * 64], HKout[g][ai][:, :], idxk[:, g, col:col + 1], reads=[bHout, b_idxk], writes=[b_kh], track=b_kh)
            S.op("act", lambda e: e.activation(out=dst, in_=kh[:, 0:d * 64].rearrange("p (r l) -> p r l", l=64), func=AF.Copy),
                 reads=[b_kh], writes=[b_ks])

        def vhalo(dst, g, d, col, b_vs):
            vh, b_vh = vh_ring.next()
            S.idma(vh[0:64, 0:d * 128], HVout[g][ai][:, :], idxv[0:64, g, col:col + 1], reads=[bHout, b_idxv], writes=[b_vh], track=b_vh)
            S.op("act", lambda e: e.activation(out=dst, in_=vh[0:64, 0:d * 128].rearrange("k (r c) -> k r c", c=128), func=AF.Copy),
                 reads=[b_vh], writes=[b_vs])

        acc, b_acc = sbuf([128, 2, ST], F32, "acc")
        ex_ring = Ring([sbuf([128, 256], F32, "ex") for _ in range(4)])
        pt_ring = Ring([sbuf([128, 256], BF16, "pT") for _ in range(4)])
        sg_ring = Ring([sbuf([128, ST], BF16, "sg") for _ in range(2)])
        yo_ring = Ring([sbuf([128, ST], BF16, "yo") for _ in range(2)])
        tmp, b_tmp = sbuf([128, ST], F32, "tmp")
        nst = T // ST
        for s_ in range(nst):
            T0 = s_ * ST
            for h in range(HPG):
                mt, b_mt = mt_ring.next()
                S.dma("sp", mt[:], mtab_in.rearrange("p (g h) c -> p h g c", g=3)[:, h, :, :], writes=[b_mt], track=b_mt)
                hc = slice(h * 128, (h + 1) * 128)
                for g, d in enumerate(DIL):
                    Lq = ST // d
                    nqb = Lq // 128
                    P0 = T0 // d
                    Wd = Lq + 128
                    qs, b_qs = qs_ring.next()
                    ks, b_ks = ks_ring.next()
                    vs, b_vs = vs_ring.next()
                    S.dma("sp", qs[:, 0:ST].rearrange("p (r l) -> p r l", r=d), QT[g][h, :, :, P0:P0 + Lq],
                          reads=[bQKV], writes=[b_qs], track=b_qs)
                    ks3 = ks[:, 0:d * Wd].rearrange("p (r w) -> p r w", r=d)
                    S.dma("sp", ks3[:, :, 64:Lq], KTs[g][h, :, :, P0:P0 + Lq - 64], reads=[bQKV], writes=[b_ks], track=b_ks)
                    S.dma("sp", ks3[:, :, Lq + 64:Lq + 128], KTs[g][h, :, :, P0 + Lq - 64:P0 + Lq], reads=[bQKV], writes=[b_ks], track=b_ks)
                    hk = HKout[g][ai].rearrange("n (r l) -> n r l", l=64)
                    if s_ > 0:
                        S.dma("sp", ks3[:, :, 0:64], KTs[g][h, :, :, P0 - 64:P0], reads=[bQKV], writes=[b_ks], track=b_ks)
                    else:
                        khalo(ks3[:, :, 0:64], g, d, h, b_ks)
                    if s_ < nst - 1:
                        S.dma("sp", ks3[:, :, Lq:Lq + 64], KTs[g][h, :, :, P0 + Lq:P0 + Lq + 64], reads=[bQKV], writes=[b_ks], track=b_ks)
                    else:
                        khalo(ks3[:, :, Lq:Lq + 64], g, d, HPG + h, b_ks)
                    vs4 = vs[:, 0:d * (nqb + 1), :].rearrange("k (r j) c -> k r j c", j=nqb + 1)
                    if nqb > 1:
                        for rho in range(d):
                            S.dma("sp", vs4[:, rho, 1:nqb, :],
                                  VS[g][rho, P0 + 64:P0 + 64 + 128 * (nqb - 1), hc].rearrange("(j k) c -> k j c", k=128),
                                  reads=[bQKV], writes=[b_vs], track=b_vs)
                    S.dma("sp", vs4[64:128, :, 0, :], VS[g][:, P0:P0 + 64, hc].rearrange("r k c -> k r c"),
                          reads=[bQKV], writes=[b_vs], track=b_vs)
                    S.dma("sp", vs4[64:128, :, nqb, :], VS[g][:, P0 + Lq - 64:P0 + Lq, hc].rearrange("r k c -> k r c"),
                          reads=[bQKV], writes=[b_vs], track=b_vs)
                    hvv = HVout[g][ai].rearrange("n (r c) -> n r c", c=128)
                    if s_ > 0:
                        S.dma("sp", vs4[0:64, :, 0, :], VS[g][:, P0 - 64:P0, hc].rearrange("r k c -> k r c"),
                              reads=[bQKV], writes=[b_vs], track=b_vs)
                    else:
                        vhalo(vs4[0:64, :, 0, :], g, d, h, b_vs)
                    if s_ < nst - 1:
                        S.dma("sp", vs4[0:64, :, nqb, :], VS[g][:, P0 + Lq:P0 + Lq + 64, hc].rearrange("r k c -> k r c"),
                              reads=[bQKV], writes=[b_vs], track=b_vs)
                    else:
                        vhalo(vs4[0:64, :, nqb, :], g, d, HPG + h, b_vs)
                    def stageA(rho, qb):
                        qcol = rho * Lq + qb * 128
                        sc, b_sc = PS.next()
                        for e in range(2):
                            kcol = rho * Wd + (qb + e) * 128
                            S.op("pe", lambda en: en.matmul(sc[:, e * 128:(e + 1) * 128], lhsT=ks[:, kcol:kcol + 128],
                                                           rhs=qs[:, qcol:qcol + 128], start=True, stop=True),
                                 reads=[b_ks, b_qs], writes=[b_sc], signal=(e == 1))
                        ex, b_ex = ex_ring.next()
                        S.op("act", lambda en: en.activation(out=ex[:], in_=sc[:, 0:256], func=AF.Exp, scale=128.0 ** -0.5),
                             reads=[b_sc], writes=[b_ex])
                        pT, b_pT = pt_ring.next()
                        for e in range(2):
                            jcol = cfg.NKB_OFF[g] + P0 // 128 + qb + e
                            swapped = (qb + e == nqb)
                            vtab = vts if swapped else vtn
                            mv = 2 if swapped else e
                            S.op("dve", lambda en: en.scalar_tensor_tensor(out=pT[:, e * 128:(e + 1) * 128], in0=ex[:, e * 128:(e + 1) * 128],
                                                                          scalar=vtab[:, jcol:jcol + 1], in1=mt[:, g, mv * 128:(mv + 1) * 128],
                                                                          op0=ALU.mult, op1=ALU.mult),
                                 reads=[b_ex, b_vt, b_vts, b_mt], writes=[b_pT])
                        return pT, b_pT

                    def stageB(rho, qb, pT, b_pT):
                        nd, b_nd = PS.next()
                        for e in range(2):
                            vb = rho * (nqb + 1) + qb + e
                            S.op("pe", lambda en: en.matmul(nd[:, 0:128], lhsT=vs[:, vb, :], rhs=pT[:, e * 128:(e + 1) * 128],
                                                           start=(e == 0), stop=(e == 1)),
                                 reads=[b_vs, b_pT], writes=[b_nd], signal=False)
                        for e in range(2):
                            S.op("pe", lambda en: en.matmul(nd[:, 128:256], lhsT=ones_bf[:], rhs=pT[:, e * 128:(e + 1) * 128],
                                                           start=(e == 0), stop=(e == 1)),
                                 reads=[b_ones, b_pT], writes=[b_nd], signal=(e == 1))
                        dst = acc[:, :, :].rearrange("p a (i r) -> p a r i", r=d)[:, :, rho, qb * 128:(qb + 1) * 128]
                        src = nd[:, 0:256].rearrange("p (a b) -> p a b", a=2)
                        if g == 0:
                            S.op("act", lambda en: en.activation(out=dst, in_=src, func=AF.Copy), reads=[b_nd], writes=[b_acc])
                        else:
                            S.op("dve", lambda en: en.tensor_tensor(out=dst, in0=dst, in1=src, op=ALU.add),
                                 reads=[b_nd, b_acc], writes=[b_acc])

                    blks = [(rho, qb) for rho in range(d) for qb in range(nqb)]
                    cur = stageA(*blks[0])
                    for bi, (rho, qb) in enumerate(blks):
                        nxtp = stageA(*blks[bi + 1]) if bi + 1 < len(blks) else None
                        stageB(rho, qb, *cur)
                        cur = nxtp
                sg, b_sg = sg_ring.next()
                S.dma("sp", sg[:], SGT[h * 128:(h + 1) * 128, T0:T0 + ST], reads=[bSGT], writes=[b_sg], track=b_sg)
                S.op("dve", lambda en: en.tensor_scalar(out=tmp[:], in0=acc[:, 1, :], scalar1=1e-30, scalar2=None, op0=ALU.add),
                     reads=[b_acc], writes=[b_tmp])
                S.op("dve", lambda en: en.reciprocal(out=tmp[:], in_=tmp[:]), reads=[b_tmp], writes=[b_tmp])
                S.op("dve", lambda en: en.tensor_tensor(out=tmp[:], in0=tmp[:], in1=acc[:, 0, :], op=ALU.mult),
                     reads=[b_tmp, b_acc], writes=[b_tmp])
                yo, b_yo = yo_ring.next()
                S.op("dve", lambda en: en.tensor_tensor(out=yo[:], in0=tmp[:], in1=sg[:], op=ALU.mult),
                     reads=[b_tmp, b_sg], writes=[b_yo])
                S.store(YT[h * 128:(h + 1) * 128, T0:T0 + ST], yo[:], reads=[b_yo], writes=[bYT], track=b_yo)

    rr = [0]

    def load_w_resident(dst, b_dst, W_rows_ap, kcn, ncols):
        for kc0 in range(0, kcn, KC // 2 if KC >= 2 else 1):
            kn = min(KC // 2 if KC >= 2 else 1, kcn - kc0)
            for c0 in range(0, ncols, TT):
                wst, b_wst = W["wst_ring"].next()
                S.dma("sp", wst[:, 0:kn, :], W_rows_ap[kc0 * 128:(kc0 + kn) * 128, c0:c0 + TT].rearrange("(kc p) f -> p kc f", p=128),
                      writes=[b_wst], track=b_wst)
                rr[0] += 1
                if rr[0] % 2 == 0:
                    S.op("act", lambda e: e.activation(out=dst[:, kc0:kc0 + kn, c0:c0 + TT], in_=wst[:, 0:kn, :], func=AF.Copy),
                         reads=[b_wst], writes=[b_dst])
                else:
                    S.op("dve", lambda e: e.tensor_copy(out=dst[:, kc0:kc0 + kn, c0:c0 + TT], in_=wst[:, 0:kn, :]),
                         reads=[b_wst], writes=[b_dst])

    def wout_pass(Xsrc, bXsrc, Xdst, bXdst, W_rows_ap, yT_rows0, wres, b_wres):
        load_w_resident(wres, b_wres, W_rows_ap, KC, D)
        yt_ring = Ring([sbuf([128, KC, TT], BF16, "yt") for _ in range(2)])
        for tb in range(T // 128):
            t0 = tb * 128
            if tb % 4 == 0:
                ytt, b_yt = yt_ring.next()
                S.dma("sp", ytt[:], YT[yT_rows0:yT_rows0 + D, t0:t0 + TT].rearrange("(kc p) t -> p kc t", p=128),
                      reads=[bYT], writes=[b_yt], track=b_yt)
            yt = ytt[:, :, (tb % 4) * 128:(tb % 4 + 1) * 128]
            xt, b_xt = W["xt_ring"].next()
            S.dma("sp", xt[:], Xsrc[t0:t0 + 128, :], reads=[bXsrc], writes=[b_xt], track=b_xt)
            for fb in range(D // TT):
                ps, b_ps = PS.next()
                for kc in range(KC):
                    S.op("pe", lambda e: e.matmul(ps[:], lhsT=yt[:, kc, :], rhs=wres[:, kc, fb * TT:(fb + 1) * TT],
                                                  start=(kc == 0), stop=(kc == KC - 1)),
                         reads=[b_yt, b_wres], writes=[b_ps], signal=(kc == KC - 1))
                S.op("dve", lambda e: e.tensor_tensor(out=xt[:, fb * TT:(fb + 1) * TT], in0=ps[:], in1=xt[:, fb * TT:(fb + 1) * TT], op=ALU.add),
                     reads=[b_ps, b_xt], writes=[b_xt])
            S.store(Xdst[t0:t0 + 128, :], xt[:], reads=[b_xt], writes=[bXdst], track=b_xt)

    def ple_pass(Xsrc, bXsrc, Xdst, bXdst, li, wres, b_wres, wp, b_wp):
        set_gain(cfg.DEPTH + li)
        load_w_resident(wres, b_wres, ple_gate[li], KC, D)
        PK = cfg.PLE // 128
        load_w_resident(wp, b_wp, ple_proj[li], PK, D)
        h2_ring = Ring([sbuf([128, KC, 128], BF16, "h2") for _ in range(2)])
        pin_ring = Ring([sbuf([128, cfg.PLE], F32, "pin") for _ in range(2)])
        pb_ring = Ring([sbuf([128, cfg.PLE], BF16, "pb") for _ in range(2)])
        pT_ring = Ring([sbuf([128, PK, 128], BF16, "ppT") for _ in range(2)])
        sig_ring = Ring([sbuf([128, TT], F32, "sig") for _ in range(2)])
        xo_ring = Ring([sbuf([128, D], F32, "xo") for _ in range(2)])
        for tb in range(T // 128):
            t0 = tb * 128
            h2, b_h2 = h2_ring.next()
            prep_block(Xsrc[t0:t0 + 128, :], bXsrc, h2, b_h2, 0)
            xo, b_xo = xo_ring.next()
            S.dma("sp", xo[:], Xsrc[t0:t0 + 128, :], reads=[bXsrc], writes=[b_xo], track=b_xo)
            pin, b_pin = pin_ring.next()
            S.dma("sp", pin[:], p_in[li, t0:t0 + 128, :], writes=[b_pin], track=b_pin)
            pb, b_pb = pb_ring.next()
            S.op("act", lambda e: e.activation(out=pb[:], in_=pin[:], func=AF.Copy), reads=[b_pin], writes=[b_pb])
            ptp, b_ptp = PTR.next()
            for k in range(PK):
                S.op("pe", lambda e: e.transpose(out=ptp[:, k * 128:(k + 1) * 128], in_=pb[:, k * 128:(k + 1) * 128], identity=ident[:]),
                     reads=[b_pb, b_ident], writes=[b_ptp], signal=(k == PK - 1))
            ppT, b_ppT = pT_ring.next()
            S.op("act", lambda e: e.activation(out=ppT[:], in_=ptp[:, 0:PK * 128].rearrange("p (k c) -> p k c", k=PK), func=AF.Copy),
                 reads=[b_ptp], writes=[b_ppT])
            for fb in range(D // TT):
                ps, b_ps = PS.next()
                for kc in range(KC):
                    S.op("pe", lambda e: e.matmul(ps[:], lhsT=h2[:, kc, :], rhs=wres[:, kc, fb * TT:(fb + 1) * TT],
                                                  start=(kc == 0), stop=(kc == KC - 1)),
                         reads=[b_h2, b_wres], writes=[b_ps], signal=(kc == KC - 1))
                sig, b_sig = sig_ring.next()
                S.op("act", lambda e: e.activation(out=sig[:], in_=ps[:], func=AF.Sigmoid), reads=[b_ps], writes=[b_sig])
                ps2, b_ps2 = PS.next()
                for k in range(PK):
                    S.op("pe", lambda e: e.matmul(ps2[:], lhsT=ppT[:, k, :], rhs=wp[:, k, fb * TT:(fb + 1) * TT],
                                                  start=(k == 0), stop=(k == PK - 1)),
                         reads=[b_ppT, b_wp], writes=[b_ps2], signal=(k == PK - 1))
                S.op("dve", lambda e: e.tensor_tensor(out=sig[:], in0=ps2[:], in1=sig[:], op=ALU.mult),
                     reads=[b_ps2, b_sig], writes=[b_sig])
                S.op("dve", lambda e: e.tensor_tensor(out=xo[:, fb * TT:(fb + 1) * TT], in0=xo[:, fb * TT:(fb + 1) * TT], in1=sig[:], op=ALU.add),
                     reads=[b_sig, b_xo], writes=[b_xo])
            S.store(Xdst[t0:t0 + 128, :], xo[:], reads=[b_xo], writes=[bXdst], track=b_xo)

    def fnet_proj(Xc, bXc, li, j):
        Wi = fnet_w_in[j]
        hT, b_hT = W["hT"]
        set_gain(li)
        for s in range(T // ST):
            for bb in range(ST // 128):
                t0 = s * ST + bb * 128
                prep_block(Xc[t0:t0 + 128, :], bXc, hT, b_hT, bb * 128)
            items = []
            for fb in range(cfg.FW // TT):
                def u_block(wb, b_wb, fb=fb, s=s):
                    for bb in range(ST // 128):
                        ps, b_ps = mm_tm(wb, b_wb, hT, b_hT, bb * 128)
                        o, b_o = W["ev_ring"].next()
                        S.op("act", lambda e: e.activation(out=o[:], in_=ps[:], func=AF.Copy), reads=[b_ps], writes=[b_o])
                        t0 = s * ST + bb * 128
                        uv = UUloc.rearrange("(c t) e -> t c e", t=T)
                        S.store(uv[t0:t0 + 128, fb * 8:(fb + 1) * 8, :], o[:].rearrange("t (c e) -> t c e", e=64),
                                reads=[b_o], writes=[bUU], track=b_o)
                items.append((Wi[:, fb * TT:(fb + 1) * TT], u_block))
            run_blocks(items)
        S.flush_stores()
        ag_chunks(UUloc.rearrange("(n b) e -> n (b e)", b=128), UUall[j].rearrange("(n b) e -> n (b e)", b=128),
                  PL * (T // 128), 128 * 64, [bUU], [bUUall])
        for s in range(T // ST):
            for bb in range(ST // 128):
                t0 = s * ST + bb * 128
                prep_block(Xc[t0:t0 + 128, :], bXc, hT, b_hT, bb * 128)
            items = []
            for fb in range(cfg.FW // TT):
                def fg_block(wb, b_wb, fb=fb, s=s):
                    for m in range(4):
                        for n in range(ST // TT):
                            ps, b_ps = mm_fm(wb, b_wb, m, hT, b_hT, n * TT, TT)
                            o, b_o = W["ev_ring"].next()
                            S.op("act", lambda e: e.activation(out=o[:], in_=ps[:], func=AF.Silu), reads=[b_ps], writes=[b_o])
                            fr = fb * TT + m * 128
                            S.store(SGT[fr:fr + 128, s * ST + n * TT:s * ST + (n + 1) * TT], o[:], reads=[b_o], writes=[bSGT], track=b_o)
                items.append((Wi[:, cfg.FW + fb * TT:cfg.FW + (fb + 1) * TT], fg_block))
            run_blocks(items)

    def fnet_dft(fi, dA, b_dA, dct, b_dct, dst_, b_dst_, dB, b_dB, dC3, b_dC3):
        AR = T // 128
        NA = 4 * AR
        CB = 64
        xx, b_xx = sbuf([128, 128, CB], BF16, "dx")
        y2, b_y2 = sbuf([128, CB, 2, 2, 128], BF16, "dy2")
        zs, b_zs = sbuf([64, 2, TS], BF16, "dzs")
        t_ring = Ring([sbuf([128, 2, 128], F32, "dt") for _ in range(4)])
        uall = UUall[fi].rearrange("(n b) e -> n (b e)", b=128)
        for i in range(NCBO):
            S.idma(xx[:].rearrange("a b c -> a (b c)"), uall, idxu[:, i:i + 1], reads=[bUUall, b_idxu], writes=[b_xx], track=b_xx)
            for c in range(CB):
                ps, b_ps = PS.next()
                S.op("pe", lambda e: e.matmul(ps[:], lhsT=xx[:, :, c], rhs=dA[:], start=True, stop=True),
                     reads=[b_xx, b_dA], writes=[b_ps])
                pv = ps[:].rearrange("p (q r j) -> p q r j", q=2, r=2)
                re1, im1 = pv[:, :, 0, :], pv[:, :, 1, :]
                ta, b_ta = t_ring.next()
                tb_, b_tb = t_ring.next()
                S.op("dve", lambda e: e.tensor_tensor(out=ta[:], in0=re1, in1=dct[:], op=ALU.mult), reads=[b_ps, b_dct], writes=[b_ta])
                S.op("dve", lambda e: e.tensor_tensor(out=tb_[:], in0=im1, in1=dst_[:], op=ALU.mult), reads=[b_ps, b_dst_], writes=[b_tb])
                S.op("dve", lambda e: e.tensor_tensor(out=y2[:, c, :, 0, :], in0=ta[:], in1=tb_[:], op=ALU.add),
                     reads=[b_ta, b_tb], writes=[b_y2])
                tc_, b_tc = t_ring.next()
                td, b_td = t_ring.next()
                S.op("dve", lambda e: e.tensor_tensor(out=tc_[:], in0=im1, in1=dct[:], op=ALU.mult), reads=[b_ps, b_dct], writes=[b_tc])
                S.op("dve", lambda e: e.tensor_tensor(out=td[:], in0=re1, in1=dst_[:], op=ALU.mult), reads=[b_ps, b_dst_], writes=[b_td])
                S.op("dve", lambda e: e.tensor_tensor(out=y2[:, c, :, 1, :], in0=tc_[:], in1=td[:], op=ALU.subtract),
                     reads=[b_tc, b_td], writes=[b_y2])
            for jj in range(128):
                ps, b_ps = PS.next()
                k = 0
                for sq in range(2):
                    for r in range(2):
                        S.op("pe", lambda e: e.matmul(ps[0:CB, 0:256], lhsT=y2[:, :, sq, r, jj], rhs=dB[:, sq, r, :],
                                                      start=(k == 0), stop=(k == 3)),
                             reads=[b_y2, b_dB], writes=[b_ps], signal=(k == 3))
                        k += 1
                src = ps[0:CB, 0:256].rearrange("p (r d) -> p r d", r=2)[:, :, 0:NA]
                dstz = zs[:, :, :].rearrange("c r (d j) -> c r j d", j=128)[:, :, jj, 0:NA]
                if jj % 2 == 0:
                    S.op("act", lambda e: e.activation(out=dstz, in_=src, func=AF.Copy), reads=[b_ps], writes=[b_zs])
                else:
                    S.op("dve", lambda e: e.tensor_copy(out=dstz, in_=src), reads=[b_ps], writes=[b_zs])
            S.store(ZT[i * CB:(i + 1) * CB, :, :], zs[:], reads=[b_zs], writes=[bZT], track=b_zs, lag=0)
        zt_ring = Ring([sbuf([128, 2, 2, TT], BF16, "zt") for _ in range(2)])
        for gi in range(CHL // 256):
            for n in range(TS // TT):
                zt, b_zt = zt_ring.next()
                for kch in range(2):
                    r0 = gi * 256 + kch * 128
                    S.dma("sp", zt[:, kch, :, :], ZT[r0:r0 + 128, :, n * TT:(n + 1) * TT], reads=[bZT], writes=[b_zt], track=b_zt)
                for cob in range(2):
                    ps, b_ps = PS.next()
                    k = 0
                    for kch in range(2):
                        for r in range(2):
                            S.op("pe", lambda e: e.matmul(ps[:], lhsT=dC3[:, r, kch, cob * 128:(cob + 1) * 128], rhs=zt[:, kch, r, :],
                                                          start=(k == 0), stop=(k == 3)),
                                 reads=[b_dC3, b_zt], writes=[b_ps], signal=(k == 3))
                            k += 1
                    o, b_o = W["ev_ring"].next()
                    S.op("act", lambda e: e.activation(out=o[:], in_=ps[:], func=AF.Copy), reads=[b_ps], writes=[b_o])
                    q = (n * TT) // T
                    tl = (n * TT) % T
                    row = q * CHL + gi * 256 + cob * 128
                    S.store(MXloc[row:row + 128, tl:tl + TT], o[:], reads=[b_o], writes=[bMX], track=b_o)

    def fnet_gate(fi):
        mx_ring = Ring([sbuf([128, T], BF16, "mx") for _ in range(2)])
        sg_ring = Ring([sbuf([128, T], BF16, "sgf") for _ in range(2)])
        yo_ring = Ring([sbuf([128, T], BF16, "yof") for _ in range(2)])
        for gb in range(cfg.FW // 128):
            mx, b_mx = mx_ring.next()
            S.idma(mx[:], MXall[fi][:, :], idxm[:, gb:gb + 1], reads=[bMXall, b_idxm], writes=[b_mx], track=b_mx)
            sg, b_sg = sg_ring.next()
            S.dma("sp", sg[:], SGT[gb * 128:(gb + 1) * 128, :], reads=[bSGT], writes=[b_sg], track=b_sg)
            yo, b_yo = yo_ring.next()
            S.op("dve", lambda e: e.tensor_tensor(out=yo[:], in0=mx[:], in1=sg[:], op=ALU.mult), reads=[b_mx, b_sg], writes=[b_yo])
            S.dma("sp", YT[gb * 128:(gb + 1) * 128, :], yo[:], reads=[b_yo], writes=[bYT], track=b_yo)

    cur, bcur = x_in, Buf("xin", True)
    pp = [(XA, bXA), (XB, bXB)]
    ppi = [0]

    def nxt(final=False):
        if final:
            return y_out, bY
        r = pp[ppi[0] % 2]
        ppi[0] += 1
        return r

    def mix_phase(src, bsrc, dst, bdst, W_rows, yrow0):
        with Phase():
            alloc_prep()
            wres, b_wres = sbuf([128, KC, D], BF16, "wres")
            wout_pass(src, bsrc, dst, bdst, W_rows, yrow0, wres, b_wres)

    try:
        for li in range(cfg.DEPTH):
            j = li // 2
            if li % 2 == 0:
                with Phase():
                    alloc_proj()
                    attn_proj(cur, bcur, li, j)
                with Phase():
                    attn_core(j)
                d1, bd1 = nxt()
                mix_phase(cur, bcur, d1, bd1, attn_w_out[j], 0)
            else:
                with Phase():
                    alloc_proj()
                    fnet_proj(cur, bcur, li, j)
                with Phase():
                    alloc_proj_ev()
                    dA, b_dA = sbuf([128, 512], BF16, "dA")
                    S.dma("sp", dA[:], dftA_in[:, :], writes=[b_dA], track=b_dA)
                    dct, b_dct = sbuf([128, 2, 128], F32, "dct")
                    S.dma("sp", dct[:], dftct_in[:, :, :], writes=[b_dct], track=b_dct)
                    dst_, b_dst_ = sbuf([128, 2, 128], F32, "dst")
                    S.dma("sp", dst_[:], dftst_in[:, :, :], writes=[b_dst_], track=b_dst_)
                    dB, b_dB = sbuf([128, 2, 2, 256], BF16, "dB")
                    S.dma("sp", dB[:], dftB_in[:, :, :, :], writes=[b_dB], track=b_dB)
                    dC3, b_dC3 = sbuf([128, 2, 2, 256], BF16, "dC3")
                    S.dma("sp", dC3[:], dftC3_in[:, :, :, :], writes=[b_dC3], track=b_dC3)
                    fnet_dft(j, dA, b_dA, dct, b_dct, dst_, b_dst_, dB, b_dB, dC3, b_dC3)
                with Phase():
                    ag_chunks(MXloc, MXall[j], 4 * CHL, T, [bMX], [bMXall])
                    fnet_gate(j)
                d0, bd0 = nxt()
                mix_phase(cur, bcur, d0, bd0, fnet_w_out[j][0:D, :], 0)
                d1, bd1 = nxt()
                mix_phase(d0, bd0, d1, bd1, fnet_w_out[j][D:2 * D, :], D)
            d2, bd2 = nxt(final=(li == cfg.DEPTH - 1))
            with Phase():
                alloc_prep()
                wres, b_wres = sbuf([128, KC, D], BF16, "wres")
                wp, b_wp = sbuf([128, max(cfg.PLE // 128, 1), D], BF16, "wp")
                ple_pass(d1, bd1, d2, bd2, li, wres, b_wres, wp, b_wp)
            cur, bcur = d2, bd2
    except _StopBuild:
        pass
    S.barrier()
    nc._sched_ninst = S.ninst
    return nc


_W_NAMES = ["norm_in", "attn_w_in", "attn_q_norm", "attn_k_norm", "attn_w_out", "fnet_w_in", "fnet_w_out",
            "ple_proj", "ple_gate", "ple_norm"]


def run_groups(cfg, groups, weights):
    nc = build(cfg)
    T = cfg.T
    in_maps = []
    wts = {k: np.ascontiguousarray(weights[k], dtype=np.float32) for k in _W_NAMES}
    for c in range(8):
        g, r = c // 4, c % 4
        x, p, nseq = groups[g]
        m = {"x": np.ascontiguousarray(x[r * T:(r + 1) * T]), "p": np.ascontiguousarray(p[:, r * T:(r + 1) * T])}
        m.update(wts)
        m.update(host_tables(cfg, r, nseq))
        in_maps.append(m)
    res = run_bass_kernel_spmd(nc, in_maps, core_ids=list(range(8)))
    return [np.concatenate([np.asarray(res.results[g * 4 + r]["y"]) for r in range(4)], axis=0) for g in range(2)]


def kernel(x_prompt, x_sample, p_prompt, p_sample, **weights):
    cfg = Cfg(T=4096)
    x_prompt = np.asarray(x_prompt, np.float32)
    x_sample = np.asarray(x_sample, np.float32)
    p_prompt = np.asarray(p_prompt, np.float32)
    p_sample = np.asarray(p_sample, np.float32)
    B, SQ, D = x_prompt.shape
    gA = (x_prompt.reshape(B * SQ, D), p_prompt.reshape(p_prompt.shape[0], B * SQ, -1), B)
    gB = (x_sample[0], p_sample[:, 0], 1)
    ys = run_groups(cfg, [gA, gB], weights)
    y_prompt = ys[0].reshape(B, SQ, D).astype(np.float32)
    y_sample = ys[1][None].astype(np.float32)
    return (y_prompt, y_sample)
```

```python
from contextlib import ExitStack
import numpy as np
import ml_dtypes
import concourse.bass as bass
import concourse.mybir as mybir
from concourse.bass_utils import run_bass_kernel_spmd

F32 = mybir.dt.float32
BF16 = mybir.dt.bfloat16
I32 = mybir.dt.int32
AF = mybir.ActivationFunctionType
ALU = mybir.AluOpType
EPS = 1e-6
DIL = (1, 4, 16)
ST = 2048
TT = 512


class Buf:
    __slots__ = ("name", "w", "r", "sem", "cnt", "multi")

    def __init__(self, name, multi=False):
        self.name = name
        self.w = [] if multi else None
        self.r = []
        self.sem = None
        self.cnt = 0
        self.multi = multi


def _prune(deps):
    best = {}
    for d in deps:
        k = id(d[0])
        if k not in best or best[k][1] < d[1]:
            best[k] = d
    return list(best.values())


class Sched:
    def __init__(self, nc):
        self.nc = nc
        self.E = {}
        for name, e in [("pe", nc.tensor), ("act", nc.scalar), ("dve", nc.vector),
                        ("pool", nc.gpsimd), ("sp", nc.sync)]:
            self.E[name] = dict(eng=e, sem=nc.alloc_semaphore("e_" + name), cnt=0, waited={})
        self.ninst = 0
        self.pending = []
        self.free_sems = []
        self.all_dma = {}

    def acquire(self, b):
        if self.free_sems:
            b.sem, b.cnt = self.free_sems.pop()
        else:
            b.sem, b.cnt = self.nc.alloc_semaphore("d_" + b.name), 0
        self.all_dma[id(b.sem)] = [b.sem, b.cnt]

    def release(self, bufs):
        for b in bufs:
            if b.sem is not None:
                self.free_sems.append((b.sem, b.cnt))
                b.sem = None

    def barrier(self):
        self.flush_stores()
        for en, E in self.E.items():
            deps = [(F["sem"], F["cnt"], fn) for fn, F in self.E.items() if fn != en and F["cnt"] > 0]
            deps += [(sem, cnt, "dma") for sem, cnt in self.all_dma.values() if cnt > 0]
            self._wait(en, deps)

    def _wait(self, en, deps):
        E = self.E[en]
        need = {}
        for d in deps:
            if d is None:
                continue
            sem, val, owner = d
            if owner == en:
                continue
            k = id(sem)
            if k not in need or need[k][1] < val:
                need[k] = (sem, val)
        for k, (sem, val) in need.items():
            if E["waited"].get(k, 0) >= val:
                continue
            E["eng"].wait_ge(sem, val)
            E["waited"][k] = val
            self.ninst += 1

    @staticmethod
    def _deps(reads, writes):
        deps = []
        for b in reads:
            if b.multi:
                deps.extend(b.w)
            else:
                deps.append(b.w)
        for b in writes:
            if b.multi:
                deps.extend(b.w)
            else:
                deps.append(b.w)
            deps.extend(b.r)
        return deps

    @staticmethod
    def _update(dep, reads, writes):
        for b in writes:
            if b.multi:
                b.w.append(dep)
                if len(b.w) > 12:
                    b.w = _prune(b.w)
            else:
                b.w = dep
            b.r = []
        for b in reads:
            b.r.append(dep)
            if len(b.r) > 12:
                b.r = _prune(b.r)

    def op(self, en, fn, reads=(), writes=(), signal=True):
        E = self.E[en]
        self._wait(en, self._deps(reads, writes))
        ins = fn(E["eng"])
        self.ninst += 1
        dep = (E["sem"], E["cnt"] + 1, en)
        if signal:
            ins.then_inc(E["sem"], 1)
            E["cnt"] += 1
        self._update(dep, reads, writes)
        return ins

    def dma(self, qn, out_ap, in_ap, reads=(), writes=(), track=None, strict=False):
        E = self.E[qn]
        self._wait(qn, self._deps(reads, writes))
        tb = track
        if tb.sem is None:
            self.acquire(tb)
        ins = E["eng"].dma_start(out=out_ap, in_=in_ap)
        ins.then_inc(tb.sem, 16)
        tb.cnt += 16
        self.all_dma[id(tb.sem)][1] = tb.cnt
        self.ninst += 1
        dep = (tb.sem, tb.cnt, "dma")
        self._update(dep, reads, writes)
        return ins

    def collective(self, in_ap, out_ap, groups, reads=(), writes=()):
        E = self.E["pool"]
        self._wait("pool", self._deps(reads, writes))
        if getattr(self, "cc_sem", None) is None:
            self.cc_sem = self.nc.alloc_semaphore("cc_sem")
            self.cc_cnt = 0
        ins = E["eng"].collective_compute("AllGather", ALU.bypass, replica_groups=groups, ins=[in_ap], outs=[out_ap])
        ins.then_inc(self.cc_sem, 1)
        self.cc_cnt += 1
        self.all_dma[id(self.cc_sem)] = [self.cc_sem, self.cc_cnt]
        self.ninst += 1
        self._update((self.cc_sem, self.cc_cnt, "cc"), reads, writes)

    def idma(self, out_ap, in_ap, idx_ap, reads=(), writes=(), track=None):
        E = self.E["pool"]
        self._wait("pool", self._deps(reads, writes))
        tb = track
        if tb.sem is None:
            self.acquire(tb)
        ins = E["eng"].indirect_dma_start(out=out_ap, out_offset=None, in_=in_ap,
                                          in_offset=bass.IndirectOffsetOnAxis(ap=idx_ap, axis=0))
        ins.then_inc(tb.sem, 16)
        tb.cnt += 16
        self.all_dma[id(tb.sem)][1] = tb.cnt
        self.ninst += 1
        self._update((tb.sem, tb.cnt, "dma"), reads, writes)

    def store(self, out_ap, in_ap, reads=(), writes=(), track=None, lag=1):
        self.pending.append((out_ap, in_ap, reads, writes, track))
        while len(self.pending) > lag:
            self._emit_store()

    def _emit_store(self):
        out_ap, in_ap, reads, writes, track = self.pending.pop(0)
        self.dma("sp", out_ap, in_ap, reads=reads, writes=writes, track=track, strict=True)

    def flush_stores(self):
        while self.pending:
            self._emit_store()

    def wait_all(self, en, bufs):
        deps = []
        for b in bufs:
            deps.extend(b.w if b.multi else [b.w])
            deps.extend(b.r)
        self._wait(en, deps)


class _StopBuild(Exception):
    pass


class Ring:
    def __init__(self, items):
        self.items = items
        self.i = 0

    def next(self):
        it = self.items[self.i % len(self.items)]
        self.i += 1
        return it


class Cfg:
    def __init__(self, T=16384, D=2048, HPG=16, FG=16, PLE=256, DEPTH=4):
        self.T, self.D, self.HPG, self.FG, self.PLE, self.DEPTH = T, D, HPG, FG, PLE, DEPTH
        self.KC = D // 128
        self.NH = 3 * HPG
        self.QKV = self.NH * 128
        self.AW = HPG * 128
        self.AIN = 3 * self.QKV + self.AW
        self.FW = 2 * D
        self.FGD = self.FW // FG
        assert self.FGD == 256 and T % ST == 0
        self.NATT = (DEPTH + 1) // 2
        self.NFN = max(DEPTH // 2, 1)
        self.R = 4
        self.TS = 4 * T
        self.PL = self.FW // 64
        self.NCBO = self.PL // 4
        self.CHL = self.FW // 4
        assert self.CHL % 256 == 0
        self.NKB = [T // d // 128 + 1 for d in DIL]
        self.NKB_OFF = [0, self.NKB[0], self.NKB[0] + self.NKB[1]]
        self.NKBT = sum(self.NKB)


CH_EL = 512 * 1024


def _rpc(nr, rl):
    return max(1, min(nr, CH_EL // rl))


def _grow(r_loc, rank, nr, rl):
    rpc = _rpc(nr, rl)
    return ((r_loc // rpc) * 4 + rank) * rpc + (r_loc % rpc)


def host_tables(cfg, r, nseq):
    T, TS = cfg.T, cfg.TS
    L = TS // nseq
    bf = ml_dtypes.bfloat16
    tabs = {}
    seq_of_core = (r * T) // L
    vtn = np.zeros((128, cfg.NKBT), np.float32)
    kk = np.arange(128)[:, None]
    for g, d in enumerate(DIL):
        Lq = T // d
        j = np.arange(cfg.NKB[g])[None, :]
        X = r * Lq + 128 * j - 64 + kk
        vtn[:, cfg.NKB_OFF[g]:cfg.NKB_OFF[g] + cfg.NKB[g]] = ((X >= seq_of_core * (L // d)) & (X < (seq_of_core + 1) * (L // d))).astype(np.float32)
    tabs["vtn"] = vtn
    tabs["vts"] = np.ascontiguousarray(np.roll(vtn, 64, axis=0))
    slopes = np.exp2(-8.0 * np.arange(1, cfg.NH + 1, dtype=np.float64) / cfg.NH)
    m = np.zeros((128, cfg.NH, 3, 128), np.float64)
    qq = np.arange(128)[None, :]
    for g, d in enumerate(DIL):
        for h in range(cfg.HPG):
            for e in range(2):
                rel = 128 * e - 64 + kk - qq
                m[:, g * cfg.HPG + h, e, :] = np.where(np.abs(rel) <= 64,
                                                       np.exp(-slopes[g * cfg.HPG + h] * d * np.abs(rel)), 0.0)
            m[:, g * cfg.HPG + h, 2, :] = np.roll(m[:, g * cfg.HPG + h, 1, :], 64, axis=0)
    tabs["mtab"] = m.reshape(128, cfg.NH, 384).astype(np.float32)
    H = cfg.HPG
    p = np.arange(128)
    idxk = np.zeros((128, 3, 2 * H), np.int32)
    idxv = np.zeros((128, 3, 2 * H), np.int32)
    for g, d in enumerate(DIL):
        for side in range(2):
            nb = min(max(r - 1 if side == 0 else r + 1, 0), 3)
            src = 1 - side
            for h in range(H):
                idxk[:, g, side * H + h] = _grow((h * 2 + src) * 128 + p, nb, H * 2 * 128, d * 64)
                idxv[:, g, side * H + h] = _grow((h * 2 + src) * 64 + (p % 64), nb, H * 2 * 64, d * 128)
    tabs["idxk"] = idxk
    tabs["idxv"] = idxv
    AR = T // 128
    a = np.arange(128)
    rk = np.minimum(a // AR, 3)
    idxu = np.zeros((128, cfg.NCBO), np.int32)
    for i in range(cfg.NCBO):
        cb = r * cfg.NCBO + i
        idxu[:, i] = np.where(a < 4 * AR, _grow(cb * AR + (a % AR), rk, cfg.PL * AR, 128 * 64), 0)
    tabs["idxu"] = idxu
    ngb = cfg.FW // 128
    idxm = np.zeros((128, ngb), np.int32)
    per = cfg.CHL // 128
    for gb in range(ngb):
        idxm[:, gb] = _grow(r * cfg.CHL + (gb % per) * 128 + p, gb // per, 4 * cfg.CHL, T)
    tabs["idxm"] = idxm
    N = L
    N1 = N // 128
    aa = np.arange(128)[:, None]
    jj = np.arange(128)[None, :]
    c = jj % N1
    e = jj // N1
    dA = np.zeros((128, 2, 2, 128), np.float64)
    dB = np.zeros((128, 2, 2, 256), np.float64)
    b = np.arange(128)[:, None]
    dd = np.arange(128)[None, :]
    for sq in range(nseq):
        a0 = sq * N1
        ang = 2 * np.pi * ((aa - a0) * c) / N1
        ina = (aa >= a0) & (aa < a0 + N1)
        dA[:, sq, 0, :] = np.where(ina, np.cos(ang), 0.0)
        dA[:, sq, 1, :] = np.where(ina, -np.sin(ang), 0.0)
        angb = 2 * np.pi * (b * (dd - a0)) / N1
        ind = (dd >= a0) & (dd < a0 + N1)
        C2 = np.where(ind, np.cos(angb), 0.0)
        S2 = np.where(ind, np.sin(angb), 0.0)
        dB[:, sq, 0, :] = np.concatenate([C2, -S2], axis=1)
        dB[:, sq, 1, :] = np.concatenate([S2, C2], axis=1)
    tabs["dftA"] = dA.reshape(128, 512).astype(bf)
    tabs["dftB"] = dB.astype(bf)
    angt = 2 * np.pi * (b * c / N + b * e / 128.0)
    ct = np.cos(angt)
    st = np.sin(angt)
    tabs["dftct"] = np.stack([ct, ct], axis=1).astype(np.float32)
    tabs["dftst"] = np.stack([st, st], axis=1).astype(np.float32)
    ci = np.arange(256)[:, None]
    co = np.arange(256)[None, :]
    ang3 = 2 * np.pi * (ci * co) / 256.0
    sc = 1.0 / np.sqrt(float(N) * 256.0)
    C3 = (np.cos(ang3) * sc).reshape(2, 128, 256).transpose(1, 0, 2)
    S3 = (np.sin(ang3) * sc).reshape(2, 128, 256).transpose(1, 0, 2)
    tabs["dftC3"] = np.ascontiguousarray(np.stack([C3, S3], axis=1)).astype(bf)
    tabs["ident"] = np.eye(128, dtype=np.float32).astype(bf)
    return tabs


def build(cfg):
    T, D, KC, HPG, NH = cfg.T, cfg.D, cfg.KC, cfg.HPG, cfg.NH
    nc = bass.Bass("TRN2", target_bir_lowering=False, num_devices=8)
    S = Sched(nc)
    GROUPS = [[0, 1, 2, 3], [4, 5, 6, 7]]
    TS, PL, NCBO, CHL = cfg.TS, cfg.PL, cfg.NCBO, cfg.CHL

    def din(name, shape, dt=F32):
        return nc.dram_tensor(name, list(shape), dt, kind="ExternalInput").ap()

    x_in = din("x", [T, D])
    p_in = din("p", [cfg.DEPTH, T, cfg.PLE])
    norm_in = din("norm_in", [cfg.DEPTH, D])
    attn_w_in = din("attn_w_in", [cfg.NATT, D, cfg.AIN])
    qn_in = din("attn_q_norm", [cfg.NATT, 128])
    kn_in = din("attn_k_norm", [cfg.NATT, 128])
    attn_w_out = din("attn_w_out", [cfg.NATT, cfg.AW, D])
    fnet_w_in = din("fnet_w_in", [cfg.NFN, D, 2 * cfg.FW])
    fnet_w_out = din("fnet_w_out", [cfg.NFN, cfg.FW, D])
    ple_proj = din("ple_proj", [cfg.DEPTH, cfg.PLE, D])
    ple_gate = din("ple_gate", [cfg.DEPTH, D, D])
    ple_norm = din("ple_norm", [cfg.DEPTH, D])
    vtn_in = din("vtn", [128, cfg.NKBT])
    vts_in = din("vts", [128, cfg.NKBT])
    mtab_in = din("mtab", [128, NH, 384])
    idxk_in = din("idxk", [128, 3, 2 * HPG], I32)
    idxv_in = din("idxv", [128, 3, 2 * HPG], I32)
    idxu_in = din("idxu", [128, NCBO], I32)
    idxm_in = din("idxm", [128, cfg.FW // 128], I32)
    dftA_in = din("dftA", [128, 512], BF16)
    dftct_in = din("dftct", [128, 2, 128])
    dftst_in = din("dftst", [128, 2, 128])
    dftB_in = din("dftB", [128, 2, 2, 256], BF16)
    dftC3_in = din("dftC3", [128, 2, 2, 256], BF16)
    ident_in = din("ident", [128, 128], BF16)
    y_out = nc.dram_tensor("y", [T, D], F32, kind="ExternalOutput").ap()

    all_inputs = [x_in, p_in, norm_in, attn_w_in, qn_in, kn_in, attn_w_out, fnet_w_in, fnet_w_out, ple_proj, ple_gate,
                  ple_norm, vtn_in, vts_in, mtab_in, idxk_in, idxv_in, idxu_in, idxm_in, dftA_in, dftct_in, dftst_in,
                  dftB_in, dftC3_in, ident_in]

    def touch_inputs():
        tiles = {}
        for ap in all_inputs:
            a = ap
            while len(a.shape) > 2:
                a = a[0]
            n = min(16, a.shape[-1])
            key = str(ap.dtype)
            if key not in tiles:
                tiles[key] = (nc.alloc_sbuf_tensor("touch_" + str(len(tiles)), [1, 16], ap.dtype), Buf("touch" + str(len(tiles))))
            t, b = tiles[key]
            S.dma("sp", t[0:1, 0:n], a[0:1, 0:n], writes=[b], track=b)

    def dscr(name, shape, dt):
        return nc.dram_tensor(name, list(shape), dt, kind="Internal").ap()

    XA = dscr("XA", [T, D], F32)
    XB = dscr("XB", [T, D], F32)
    bXA, bXB, bY = Buf("XA", True), Buf("XB", True), Buf("Y", True)
    QT, KTs, VS = [], [], []
    HKin, HKout, HVin, HVout = [], [], [], []
    for g, d in enumerate(DIL):
        LP = T // d
        QT.append(dscr(f"QT{g}", [HPG, 128, d, LP], BF16))
        KTs.append(dscr(f"KT{g}", [HPG, 128, d, LP], BF16))
        VS.append(dscr(f"VS{g}", [d, LP, HPG * 128], BF16))
        HKin.append(dscr(f"HKin{g}", [HPG * 2 * 128, d * 64], BF16))
        HVin.append(dscr(f"HVin{g}", [HPG * 2 * 64, d * 128], BF16))
        HKout.append([dscr(f"HKout{g}_{a}", [4 * HPG * 2 * 128, d * 64], BF16) for a in range(cfg.NATT)])
        HVout.append([dscr(f"HVout{g}_{a}", [4 * HPG * 2 * 64, d * 128], BF16) for a in range(cfg.NATT)])
    bQKV = Buf("QKV", True)
    bHin, bHout = [Buf(f"Hin{g}", True) for g in range(3)], Buf("Hout", True)
    SGT = dscr("SGT", [cfg.FW, T], BF16)
    bSGT = Buf("SGT", True)
    YT = dscr("YT", [cfg.FW, T], BF16)
    bYT = Buf("YT", True)
    UUloc = dscr("UUloc", [PL * T, 64], BF16)
    UUall = [dscr(f"UUall{a}", [4 * PL * T, 64], BF16) for a in range(max(cfg.NFN, 1))]
    bUU, bUUall = Buf("UU", True), Buf("UUall", True)
    ZT = dscr("ZT", [CHL, 2, TS], BF16)
    bZT = Buf("ZT", True)
    MXloc = dscr("MXloc", [4 * CHL, T], BF16)
    MXall = [dscr(f"MXall{a}", [16 * CHL, T], BF16) for a in range(max(cfg.NFN, 1))]
    bMX, bMXall = Buf("MX", True), Buf("MXall", True)

    sb_i = [0]
    nph = [0]
    phase = [None]

    def sbuf(shape, dt, name=None):
        sb_i[0] += 1
        nm = f"{name or 't'}_{sb_i[0]}"
        b = Buf(nm)
        if phase[0] is None:
            return nc.alloc_sbuf_tensor(nm, list(shape), dt), b
        t = phase[0][0].enter_context(nc.sbuf_tensor(nm, list(shape), dt))
        phase[0][1].append(b)
        return t, b

    class Phase:
        def __enter__(self):
            self.es = ExitStack()
            phase[0] = (self.es, [])
            return self

        def __exit__(self, *a):
            S.barrier()
            S.release(phase[0][1])
            self.es.close()
            phase[0] = None
            if a[0] is None:
                nph[0] += 1
                if nph[0] == getattr(cfg, "stop_after", -1):
                    raise _StopBuild()
            return False

    PS = Ring([(nc.alloc_psum_tensor(f"ps{i}", [128, 512], F32), Buf(f"ps{i}")) for i in range(6)])
    PTR = Ring([(nc.alloc_psum_tensor(f"pt{i}", [128, 1024], BF16), Buf(f"pt{i}")) for i in range(2)])

    touch_inputs()
    ident, b_ident = sbuf([128, 128], BF16, "ident")
    S.dma("sp", ident[:], ident_in[:, :], writes=[b_ident], track=b_ident)
    ones_bf, b_ones = sbuf([128, 128], BF16, "ones")
    S.op("dve", lambda e: e.memset(ones_bf[:], 1.0), writes=[b_ones])
    vtn, b_vt = sbuf([128, cfg.NKBT], F32, "vtn")
    S.dma("sp", vtn[:], vtn_in[:, :], writes=[b_vt], track=b_vt)
    vts, b_vts = sbuf([128, cfg.NKBT], F32, "vts")
    S.dma("sp", vts[:], vts_in[:, :], writes=[b_vts], track=b_vts)
    idxk, b_idxk = sbuf([128, 3, 2 * HPG], I32, "idxk")
    S.dma("sp", idxk[:], idxk_in[:, :, :], writes=[b_idxk], track=b_idxk)
    idxv, b_idxv = sbuf([128, 3, 2 * HPG], I32, "idxv")
    S.dma("sp", idxv[:], idxv_in[:, :, :], writes=[b_idxv], track=b_idxv)

    def ag_chunks(in_t, out_t, nr, rl, reads, writes):
        rpc = _rpc(nr, rl)
        for k in range(nr // rpc):
            ic = in_t[k * rpc:(k + 1) * rpc, :]
            oc = out_t[k * 4 * rpc:(k + 1) * 4 * rpc, :]
            if rl < 512 and (rpc * rl) % 512 == 0 and rpc % (512 // rl) == 0:
                b = 512 // rl
                ic = ic.rearrange("(a b) e -> a (b e)", b=b)
                oc = oc.rearrange("(a b) e -> a (b e)", b=b)
            elif rl > 512 and rl % 512 == 0:
                ic = ic.rearrange("a (b e) -> (a b) e", e=512)
                oc = oc.rearrange("a (b e) -> (a b) e", e=512)
            S.collective(ic, oc, GROUPS, reads=reads, writes=writes)
    idxu, b_idxu = sbuf([128, NCBO], I32, "idxu")
    S.dma("sp", idxu[:], idxu_in[:, :], writes=[b_idxu], track=b_idxu)
    idxm, b_idxm = sbuf([128, cfg.FW // 128], I32, "idxm")
    S.dma("sp", idxm[:], idxm_in[:, :], writes=[b_idxm], track=b_idxm)
    NV = 2 * cfg.DEPTH
    gvec, b_gvec = sbuf([128, NV, KC], F32, "gvec")
    with nc.allow_non_contiguous_dma(reason="tiny gain vectors"):
        for i in range(cfg.DEPTH):
            S.dma("sp", gvec[:, i, :], norm_in[i].rearrange("(kc p) -> p kc", p=128), writes=[b_gvec], track=b_gvec)
            S.dma("sp", gvec[:, cfg.DEPTH + i, :], ple_norm[i].rearrange("(kc p) -> p kc", p=128),
                  writes=[b_gvec], track=b_gvec)
        qkg, b_qkg = sbuf([128, 2 * cfg.NATT], F32, "qkg")
        for j in range(cfg.NATT):
            S.dma("sp", qkg[:, 2 * j:2 * j + 1], qn_in[j].rearrange("(p o) -> p o", o=1), writes=[b_qkg], track=b_qkg)
            S.dma("sp", qkg[:, 2 * j + 1:2 * j + 2], kn_in[j].rearrange("(p o) -> p o", o=1), writes=[b_qkg], track=b_qkg)
    eps128, b_eps = sbuf([128, 1], F32, "eps128")
    S.op("dve", lambda e: e.memset(eps128[:], EPS), writes=[b_eps])
    gfull, b_gfull = sbuf([128, KC, 128], F32, "gfull")
    onesf, b_onesf = sbuf([128, 128], F32, "onesf")
    S.op("dve", lambda e: e.memset(onesf[:], 1.0), writes=[b_onesf])

    def set_gain(vi):
        for kc in range(KC):
            S.op("dve", lambda e: e.tensor_scalar(out=gfull[:, kc, :], in0=onesf[:], scalar1=gvec[:, vi, kc:kc + 1],
                                                  scalar2=float(D) ** 0.5, op0=ALU.mult, op1=ALU.mult),
                 reads=[b_onesf, b_gvec], writes=[b_gfull])

    stat_ring = Ring([sbuf([128, 4], F32, "stat") for _ in range(4)])
    W = {}

    def alloc_prep():
        W["xt_ring"] = Ring([sbuf([128, D], F32, "xt") for _ in range(2)])
        W["xs_ring"] = Ring([sbuf([128, D], BF16, "xs") for _ in range(2)])
        W["xs_hi"] = {}
        W["junk"] = sbuf([128, D], BF16, "junk")
        W["wst_ring"] = Ring([sbuf([128, max(KC // 2, 1), TT], F32, "wst") for _ in range(2)])

    def alloc_proj_ev():
        W["ev_ring"] = Ring([sbuf([128, TT], BF16, "ev") for _ in range(3)])

    def alloc_proj():
        alloc_prep()
        W["hT"] = sbuf([128, KC, ST], BF16, "hT")
        W["wb_ring"] = Ring([sbuf([128, KC, TT], BF16, "wb") for _ in range(2)])
        W["ev_ring"] = Ring([sbuf([128, TT], BF16, "ev") for _ in range(3)])
        W["evf_ring"] = Ring([sbuf([128, TT], F32, "evf") for _ in range(3)])
        W["o_ring"] = Ring([sbuf([128, TT], BF16, "o") for _ in range(3)])
        W["qr_ring"] = Ring([sbuf([128, TT], F32, "qr") for _ in range(3)])

    def rstd_from_ssq(ssq_ap, b_ssq, n, out_ap, b_out):
        S.op("dve", lambda e: e.tensor_scalar(out=out_ap, in0=ssq_ap, scalar1=float(n) * EPS, scalar2=None,
                                              op0=ALU.add), reads=[b_ssq], writes=[b_out])
        S.op("act", lambda e: e.activation(out=out_ap, in_=out_ap, func=AF.Sqrt), reads=[b_out], writes=[b_out])
        S.op("dve", lambda e: e.reciprocal(out=out_ap, in_=out_ap), reads=[b_out], writes=[b_out])

    def prep_block(src_rows_ap, b_src, dst, b_dst, col0, kcn=KC):
        xt, b_xt = W["xt_ring"].next()
        S.dma("sp", xt[:], src_rows_ap, reads=[b_src], writes=[b_xt], track=b_xt)
        st, b_st = stat_ring.next()
        junk, b_junk = W["junk"]
        S.op("act", lambda e: e.activation(out=junk[:], in_=xt[:], func=AF.Square, accum_out=st[:, 0:1]),
             reads=[b_xt], writes=[b_junk, b_st])
        rstd_from_ssq(st[:, 0:1], b_st, D, st[:, 1:2], b_st)
        xs, b_xs = W["xs_ring"].next()
        b_xh = W["xs_hi"].setdefault(id(b_xs), Buf("xs_hi"))
        Hh = D // 2
        S.op("act", lambda e: e.activation(out=xs[:, 0:Hh], in_=xt[:, 0:Hh], func=AF.Copy, scale=st[:, 1:2]),
             reads=[b_xt, b_st], writes=[b_xs])
        S.op("dve", lambda e: e.tensor_scalar(out=xs[:, Hh:D], in0=xt[:, Hh:D], scalar1=st[:, 1:2], scalar2=None, op0=ALU.mult),
             reads=[b_xt, b_st], writes=[b_xh])
        for half in range(KC // 8 if KC >= 8 else 1):
            n = min(8, KC)
            pt, b_pt = PTR.next()
            for k in range(n):
                kc = half * 8 + k
                S.op("pe", lambda e: e.transpose(out=pt[:, k * 128:(k + 1) * 128], in_=xs[:, kc * 128:(kc + 1) * 128],
                                                 identity=ident[:]),
                     reads=[b_xs, b_xh, b_ident], writes=[b_pt], signal=(k == n - 1))
            S.op("dve", lambda e: e.tensor_tensor(out=dst[:, half * 8:half * 8 + n, col0:col0 + 128],
                                                  in0=pt[:, 0:n * 128].rearrange("p (k c) -> p k c", k=n),
                                                  in1=gfull[:, half * 8:half * 8 + n, :], op=ALU.mult),
                 reads=[b_pt, b_gfull], writes=[b_dst])

    def load_w(w_ap_cols, b_w=None):
        wb, b_wb = W["wb_ring"].next()
        nh = 2 if KC >= 2 else 1
        kh = KC // nh
        for hh in range(nh):
            wst, b_wst = W["wst_ring"].next()
            S.dma("sp", wst[:, 0:kh, :], w_ap_cols[hh * kh * 128:(hh + 1) * kh * 128, :].rearrange("(kc p) f -> p kc f", p=128),
                  writes=[b_wst], track=b_wst)
            step = 2 if kh >= 2 else 1
            for q0 in range(0, kh, step):
                if (q0 // step) % 2 == 0:
                    S.op("act", lambda e: e.activation(out=wb[:, hh * kh + q0:hh * kh + q0 + step, :], in_=wst[:, q0:q0 + step, :], func=AF.Copy),
                         reads=[b_wst], writes=[b_wb])
                else:
                    S.op("dve", lambda e: e.tensor_copy(out=wb[:, hh * kh + q0:hh * kh + q0 + step, :], in_=wst[:, q0:q0 + step, :]),
                         reads=[b_wst], writes=[b_wb])
        return wb, b_wb

    def run_blocks(items):
        if not items:
            return
        nxt = load_w(items[0][0])
        for k, (w_ap, fn) in enumerate(items):
            cur = nxt
            if k + 1 < len(items):
                nxt = load_w(items[k + 1][0])
            fn(*cur)

    def mm_fm(wb, b_wb, m, src, b_src, c0, ncols, kcn=KC):
        ps, b_ps = PS.next()
        for kc in range(kcn):
            S.op("pe", lambda e: e.matmul(ps[:, 0:ncols], lhsT=wb[:, kc, m * 128:(m + 1) * 128],
                                          rhs=src[:, kc, c0:c0 + ncols], start=(kc == 0), stop=(kc == kcn - 1)),
                 reads=[b_wb, b_src], writes=[b_ps], signal=(kc == kcn - 1))
        return ps, b_ps

    def mm_tm(wb, b_wb, src, b_src, c0, kcn=KC, ncols=TT):
        ps, b_ps = PS.next()
        for kc in range(kcn):
            S.op("pe", lambda e: e.matmul(ps[:, 0:ncols], lhsT=src[:, kc, c0:c0 + 128], rhs=wb[:, kc, 0:ncols],
                                          start=(kc == 0), stop=(kc == kcn - 1)),
                 reads=[b_wb, b_src], writes=[b_ps], signal=(kc == kcn - 1))
        return ps, b_ps

    def attn_proj(Xc, bXc, li, j):
        Wi = attn_w_in[j]
        hT, b_hT = W["hT"]
        for g, d in enumerate(DIL):
            Lq = T // d
            nst = T // ST
            for s in range(nst):
                per = Lq // ST if Lq >= ST else 0
                blocks = []
                if Lq >= ST:
                    rho, pos0 = s // per, (s % per) * ST
                    blocks = [(rho, pos0, ST)]
                else:
                    nr = ST // Lq
                    blocks = [(s * nr + r, 0, Lq) for r in range(nr)]
                set_gain(li) if (g == 0 and s == 0) else None
                col = 0
                for (rho, pos0, npos) in blocks:
                    for bb in range(npos // 128):
                        t0 = rho + d * (pos0 + bb * 128)
                        pp0 = pos0 + bb * 128
                        rows = Xc.rearrange("(i r) c -> r i c", r=d)[rho, pp0:pp0 + 128, :]
                        prep_block(rows, bXc, hT, b_hT, col)
                        col += 128
                pendB = [None]

                def flushB():
                    if pendB[0] is not None:
                        f = pendB[0]
                        pendB[0] = None
                        f()

                items = []
                for which in range(2):
                  for hb in range(HPG // 4):
                    def qk_block(wb, b_wb, which=which, hb=hb):
                        dstT = QT[g] if which == 0 else KTs[g]
                        for m in range(4):
                            h = hb * 4 + m
                            for n in range(ST // TT):
                                ps, b_ps = mm_fm(wb, b_wb, m, hT, b_hT, n * TT, TT)
                                sq, b_sq = W["ev_ring"].next()
                                S.op("act", lambda e: e.activation(out=sq[:], in_=ps[:], func=AF.Square),
                                     reads=[b_ps], writes=[b_sq])
                                qr, b_qr = W["qr_ring"].next()
                                S.op("dve", lambda e: e.tensor_copy(out=qr[:], in_=ps[:]), reads=[b_ps, b_sq], writes=[b_qr])
                                flushB()

                                def partB(ps=qr, b_ps=b_qr, sq=sq, b_sq=b_sq, h=h, n=n, which=which, dstT=dstT):
                                    ps2, b_ps2 = PS.next()
                                    S.op("pe", lambda e: e.matmul(ps2[:], lhsT=ones_bf[:], rhs=sq[:], start=True, stop=True),
                                         reads=[b_ones, b_sq], writes=[b_ps2])
                                    rs, b_rs = W["evf_ring"].next()
                                    S.op("act", lambda e: e.activation(out=rs[:], in_=ps2[:], func=AF.Sqrt, bias=eps128[:, 0:1], scale=1.0 / 128),
                                         reads=[b_ps2, b_eps], writes=[b_rs])
                                    S.op("dve", lambda e: e.reciprocal(out=rs[:], in_=rs[:]), reads=[b_rs], writes=[b_rs])
                                    o, b_o = W["o_ring"].next()
                                    S.op("dve", lambda e: e.scalar_tensor_tensor(out=o[:], in0=ps[:], scalar=qkg[:, 2 * j + which:2 * j + which + 1],
                                                                                 in1=rs[:], op0=ALU.mult, op1=ALU.mult),
                                         reads=[b_ps, b_rs, b_qkg], writes=[b_o])
                                    c = n * TT
                                    cc = 0
                                    for (rho, pos0, npos) in blocks:
                                        lo, hi = max(c, cc), min(c + TT, cc + npos)
                                        if lo < hi:
                                            S.store(dstT[h, :, rho, pos0 + (lo - cc):pos0 + (hi - cc)], o[:, lo - c:hi - c],
                                                    reads=[b_o], writes=[bQKV], track=b_o)
                                            if which == 1:
                                                if pos0 == 0 and lo == cc:
                                                    S.store(HKin[g][(h * 2) * 128:(h * 2 + 1) * 128, rho * 64:(rho + 1) * 64],
                                                            o[:, lo - c:lo - c + 64], reads=[b_o], writes=[bHin[g]], track=b_o)
                                                if pos0 + npos == Lq and hi == cc + npos:
                                                    S.store(HKin[g][(h * 2 + 1) * 128:(h * 2 + 2) * 128, rho * 64:(rho + 1) * 64],
                                                            o[:, hi - c - 64:hi - c], reads=[b_o], writes=[bHin[g]], track=b_o)
                                        cc += npos
                                pendB[0] = partB
                    f0 = which * cfg.QKV + (g * HPG + hb * 4) * 128
                    items.append((Wi[:, f0:f0 + TT], qk_block))
                for hb in range(HPG // 4):
                  def v_block(wb, b_wb, hb=hb):
                    flushB()
                    col = 0
                    for (rho, pos0, npos) in blocks:
                        for bb in range(npos // 128):
                            ps, b_ps = mm_tm(wb, b_wb, hT, b_hT, col)
                            o, b_o = W["ev_ring"].next()
                            S.op("act", lambda e: e.activation(out=o[:], in_=ps[:], func=AF.Copy), reads=[b_ps], writes=[b_o])
                            r0 = pos0 + bb * 128
                            S.store(VS[g][rho, r0:r0 + 128, hb * TT:(hb + 1) * TT], o[:], reads=[b_o], writes=[bQKV], track=b_o)
                            hv = HVin[g].rearrange("(h s k) (r c) -> k h s r c", s=2, k=64, c=128)
                            if r0 == 0:
                                S.store(hv[:, hb * 4:(hb + 1) * 4, 0, rho, :], o[0:64, :].rearrange("k (h c) -> k h c", c=128),
                                      reads=[b_o], writes=[bHin[g]], track=b_o)
                            if r0 + 128 == Lq:
                                S.store(hv[:, hb * 4:(hb + 1) * 4, 1, rho, :], o[64:128, :].rearrange("k (h c) -> k h c", c=128),
                                      reads=[b_o], writes=[bHin[g]], track=b_o)
                            col += 128
                  f0 = 2 * cfg.QKV + (g * HPG + hb * 4) * 128
                  items.append((Wi[:, f0:f0 + TT], v_block))
                run_blocks(items)
                flushB()
            if g > 0:
                halo_exchange_g(j, g - 1)
        for s in range(T // ST):
            for bb in range(ST // 128):
                t0 = s * ST + bb * 128
                prep_block(Xc[t0:t0 + 128, :], bXc, hT, b_hT, bb * 128)
            items = []
            for fb in range(cfg.AW // TT):
                def g_block(wb, b_wb, fb=fb, s=s):
                    for m in range(4):
                        for n in range(ST // TT):
                            ps, b_ps = mm_fm(wb, b_wb, m, hT, b_hT, n * TT, TT)
                            o, b_o = W["ev_ring"].next()
                            S.op("act", lambda e: e.activation(out=o[:], in_=ps[:], func=AF.Silu), reads=[b_ps], writes=[b_o])
                            fr = fb * TT + m * 128
                            S.store(SGT[fr:fr + 128, s * ST + n * TT:s * ST + (n + 1) * TT], o[:], reads=[b_o], writes=[bSGT], track=b_o)
                f0 = 3 * cfg.QKV + fb * TT
                items.append((Wi[:, f0:f0 + TT], g_block))
            run_blocks(items)
            if s == 0:
                halo_exchange_g(j, 2)

    def halo_exchange_g(ai, g):
        d = DIL[g]
        S.flush_stores()
        ag_chunks(HKin[g], HKout[g][ai], HPG * 2 * 128, d * 64, [bHin[g]], [bHout])
        ag_chunks(HVin[g], HVout[g][ai], HPG * 2 * 64, d * 128, [bHin[g]], [bHout])

    def attn_core(ai):
        qs_ring = Ring([sbuf([128, ST], BF16, "qs") for _ in range(3)])
        ks_ring = Ring([sbuf([128, 2 * ST], BF16, "ks") for _ in range(3)])
        vs_ring = Ring([sbuf([128, 32, 128], BF16, "vs") for _ in range(3)])
        mt_ring = Ring([sbuf([128, 3, 384], F32, "mt") for _ in range(2)])
        kh_ring = Ring([sbuf([128, 1024], BF16, "kh") for _ in range(3)])
        vh_ring = Ring([sbuf([128, 2048], BF16, "vh") for _ in range(3)])

        def khalo(dst, g, d, col, b_ks):
            kh, b_kh = kh_ring.next()
            S.idma(kh[:, 0:d * 64], HKout[g][ai][:, :], idxk[:, g, col:col + 1], reads=[bHout, b_idxk], writes=[b_kh], track=b_kh)
            S.op("act", lambda e: e.activation(out=dst, in_=kh[:, 0:d * 64].rearrange("p (r l) -> p r l", l=64), func=AF.Copy),
                 reads=[b_kh], writes=[b_ks])

        def vhalo(dst, g, d, col, b_vs):
            vh, b_vh = vh_ring.next()
            S.idma(vh[0:64, 0:d * 128], HVout[g][ai][:, :], idxv[0:64, g, col:col + 1], reads=[bHout, b_idxv], writes=[b_vh], track=b_vh)
            S.op("act", lambda e: e.activation(out=dst, in_=vh[0:64, 0:d * 128].rearrange("k (r c) -> k r c", c=128), func=AF.Copy),
                 reads=[b_vh], writes=[b_vs])

        acc, b_acc = sbuf([128, 2, ST], F32, "acc")
        ex_ring = Ring([sbuf([128, 256], F32, "ex") for _ in range(4)])
        pt_ring = Ring([sbuf([128, 256], BF16, "pT") for _ in range(4)])
        sg_ring = Ring([sbuf([128, ST], BF16, "sg") for _ in range(2)])
        yo_ring = Ring([sbuf([128, ST], BF16, "yo") for _ in range(2)])
        tmp, b_tmp = sbuf([128, ST], F32, "tmp")
        nst = T // ST
        for s_ in range(nst):
            T0 = s_ * ST
            for h in range(HPG):
                mt, b_mt = mt_ring.next()
                S.dma("sp", mt[:], mtab_in.rearrange("p (g h) c -> p h g c", g=3)[:, h, :, :], writes=[b_mt], track=b_mt)
                hc = slice(h * 128, (h + 1) * 128)
                for g, d in enumerate(DIL):
                    Lq = ST // d
                    nqb = Lq // 128
                    P0 = T0 // d
                    Wd = Lq + 128
                    qs, b_qs = qs_ring.next()
                    ks, b_ks = ks_ring.next()
                    vs, b_vs = vs_ring.next()
                    S.dma("sp", qs[:, 0:ST].rearrange("p (r l) -> p r l", r=d), QT[g][h, :, :, P0:P0 + Lq],
                          reads=[bQKV], writes=[b_qs], track=b_qs)
                    ks3 = ks[:, 0:d * Wd].rearrange("p (r w) -> p r w", r=d)
                    S.dma("sp", ks3[:, :, 64:Lq], KTs[g][h, :, :, P0:P0 + Lq - 64], reads=[bQKV], writes=[b_ks], track=b_ks)
                    S.dma("sp", ks3[:, :, Lq + 64:Lq + 128], KTs[g][h, :, :, P0 + Lq - 64:P0 + Lq], reads=[bQKV], writes=[b_ks], track=b_ks)
                    hk = HKout[g][ai].rearrange("n (r l) -> n r l", l=64)
                    if s_ > 0:
                        S.dma("sp", ks3[:, :, 0:64], KTs[g][h, :, :, P0 - 64:P0], reads=[bQKV], writes=[b_ks], track=b_ks)
                    else:
                        khalo(ks3[:, :, 0:64], g, d, h, b_ks)
                    if s_ < nst - 1:
                        S.dma("sp", ks3[:, :, Lq:Lq + 64], KTs[g][h, :, :, P0 + Lq:P0 + Lq + 64], reads=[bQKV], writes=[b_ks], track=b_ks)
                    else:
                        khalo(ks3[:, :, Lq:Lq + 64], g, d, HPG + h, b_ks)
                    vs4 = vs[:, 0:d * (nqb + 1), :].rearrange("k (r j) c -> k r j c", j=nqb + 1)
                    if nqb > 1:
                        for rho in range(d):
                            S.dma("sp", vs4[:, rho, 1:nqb, :],
                                  VS[g][rho, P0 + 64:P0 + 64 + 128 * (nqb - 1), hc].rearrange("(j k) c -> k j c", k=128),
                                  reads=[bQKV], writes=[b_vs], track=b_vs)
                    S.dma("sp", vs4[64:128, :, 0, :], VS[g][:, P0:P0 + 64, hc].rearrange("r k c -> k r c"),
                          reads=[bQKV], writes=[b_vs], track=b_vs)
                    S.dma("sp", vs4[64:128, :, nqb, :], VS[g][:, P0 + Lq - 64:P0 + Lq, hc].rearrange("r k c -> k r c"),
                          reads=[bQKV], writes=[b_vs], track=b_vs)
                    hvv = HVout[g][ai].rearrange("n (r c) -> n r c", c=128)
                    if s_ > 0:
                        S.dma("sp", vs4[0:64, :, 0, :], VS[g][:, P0 - 64:P0, hc].rearrange("r k c -> k r c"),
                              reads=[bQKV], writes=[b_vs], track=b_vs)
                    else:
                        vhalo(vs4[0:64, :, 0, :], g, d, h, b_vs)
                    if s_ < nst - 1:
                        S.dma("sp", vs4[0:64, :, nqb, :], VS[g][:, P0 + Lq:P0 + Lq + 64, hc].rearrange("r k c -> k r c"),
                              reads=[bQKV], writes=[b_vs], track=b_vs)
                    else:
                        vhalo(vs4[0:64, :, nqb, :], g, d, HPG + h, b_vs)
                    def stageA(rho, qb):
                        qcol = rho * Lq + qb * 128
                        sc, b_sc = PS.next()
                        for e in range(2):
                            kcol = rho * Wd + (qb + e) * 128
                            S.op("pe", lambda en: en.matmul(sc[:, e * 128:(e + 1) * 128], lhsT=ks[:, kcol:kcol + 128],
                                                           rhs=qs[:, qcol:qcol + 128], start=True, stop=True),
                                 reads=[b_ks, b_qs], writes=[b_sc], signal=(e == 1))
                        ex, b_ex = ex_ring.next()
                        S.op("act", lambda en: en.activation(out=ex[:], in_=sc[:, 0:256], func=AF.Exp, scale=128.0 ** -0.5),
                             reads=[b_sc], writes=[b_ex])
                        pT, b_pT = pt_ring.next()
                        for e in range(2):
                            jcol = cfg.NKB_OFF[g] + P0 // 128 + qb + e
                            swapped = (qb + e == nqb)
                            vtab = vts if swapped else vtn
                            mv = 2 if swapped else e
                            S.op("dve", lambda en: en.scalar_tensor_tensor(out=pT[:, e * 128:(e + 1) * 128], in0=ex[:, e * 128:(e + 1) * 128],
                                                                          scalar=vtab[:, jcol:jcol + 1], in1=mt[:, g, mv * 128:(mv + 1) * 128],
                                                                          op0=ALU.mult, op1=ALU.mult),
                                 reads=[b_ex, b_vt, b_vts, b_mt], writes=[b_pT])
                        return pT, b_pT

                    def stageB(rho, qb, pT, b_pT):
                        nd, b_nd = PS.next()
                        for e in range(2):
                            vb = rho * (nqb + 1) + qb + e
                            S.op("pe", lambda en: en.matmul(nd[:, 0:128], lhsT=vs[:, vb, :], rhs=pT[:, e * 128:(e + 1) * 128],
                                                           start=(e == 0), stop=(e == 1)),
                                 reads=[b_vs, b_pT], writes=[b_nd], signal=False)
                        for e in range(2):
                            S.op("pe", lambda en: en.matmul(nd[:, 128:256], lhsT=ones_bf[:], rhs=pT[:, e * 128:(e + 1) * 128],
                                                           start=(e == 0), stop=(e == 1)),
                                 reads=[b_ones, b_pT], writes=[b_nd], signal=(e == 1))
                        dst = acc[:, :, :].rearrange("p a (i r) -> p a r i", r=d)[:, :, rho, qb * 128:(qb + 1) * 128]
                        src = nd[:, 0:256].rearrange("p (a b) -> p a b", a=2)
                        if g == 0:
                            S.op("act", lambda en: en.activation(out=dst, in_=src, func=AF.Copy), reads=[b_nd], writes=[b_acc])
                        else:
                            S.op("dve", lambda en: en.tensor_tensor(out=dst, in0=dst, in1=src, op=ALU.add),
                                 reads=[b_nd, b_acc], writes=[b_acc])

                    blks = [(rho, qb) for rho in range(d) for qb in range(nqb)]
                    cur = stageA(*blks[0])
                    for bi, (rho, qb) in enumerate(blks):
                        nxtp = stageA(*blks[bi + 1]) if bi + 1 < len(blks) else None
                        stageB(rho, qb, *cur)
                        cur = nxtp
                sg, b_sg = sg_ring.next()
                S.dma("sp", sg[:], SGT[h * 128:(h + 1) * 128, T0:T0 + ST], reads=[bSGT], writes=[b_sg], track=b_sg)
                S.op("dve", lambda en: en.tensor_scalar(out=tmp[:], in0=acc[:, 1, :], scalar1=1e-30, scalar2=None, op0=ALU.add),
                     reads=[b_acc], writes=[b_tmp])
                S.op("dve", lambda en: en.reciprocal(out=tmp[:], in_=tmp[:]), reads=[b_tmp], writes=[b_tmp])
                S.op("dve", lambda en: en.tensor_tensor(out=tmp[:], in0=tmp[:], in1=acc[:, 0, :], op=ALU.mult),
                     reads=[b_tmp, b_acc], writes=[b_tmp])
                yo, b_yo = yo_ring.next()
                S.op("dve", lambda en: en.tensor_tensor(out=yo[:], in0=tmp[:], in1=sg[:], op=ALU.mult),
                     reads=[b_tmp, b_sg], writes=[b_yo])
                S.store(YT[h * 128:(h + 1) * 128, T0:T0 + ST], yo[:], reads=[b_yo], writes=[bYT], track=b_yo)

    rr = [0]

    def load_w_resident(dst, b_dst, W_rows_ap, kcn, ncols):
        for kc0 in range(0, kcn, KC // 2 if KC >= 2 else 1):
            kn = min(KC // 2 if KC >= 2 else 1, kcn - kc0)
            for c0 in range(0, ncols, TT):
                wst, b_wst = W["wst_ring"].next()
                S.dma("sp", wst[:, 0:kn, :], W_rows_ap[kc0 * 128:(kc0 + kn) * 128, c0:c0 + TT].rearrange("(kc p) f -> p kc f", p=128),
                      writes=[b_wst], track=b_wst)
                rr[0] += 1
                if rr[0] % 2 == 0:
                    S.op("act", lambda e: e.activation(out=dst[:, kc0:kc0 + kn, c0:c0 + TT], in_=wst[:, 0:kn, :], func=AF.Copy),
                         reads=[b_wst], writes=[b_dst])
                else:
                    S.op("dve", lambda e: e.tensor_copy(out=dst[:, kc0:kc0 + kn, c0:c0 + TT], in_=wst[:, 0:kn, :]),
                         reads=[b_wst], writes=[b_dst])

    def wout_pass(Xsrc, bXsrc, Xdst, bXdst, W_rows_ap, yT_rows0, wres, b_wres):
        load_w_resident(wres, b_wres, W_rows_ap, KC, D)
        yt_ring = Ring([sbuf([128, KC, TT], BF16, "yt") for _ in range(2)])
        for tb in range(T // 128):
            t0 = tb * 128
            if tb % 4 == 0:
                ytt, b_yt = yt_ring.next()
                S.dma("sp", ytt[:], YT[yT_rows0:yT_rows0 + D, t0:t0 + TT].rearrange("(kc p) t -> p kc t", p=128),
                      reads=[bYT], writes=[b_yt], track=b_yt)
            yt = ytt[:, :, (tb % 4) * 128:(tb % 4 + 1) * 128]
            xt, b_xt = W["xt_ring"].next()
            S.dma("sp", xt[:], Xsrc[t0:t0 + 128, :], reads=[bXsrc], writes=[b_xt], track=b_xt)
            for fb in range(D // TT):
                ps, b_ps = PS.next()
                for kc in range(KC):
                    S.op("pe", lambda e: e.matmul(ps[:], lhsT=yt[:, kc, :], rhs=wres[:, kc, fb * TT:(fb + 1) * TT],
                                                  start=(kc == 0), stop=(kc == KC - 1)),
                         reads=[b_yt, b_wres], writes=[b_ps], signal=(kc == KC - 1))
                S.op("dve", lambda e: e.tensor_tensor(out=xt[:, fb * TT:(fb + 1) * TT], in0=ps[:], in1=xt[:, fb * TT:(fb + 1) * TT], op=ALU.add),
                     reads=[b_ps, b_xt], writes=[b_xt])
            S.store(Xdst[t0:t0 + 128, :], xt[:], reads=[b_xt], writes=[bXdst], track=b_xt)

    def ple_pass(Xsrc, bXsrc, Xdst, bXdst, li, wres, b_wres, wp, b_wp):
        set_gain(cfg.DEPTH + li)
        load_w_resident(wres, b_wres, ple_gate[li], KC, D)
        PK = cfg.PLE // 128
        load_w_resident(wp, b_wp, ple_proj[li], PK, D)
        h2_ring = Ring([sbuf([128, KC, 128], BF16, "h2") for _ in range(2)])
        pin_ring = Ring([sbuf([128, cfg.PLE], F32, "pin") for _ in range(2)])
        pb_ring = Ring([sbuf([128, cfg.PLE], BF16, "pb") for _ in range(2)])
        pT_ring = Ring([sbuf([128, PK, 128], BF16, "ppT") for _ in range(2)])
        sig_ring = Ring([sbuf([128, TT], F32, "sig") for _ in range(2)])
        xo_ring = Ring([sbuf([128, D], F32, "xo") for _ in range(2)])
        for tb in range(T // 128):
            t0 = tb * 128
            h2, b_h2 = h2_ring.next()
            prep_block(Xsrc[t0:t0 + 128, :], bXsrc, h2, b_h2, 0)
            xo, b_xo = xo_ring.next()
            S.dma("sp", xo[:], Xsrc[t0:t0 + 128, :], reads=[bXsrc], writes=[b_xo], track=b_xo)
            pin, b_pin = pin_ring.next()
            S.dma("sp", pin[:], p_in[li, t0:t0 + 128, :], writes=[b_pin], track=b_pin)
            pb, b_pb = pb_ring.next()
            S.op("act", lambda e: e.activation(out=pb[:], in_=pin[:], func=AF.Copy), reads=[b_pin], writes=[b_pb])
            ptp, b_ptp = PTR.next()
            for k in range(PK):
                S.op("pe", lambda e: e.transpose(out=ptp[:, k * 128:(k + 1) * 128], in_=pb[:, k * 128:(k + 1) * 128], identity=ident[:]),
                     reads=[b_pb, b_ident], writes=[b_ptp], signal=(k == PK - 1))
            ppT, b_ppT = pT_ring.next()
            S.op("act", lambda e: e.activation(out=ppT[:], in_=ptp[:, 0:PK * 128].rearrange("p (k c) -> p k c", k=PK), func=AF.Copy),
                 reads=[b_ptp], writes=[b_ppT])
            for fb in range(D // TT):
                ps, b_ps = PS.next()
                for kc in range(KC):
                    S.op("pe", lambda e: e.matmul(ps[:], lhsT=h2[:, kc, :], rhs=wres[:, kc, fb * TT:(fb + 1) * TT],
                                                  start=(kc == 0), stop=(kc == KC - 1)),
                         reads=[b_h2, b_wres], writes=[b_ps], signal=(kc == KC - 1))
                sig, b_sig = sig_ring.next()
                S.op("act", lambda e: e.activation(out=sig[:], in_=ps[:], func=AF.Sigmoid), reads=[b_ps], writes=[b_sig])
                ps2, b_ps2 = PS.next()
                for k in range(PK):
                    S.op("pe", lambda e: e.matmul(ps2[:], lhsT=ppT[:, k, :], rhs=wp[:, k, fb * TT:(fb + 1) * TT],
                                                  start=(k == 0), stop=(k == PK - 1)),
                         reads=[b_ppT, b_wp], writes=[b_ps2], signal=(k == PK - 1))
                S.op("dve", lambda e: e.tensor_tensor(out=sig[:], in0=ps2[:], in1=sig[:], op=ALU.mult),
                     reads=[b_ps2, b_sig], writes=[b_sig])
                S.op("dve", lambda e: e.tensor_tensor(out=xo[:, fb * TT:(fb + 1) * TT], in0=xo[:, fb * TT:(fb + 1) * TT], in1=sig[:], op=ALU.add),
                     reads=[b_sig, b_xo], writes=[b_xo])
            S.store(Xdst[t0:t0 + 128, :], xo[:], reads=[b_xo], writes=[bXdst], track=b_xo)

    def fnet_proj(Xc, bXc, li, j):
        Wi = fnet_w_in[j]
        hT, b_hT = W["hT"]
        set_gain(li)
        for s in range(T // ST):
            for bb in range(ST // 128):
                t0 = s * ST + bb * 128
                prep_block(Xc[t0:t0 + 128, :], bXc, hT, b_hT, bb * 128)
            items = []
            for fb in range(cfg.FW // TT):
                def u_block(wb, b_wb, fb=fb, s=s):
                    for bb in range(ST // 128):
                        ps, b_ps = mm_tm(wb, b_wb, hT, b_hT, bb * 128)
                        o, b_o = W["ev_ring"].next()
                        S.op("act", lambda e: e.activation(out=o[:], in_=ps[:], func=AF.Copy), reads=[b_ps], writes=[b_o])
                        t0 = s * ST + bb * 128
                        uv = UUloc.rearrange("(c t) e -> t c e", t=T)
                        S.store(uv[t0:t0 + 128, fb * 8:(fb + 1) * 8, :], o[:].rearrange("t (c e) -> t c e", e=64),
                                reads=[b_o], writes=[bUU], track=b_o)
                items.append((Wi[:, fb * TT:(fb + 1) * TT], u_block))
            run_blocks(items)
        S.flush_stores()
        ag_chunks(UUloc.rearrange("(n b) e -> n (b e)", b=128), UUall[j].rearrange("(n b) e -> n (b e)", b=128),
                  PL * (T // 128), 128 * 64, [bUU], [bUUall])
        for s in range(T // ST):
            for bb in range(ST // 128):
                t0 = s * ST + bb * 128
                prep_block(Xc[t0:t0 + 128, :], bXc, hT, b_hT, bb * 128)
            items = []
            for fb in range(cfg.FW // TT):
                def fg_block(wb, b_wb, fb=fb, s=s):
                    for m in range(4):
                        for n in range(ST // TT):
                            ps, b_ps = mm_fm(wb, b_wb, m, hT, b_hT, n * TT, TT)
                            o, b_o = W["ev_ring"].next()
                            S.op("act", lambda e: e.activation(out=o[:], in_=ps[:], func=AF.Silu), reads=[b_ps], writes=[b_o])
                            fr = fb * TT + m * 128
                            S.store(SGT[fr:fr + 128, s * ST + n * TT:s * ST + (n + 1) * TT], o[:], reads=[b_o], writes=[bSGT], track=b_o)
                items.append((Wi[:, cfg.FW + fb * TT:cfg.FW + (fb + 1) * TT], fg_block))
            run_blocks(items)

    def fnet_dft(fi, dA, b_dA, dct, b_dct, dst_, b_dst_, dB, b_dB, dC3, b_dC3):
        AR = T // 128
        NA = 4 * AR
        CB = 64
        xx, b_xx = sbuf([128, 128, CB], BF16, "dx")
        y2, b_y2 = sbuf([128, CB, 2, 2, 128], BF16, "dy2")
        zs, b_zs = sbuf([64, 2, TS], BF16, "dzs")
        t_ring = Ring([sbuf([128, 2, 128], F32, "dt") for _ in range(4)])
        uall = UUall[fi].rearrange("(n b) e -> n (b e)", b=128)
        for i in range(NCBO):
            S.idma(xx[:].rearrange("a b c -> a (b c)"), uall, idxu[:, i:i + 1], reads=[bUUall, b_idxu], writes=[b_xx], track=b_xx)
            for c in range(CB):
                ps, b_ps = PS.next()
                S.op("pe", lambda e: e.matmul(ps[:], lhsT=xx[:, :, c], rhs=dA[:], start=True, stop=True),
                     reads=[b_xx, b_dA], writes=[b_ps])
                pv = ps[:].rearrange("p (q r j) -> p q r j", q=2, r=2)
                re1, im1 = pv[:, :, 0, :], pv[:, :, 1, :]
                ta, b_ta = t_ring.next()
                tb_, b_tb = t_ring.next()
                S.op("dve", lambda e: e.tensor_tensor(out=ta[:], in0=re1, in1=dct[:], op=ALU.mult), reads=[b_ps, b_dct], writes=[b_ta])
                S.op("dve", lambda e: e.tensor_tensor(out=tb_[:], in0=im1, in1=dst_[:], op=ALU.mult), reads=[b_ps, b_dst_], writes=[b_tb])
                S.op("dve", lambda e: e.tensor_tensor(out=y2[:, c, :, 0, :], in0=ta[:], in1=tb_[:], op=ALU.add),
                     reads=[b_ta, b_tb], writes=[b_y2])
                tc_, b_tc = t_ring.next()
                td, b_td = t_ring.next()
                S.op("dve", lambda e: e.tensor_tensor(out=tc_[:], in0=im1, in1=dct[:], op=ALU.mult), reads=[b_ps, b_dct], writes=[b_tc])
                S.op("dve", lambda e: e.tensor_tensor(out=td[:], in0=re1, in1=dst_[:], op=ALU.mult), reads=[b_ps, b_dst_], writes=[b_td])
                S.op("dve", lambda e: e.tensor_tensor(out=y2[:, c, :, 1, :], in0=tc_[:], in1=td[:], op=ALU.subtract),
                     reads=[b_tc, b_td], writes=[b_y2])
            for jj in range(128):
                ps, b_ps = PS.next()
                k = 0
                for sq in range(2):
                    for r in range(2):
                        S.op("pe", lambda e: e.matmul(ps[0:CB, 0:256], lhsT=y2[:, :, sq, r, jj], rhs=dB[:, sq, r, :],
                                                      start=(k == 0), stop=(k == 3)),
                             reads=[b_y2, b_dB], writes=[b_ps], signal=(k == 3))
                        k += 1
                src = ps[0:CB, 0:256].rearrange("p (r d) -> p r d", r=2)[:, :, 0:NA]
                dstz = zs[:, :, :].rearrange("c r (d j) -> c r j d", j=128)[:, :, jj, 0:NA]
                if jj % 2 == 0:
                    S.op("act", lambda e: e.activation(out=dstz, in_=src, func=AF.Copy), reads=[b_ps], writes=[b_zs])
                else:
                    S.op("dve", lambda e: e.tensor_copy(out=dstz, in_=src), reads=[b_ps], writes=[b_zs])
            S.store(ZT[i * CB:(i + 1) * CB, :, :], zs[:], reads=[b_zs], writes=[bZT], track=b_zs, lag=0)
        zt_ring = Ring([sbuf([128, 2, 2, TT], BF16, "zt") for _ in range(2)])
        for gi in range(CHL // 256):
            for n in range(TS // TT):
                zt, b_zt = zt_ring.next()
                for kch in range(2):
                    r0 = gi * 256 + kch * 128
                    S.dma("sp", zt[:, kch, :, :], ZT[r0:r0 + 128, :, n * TT:(n + 1) * TT], reads=[bZT], writes=[b_zt], track=b_zt)
                for cob in range(2):
                    ps, b_ps = PS.next()
                    k = 0
                    for kch in range(2):
                        for r in range(2):
                            S.op("pe", lambda e: e.matmul(ps[:], lhsT=dC3[:, r, kch, cob * 128:(cob + 1) * 128], rhs=zt[:, kch, r, :],
                                                          start=(k == 0), stop=(k == 3)),
                                 reads=[b_dC3, b_zt], writes=[b_ps], signal=(k == 3))
                            k += 1
                    o, b_o = W["ev_ring"].next()
                    S.op("act", lambda e: e.activation(out=o[:], in_=ps[:], func=AF.Copy), reads=[b_ps], writes=[b_o])
                    q = (n * TT) // T
                    tl = (n * TT) % T
                    row = q * CHL + gi * 256 + cob * 128
                    S.store(MXloc[row:row + 128, tl:tl + TT], o[:], reads=[b_o], writes=[bMX], track=b_o)

    def fnet_gate(fi):
        mx_ring = Ring([sbuf([128, T], BF16, "mx") for _ in range(2)])
        sg_ring = Ring([sbuf([128, T], BF16, "sgf") for _ in range(2)])
        yo_ring = Ring([sbuf([128, T], BF16, "yof") for _ in range(2)])
        for gb in range(cfg.FW // 128):
            mx, b_mx = mx_ring.next()
            S.idma(mx[:], MXall[fi][:, :], idxm[:, gb:gb + 1], reads=[bMXall, b_idxm], writes=[b_mx], track=b_mx)
            sg, b_sg = sg_ring.next()
            S.dma("sp", sg[:], SGT[gb * 128:(gb + 1) * 128, :], reads=[bSGT], writes=[b_sg], track=b_sg)
            yo, b_yo = yo_ring.next()
            S.op("dve", lambda e: e.tensor_tensor(out=yo[:], in0=mx[:], in1=sg[:], op=ALU.mult), reads=[b_mx, b_sg], writes=[b_yo])
            S.dma("sp", YT[gb * 128:(gb + 1) * 128, :], yo[:], reads=[b_yo], writes=[bYT], track=b_yo)

    cur, bcur = x_in, Buf("xin", True)
    pp = [(XA, bXA), (XB, bXB)]
    ppi = [0]

    def nxt(final=False):
        if final:
            return y_out, bY
        r = pp[ppi[0] % 2]
        ppi[0] += 1
        return r

    def mix_phase(src, bsrc, dst, bdst, W_rows, yrow0):
        with Phase():
            alloc_prep()
            wres, b_wres = sbuf([128, KC, D], BF16, "wres")
            wout_pass(src, bsrc, dst, bdst, W_rows, yrow0, wres, b_wres)

    try:
        for li in range(cfg.DEPTH):
            j = li // 2
            if li % 2 == 0:
                with Phase():
                    alloc_proj()
                    attn_proj(cur, bcur, li, j)
                with Phase():
                    attn_core(j)
                d1, bd1 = nxt()
                mix_phase(cur, bcur, d1, bd1, attn_w_out[j], 0)
            else:
                with Phase():
                    alloc_proj()
                    fnet_proj(cur, bcur, li, j)
                with Phase():
                    alloc_proj_ev()
                    dA, b_dA = sbuf([128, 512], BF16, "dA")
                    S.dma("sp", dA[:], dftA_in[:, :], writes=[b_dA], track=b_dA)
                    dct, b_dct = sbuf([128, 2, 128], F32, "dct")
                    S.dma("sp", dct[:], dftct_in[:, :, :], writes=[b_dct], track=b_dct)
                    dst_, b_dst_ = sbuf([128, 2, 128], F32, "dst")
                    S.dma("sp", dst_[:], dftst_in[:, :, :], writes=[b_dst_], track=b_dst_)
                    dB, b_dB = sbuf([128, 2, 2, 256], BF16, "dB")
                    S.dma("sp", dB[:], dftB_in[:, :, :, :], writes=[b_dB], track=b_dB)
                    dC3, b_dC3 = sbuf([128, 2, 2, 256], BF16, "dC3")
                    S.dma("sp", dC3[:], dftC3_in[:, :, :, :], writes=[b_dC3], track=b_dC3)
                    fnet_dft(j, dA, b_dA, dct, b_dct, dst_, b_dst_, dB, b_dB, dC3, b_dC3)
                with Phase():
                    ag_chunks(MXloc, MXall[j], 4 * CHL, T, [bMX], [bMXall])
                    fnet_gate(j)
                d0, bd0 = nxt()
                mix_phase(cur, bcur, d0, bd0, fnet_w_out[j][0:D, :], 0)
                d1, bd1 = nxt()
                mix_phase(d0, bd0, d1, bd1, fnet_w_out[j][D:2 * D, :], D)
            d2, bd2 = nxt(final=(li == cfg.DEPTH - 1))
            with Phase():
                alloc_prep()
                wres, b_wres = sbuf([128, KC, D], BF16, "wres")
                wp, b_wp = sbuf([128, max(cfg.PLE // 128, 1), D], BF16, "wp")
                ple_pass(d1, bd1, d2, bd2, li, wres, b_wres, wp, b_wp)
            cur, bcur = d2, bd2
    except _StopBuild:
        pass
    S.barrier()
    nc._sched_ninst = S.ninst
    return nc


_W_NAMES = ["norm_in", "attn_w_in", "attn_q_norm", "attn_k_norm", "attn_w_out", "fnet_w_in", "fnet_w_out",
            "ple_proj", "ple_gate", "ple_norm"]


def run_groups(cfg, groups, weights):
    nc = build(cfg)
    T = cfg.T
    in_maps = []
    wts = {k: np.ascontiguousarray(weights[k], dtype=np.float32) for k in _W_NAMES}
    for c in range(8):
        g, r = c // 4, c % 4
        x, p, nseq = groups[g]
        m = {"x": np.ascontiguousarray(x[r * T:(r + 1) * T]), "p": np.ascontiguousarray(p[:, r * T:(r + 1) * T])}
        m.update(wts)
        m.update(host_tables(cfg, r, nseq))
        in_maps.append(m)
    res = run_bass_kernel_spmd(nc, in_maps, core_ids=list(range(8)))
    return [np.concatenate([np.asarray(res.results[g * 4 + r]["y"]) for r in range(4)], axis=0) for g in range(2)]


def kernel(x_prompt, x_sample, p_prompt, p_sample, **weights):
    cfg = Cfg(T=4096)
    x_prompt = np.asarray(x_prompt, np.float32)
    x_sample = np.asarray(x_sample, np.float32)
    p_prompt = np.asarray(p_prompt, np.float32)
    p_sample = np.asarray(p_sample, np.float32)
    B, SQ, D = x_prompt.shape
    gA = (x_prompt.reshape(B * SQ, D), p_prompt.reshape(p_prompt.shape[0], B * SQ, -1), B)
    gB = (x_sample[0], p_sample[:, 0], 1)
    ys = run_groups(cfg, [gA, gB], weights)
    y_prompt = ys[0].reshape(B, SQ, D).astype(np.float32)
    y_sample = ys[1][None].astype(np.float32)
    return (y_prompt, y_sample)
```
